# Optimizing a Trainium2 kernel written in Bass

```python
import jax, jax.numpy as jnp
from jax import lax
import numpy as np

D_MODEL = 1024
BATCH = 16
SEQ = 2048
DEPTH = 1
DEC_BATCH = 8
DEC_SEQ = 2048
PAST_LEN = 128

D_CONV = 512
CONV_W = 3
N_HEADS = 16
N_KV_HEADS = 4
GROUP = N_HEADS // N_KV_HEADS
HEAD_DIM = 64
AXIS_DIM = HEAD_DIM // 2
ROPE_THETA = 10000.0
GRID_W = 64
Q_BLOCK = 128
N_MEM = 256
N_MEM_HEADS = 4
MEM_HEAD_DIM = 128
N_BRANCH = 3
D_FF = 2816
EPS = 1e-6
SPLIT_WIDTHS = (D_CONV, D_CONV, D_CONV, N_HEADS * HEAD_DIM, N_KV_HEADS * HEAD_DIM,
                N_KV_HEADS * HEAD_DIM, N_MEM_HEADS * MEM_HEAD_DIM, N_BRANCH * D_MODEL)
D_IN_PROJ = 3 * D_CONV + (N_HEADS + 2 * N_KV_HEADS) * HEAD_DIM + N_MEM_HEADS * MEM_HEAD_DIM + N_BRANCH * D_MODEL

kernel_name = "hybrid_gated_conv_axialgqa_memxattn_encoder"


def rmsnorm(x, g):
    xf = x.astype(jnp.float32)
    y = xf * lax.rsqrt(jnp.mean(xf * xf, axis=-1, keepdims=True) + EPS)
    return (y * g.astype(jnp.float32)).astype(x.dtype)


def swiglu(x, w1, w3, w2):
    return (jax.nn.silu(x @ w1) * (x @ w3)) @ w2


def axial_rope_tables(T):
    rows = T // GRID_W
    row = jnp.repeat(jnp.arange(rows), GRID_W).astype(jnp.float32)
    col = jnp.tile(jnp.arange(GRID_W), rows).astype(jnp.float32)
    inv = 1.0 / (ROPE_THETA ** (jnp.arange(0, AXIS_DIM, 2, dtype=jnp.float32) / AXIS_DIM))
    ang = jnp.concatenate([row[:, None] * inv, col[:, None] * inv], axis=-1)
    return jnp.cos(ang)[:, None, :], jnp.sin(ang)[:, None, :]


def apply_rope(x, cos, sin):
    xf = x.astype(jnp.float32)
    x1, x2 = xf[..., 0::2], xf[..., 1::2]
    out = jnp.stack([x1 * cos - x2 * sin, x1 * sin + x2 * cos], axis=-1)
    return out.reshape(x.shape).astype(x.dtype)


def self_attention(q, k, v):
    B, T = q.shape[0], q.shape[1]
    nblk = T // Q_BLOCK
    qb = q.reshape(B, nblk, Q_BLOCK, N_KV_HEADS, GROUP, HEAD_DIM).transpose(1, 0, 2, 3, 4, 5)
    scale = HEAD_DIM ** -0.5

    def one_block(qblk):
        s = jnp.einsum('bqkgd,bskd->bkgqs', qblk, k).astype(jnp.float32) * scale
        p = jax.nn.softmax(s, axis=-1).astype(v.dtype)
        return jnp.einsum('bkgqs,bskd->bqkgd', p, v)

    o = lax.map(one_block, qb)
    return o.transpose(1, 0, 2, 3, 4, 5).reshape(B, T, N_HEADS * HEAD_DIM)


def memory_attention(qm, mem, mem_norm, w_mem_kv):
    B, T = qm.shape[0], qm.shape[1]
    M = mem.shape[1]
    kv = rmsnorm(mem, mem_norm) @ w_mem_kv
    km, vm = jnp.split(kv, 2, axis=-1)
    km = km.reshape(B, M, N_MEM_HEADS, MEM_HEAD_DIM)
    vm = vm.reshape(B, M, N_MEM_HEADS, MEM_HEAD_DIM)
    qh = qm.reshape(B, T, N_MEM_HEADS, MEM_HEAD_DIM)
    s = jnp.einsum('bqhd,bmhd->bhqm', qh, km).astype(jnp.float32) * (MEM_HEAD_DIM ** -0.5)
    p = jax.nn.softmax(s, axis=-1).astype(vm.dtype)
    return jnp.einsum('bhqm,bmhd->bqhd', p, vm).reshape(B, T, N_MEM_HEADS * MEM_HEAD_DIM)


def encoder_layer(x, mem, p):
    B, T, _ = x.shape
    h = rmsnorm(x, p['ffn1_pre'])
    x = x + 0.5 * rmsnorm(swiglu(h, p['ffn1_w1'], p['ffn1_w3'], p['ffn1_w2']), p['ffn1_post'])

    u = rmsnorm(x, p['mix_pre'])
    proj = u @ p['w_in']
    offs, acc = [], 0
    for w in SPLIT_WIDTHS[:-1]:
        acc += w
        offs.append(acc)
    cx, cb, cc, q, k, v, qm, g = jnp.split(proj, offs, axis=-1)

    z = cc * cx
    zp = jnp.pad(z, ((0, 0), (1, 1), (0, 0)))
    cw = p['conv_w']
    zc = zp[:, :-2] * cw[0] + zp[:, 1:-1] * cw[1] + zp[:, 2:] * cw[2] + p['conv_b']
    y_conv = (cb * zc) @ p['p_conv']

    q = rmsnorm(q.reshape(B, T, N_HEADS, HEAD_DIM), p['q_norm'])
    k = rmsnorm(k.reshape(B, T, N_KV_HEADS, HEAD_DIM), p['k_norm'])
    v = v.reshape(B, T, N_KV_HEADS, HEAD_DIM)
    cos, sin = axial_rope_tables(T)
    q = apply_rope(q, cos, sin)
    k = apply_rope(k, cos, sin)
    y_attn = self_attention(q, k, v) @ p['p_attn']

    y_mem = memory_attention(qm, mem, p['mem_norm'], p['w_mem_kv']) @ p['p_mem']

    gates = jax.nn.sigmoid(g + p['b_gate']).reshape(B, T, N_BRANCH, D_MODEL)
    merged = gates[:, :, 0] * y_conv + gates[:, :, 1] * y_attn + gates[:, :, 2] * y_mem
    x = x + rmsnorm(merged @ p['w_out'], p['mix_post'])

    h = rmsnorm(x, p['ffn2_pre'])
    x = x + 0.5 * rmsnorm(swiglu(h, p['ffn2_w1'], p['ffn2_w3'], p['ffn2_w2']), p['ffn2_post'])
    return x


def setup_inputs(seed: int = 0) -> dict:
    key = jax.random.key(seed)
    ks = iter(jax.random.split(key, 40))

    def nrm(shape, scale):
        return jax.random.normal(next(ks), shape, jnp.float32) * scale

    def gain(n):
        return 1.0 + nrm((DEPTH, n), 0.02)

    L, D = DEPTH, D_MODEL
    return {
        'x_prompt': nrm((BATCH, SEQ, D), 1.0),
        'x_sample': nrm((DEC_BATCH, DEC_SEQ, D), 1.0),
        'mem_prompt': nrm((BATCH, N_MEM, D), 1.0),
        'mem_sample': nrm((DEC_BATCH, N_MEM, D), 1.0),
        'ffn1_pre': gain(D),
        'ffn1_w1': nrm((L, D, D_FF), D ** -0.5),
        'ffn1_w3': nrm((L, D, D_FF), D ** -0.5),
        'ffn1_w2': nrm((L, D_FF, D), D_FF ** -0.5),
        'ffn1_post': gain(D),
        'mix_pre': gain(D),
        'w_in': nrm((L, D, D_IN_PROJ), D ** -0.5),
        'conv_w': nrm((L, CONV_W, D_CONV), CONV_W ** -0.5),
        'conv_b': nrm((L, D_CONV), 0.02),
        'p_conv': nrm((L, D_CONV, D), D_CONV ** -0.5),
        'q_norm': gain(HEAD_DIM),
        'k_norm': gain(HEAD_DIM),
        'p_attn': nrm((L, N_HEADS * HEAD_DIM, D), (N_HEADS * HEAD_DIM) ** -0.5),
        'mem_norm': gain(D),
        'w_mem_kv': nrm((L, D, 2 * N_MEM_HEADS * MEM_HEAD_DIM), D ** -0.5),
        'p_mem': nrm((L, N_MEM_HEADS * MEM_HEAD_DIM, D), (N_MEM_HEADS * MEM_HEAD_DIM) ** -0.5),
        'b_gate': nrm((L, N_BRANCH * D), 0.02),
        'w_out': nrm((L, D, D), D ** -0.5),
        'mix_post': gain(D),
        'ffn2_pre': gain(D),
        'ffn2_w1': nrm((L, D, D_FF), D ** -0.5),
        'ffn2_w3': nrm((L, D, D_FF), D ** -0.5),
        'ffn2_w2': nrm((L, D_FF, D), D_FF ** -0.5),
        'ffn2_post': gain(D),
    }


def reference(x_prompt, x_sample, mem_prompt, mem_sample,
              ffn1_pre, ffn1_w1, ffn1_w3, ffn1_w2, ffn1_post,
              mix_pre, w_in, conv_w, conv_b, p_conv, q_norm, k_norm, p_attn,
              mem_norm, w_mem_kv, p_mem, b_gate, w_out, mix_post,
              ffn2_pre, ffn2_w1, ffn2_w3, ffn2_w2, ffn2_post):
    params = {
        'ffn1_pre': ffn1_pre, 'ffn1_w1': ffn1_w1, 'ffn1_w3': ffn1_w3, 'ffn1_w2': ffn1_w2,
        'ffn1_post': ffn1_post, 'mix_pre': mix_pre, 'w_in': w_in, 'conv_w': conv_w,
        'conv_b': conv_b, 'p_conv': p_conv, 'q_norm': q_norm, 'k_norm': k_norm,
        'p_attn': p_attn, 'mem_norm': mem_norm, 'w_mem_kv': w_mem_kv, 'p_mem': p_mem,
        'b_gate': b_gate, 'w_out': w_out, 'mix_post': mix_post, 'ffn2_pre': ffn2_pre,
        'ffn2_w1': ffn2_w1, 'ffn2_w3': ffn2_w3, 'ffn2_w2': ffn2_w2, 'ffn2_post': ffn2_post,
    }
    y_prompt = x_prompt
    y_sample = x_sample
    for l in range(DEPTH):
        p = {name: arr[l] for name, arr in params.items()}
        y_prompt = encoder_layer(y_prompt, mem_prompt, p)
        y_sample = encoder_layer(y_sample, mem_sample, p)
    return (y_prompt, y_sample)
```

```python
import numpy as np
import ml_dtypes
import concourse.bass as bass
import concourse.mybir as mybir
from concourse.bass_utils import run_bass_kernel_spmd

F32 = mybir.dt.float32
BF16 = mybir.dt.bfloat16
AF = mybir.ActivationFunctionType
ALU = mybir.AluOpType
AX = mybir.AxisListType
ENGS = ["pe", "act", "dve", "pool", "sp"]

D = 1024
DFF = 2816
NF = DFF // 128
EPS = 1e-6
N_CORES = 8
SEQ = 2048
O_CX, O_CB, O_CC, O_Q, O_K, O_V, O_QM, O_G = 0, 512, 1024, 1536, 2560, 2816, 3072, 3584


class Op:
    __slots__ = ("eng", "fn", "deps", "flag", "incval", "dma_sem", "dma_val")


class Prog:
    def __init__(self, nc):
        self.nc = nc
        self.enabled = True
        self.reset()
        self.sem = {e: nc.alloc_semaphore("s_" + e) for e in ENGS}

    def reset(self):
        self.ops = {e: [] for e in ENGS}
        self.lastw = {}
        self.readers = {}
        self.dma_counts = {}

    def op(self, eng, fn, reads=(), writes=(), dma_sem=None):
        if not self.enabled:
            return None
        o = Op()
        o.eng = eng; o.fn = fn; o.deps = []; o.flag = False
        o.dma_sem = dma_sem; o.incval = 0; o.dma_val = 0
        if dma_sem is not None:
            c = self.dma_counts.get(dma_sem, 0) + 1
            self.dma_counts[dma_sem] = c
            o.dma_val = 16 * c
        for t in reads:
            w = self.lastw.get(t)
            if w is not None:
                self._dep(o, w, True)
        for t in writes:
            rd = self.readers.get(t)
            if rd:
                for r in rd.values():
                    self._dep(o, r, False)
            w = self.lastw.get(t)
            if w is not None:
                self._dep(o, w, False)
        for t in writes:
            self.lastw[t] = o
            self.readers[t] = {}
        for t in reads:
            rd = self.readers.setdefault(t, {})
            rd[id(o) if dma_sem is not None else eng] = o
        self.ops[eng].append(o)
        return o

    def _dep(self, c, p, raw):
        if p is c:
            return
        if p.dma_sem is None and c.dma_sem is None and p.eng == c.eng:
            if (not raw) or p.eng == "pe":
                return
        for q in c.deps:
            if q is p:
                return
        c.deps.append(p)
        if p.dma_sem is None:
            p.flag = True

    def emit(self):
        nc = self.nc
        for e in ENGS:
            n = 0
            for o in self.ops[e]:
                if o.flag:
                    n += 1
                    o.incval = n
        semof = self.sem
        ops = self.ops

        def run(e, eng):
            waited = {}
            for o in ops[e]:
                need = {}
                for p in o.deps:
                    if p.dma_sem is not None:
                        s, v = p.dma_sem, p.dma_val
                    else:
                        s, v = semof[p.eng], p.incval
                    if waited.get(s, 0) < v and need.get(s, 0) < v:
                        need[s] = v
                for s, v in need.items():
                    eng.wait_ge(s, v)
                    waited[s] = v
                ins = o.fn(eng)
                if o.dma_sem is not None:
                    ins.then_inc(o.dma_sem, 16)
                elif o.flag:
                    ins.then_inc(semof[e], 1)

        with nc.Block() as block:
            @block.tensor
            def _(t):
                run("pe", t)

            @block.scalar
            def _(t):
                run("act", t)

            @block.vector
            def _(t):
                run("dve", t)

            @block.gpsimd
            def _(t):
                run("pool", t)

            @block.sync
            def _(t):
                run("sp", t)


class WStream:
    def __init__(self, P, nc, nslots=8):
        self.P = P
        self.S = nslots
        self.slots = [nc.alloc_sbuf_tensor("ring%d" % i, [128, 2048], BF16) for i in range(nslots)]
        self.sems = [nc.alloc_semaphore("rsem%d" % i) for i in range(nslots)]
        self.plan = []
        self.planning = True
        self.n_acq = 0
        self.n_issued = 0

    def start_real(self):
        self.planning = False
        self.n_acq = 0
        self.n_issued = 0
        for n in range(min(self.S, len(self.plan))):
            self._issue(n)

    def _view(self, n, nk, ncols):
        return self.slots[n % self.S][:, 0:nk * ncols].rearrange("p (k c) -> p k c", k=nk)

    def _issue(self, n):
        assert n == self.n_issued
        self.n_issued += 1
        W, r0, nk, c0, ncols = self.plan[n]
        dst = self._view(n, nk, ncols)
        src = W[r0:r0 + nk * 128, c0:c0 + ncols].rearrange("(k p) c -> p k c", p=128)
        self.P.op("pool", lambda e, dst=dst, src=src: e.dma_start(out=dst, in_=src),
                  writes=[("ring", n % self.S)], dma_sem=self.sems[n % self.S])

    def get(self, W, r0, nk, c0, ncols):
        n = self.n_acq
        self.n_acq += 1
        if self.planning:
            self.plan.append((W, r0, nk, c0, ncols))
        else:
            d = self.plan[n]
            assert d[1:] == (r0, nk, c0, ncols) and d[0] is W, "weight plan mismatch"
            assert n < self.n_issued, "piece not issued (too many pinned)"
        return n, self._view(n, nk, ncols), ("ring", n % self.S)

    def release(self, n):
        if self.planning:
            return
        m = n + self.S
        if m < len(self.plan):
            assert m == self.n_issued, "out-of-order release"
            self._issue(m)


def build_program(n_seq, T):
    NT = T // 512
    NCH = T // 128
    nc = bass.Bass("TRN2", target_bir_lowering=False)
    P = Prog(nc)

    def din(name, shape, dt=F32):
        return nc.dram_tensor(name, list(shape), dt, kind="ExternalInput").ap()

    x_d = din("x", [n_seq, T, D])
    mem_d = din("mem", [n_seq, 256, D])
    y_d = nc.dram_tensor("y", [n_seq, T, D], F32, kind="ExternalOutput").ap()
    x1s_d = nc.dram_tensor("x1s", [n_seq, T, D], F32).ap()
    Wd = {}
    for nm, shp in [("ffn1_w1", [D, DFF]), ("ffn1_w3", [D, DFF]), ("ffn1_w2", [DFF, D]),
                    ("w_in", [D, 6656]), ("p_conv", [512, D]), ("p_attn", [D, D]),
                    ("w_mem_kv", [D, D]), ("p_mem", [512, D]), ("w_out", [D, D]),
                    ("ffn2_w1", [D, DFF]), ("ffn2_w3", [D, DFF]), ("ffn2_w2", [DFF, D])]:
        Wd[nm] = din(nm, shp)
    gfm_d = din("gfm", [128, 4 * 8])
    gpost_d = din("gpost", [3 * D])
    qkg_d = din("qkg", [2 * 64])
    convw_d = din("convw", [128, 12])
    convb_d = din("convb", [128, 4])
    bgate_d = din("bgate", [128, 24])
    rope_d = din("rope", [128, NCH * 64])
    ident_d = din("ident", [128, 128], BF16)

    A = nc.alloc_sbuf_tensor
    xbuf = [A("xbuf%d" % i, [128, 4, D], F32) for i in range(2)]
    KT = A("KT", [128, 4, T], BF16)
    Vaug = A("Vaug", [128, NCH, 4, 129], BF16)
    zT = A("zT", [128, 4, T + 2], BF16)
    KmT = A("KmT", [128, 4, 256], BF16)
    Vm = A("Vm", [128, 2, 512], BF16)
    ropet = A("ropet", [128, NCH, 2, 32], F32)
    gpost = A("gpostsb", [128, 3, D], F32)
    gfm = A("gfmsb", [128, 4, 8], F32)
    qkg = A("qkgsb", [128, 2, 64], F32)
    convw = A("convwsb", [128, 3, 4], F32)
    convb = A("convbsb", [128, 4], F32)
    bgate = A("bgatesb", [128, 24], F32)
    ident = A("identsb", [128, 128], BF16)
    ones32 = A("ones32", [128, 64], F32)
    onesbf = A("onesbf", [128, 128], BF16)
    chalf = A("chalf", [128, 64], F32)
    cneg1 = A("cneg1", [128, 1], F32)
    ceps1 = A("ceps1", [128, 1], F32)
    ceps4 = A("ceps4", [128, 1], F32)
    hT = A("hT", [128, 8, 512], BF16)
    W8b = A("W8b", [128, 8, 512], BF16)
    OTn = A("OTn", [128, 8, 512], BF16)
    stg = A("stg", [128, 2, D], BF16)
    gT = A("gT", [128, NF, 512], BF16)
    f32w = A("f32w", [128, 8, 512], F32)
    junk = A("junk", [128, D], BF16)
    krot2b = A("krot", [128, 2, 4, 2, 64], BF16)
    qrot = A("qrot", [128, D], BF16)
    st_ss = A("st_ss", [128, 8], F32)
    st_pre = A("st_pre", [128, 4], F32)
    st_rstd = A("st_rstd", [128, 4], F32)
    stb_ss = A("stb_ss", [128, 8], F32)
    stb_pre = A("stb_pre", [128, 8], F32)
    stb_rstd = A("stb_rstd", [128, 8], F32)
    st_q = A("st_q", [128, 16], F32)
    st_qp = A("st_qp", [128, 16], F32)
    st_qr = A("st_qr", [128, 16], F32)
    ps = nc.alloc_psum_tensor("ps", [128, 8 * 512], F32)

    def bank(b, n=1):
        return ps[:, b * 512:(b + n) * 512]

    W = WStream(P, nc, 8)
    xl_sem = [nc.alloc_semaphore("xl%d" % i) for i in range(2)]
    xs_sem = [nc.alloc_semaphore("xs%d" % i) for i in range(2)]
    mem_sem = nc.alloc_semaphore("memsem")
    rr = {"b": 0}
    csems = {}

    def nbank():
        b = rr["b"]
        rr["b"] = (b + 1) % 8
        return b

    def PSB(b):
        return ("ps", b)

    def init():
        def cl(dst, src, tok, **kw):
            if tok not in csems:
                csems[tok] = nc.alloc_semaphore("c_" + tok)
            P.op("sp", lambda e: e.dma_start(out=dst, in_=src, **kw), writes=[tok], dma_sem=csems[tok])
        cl(gfm[:].rearrange("p a b -> p (a b)"), gfm_d, "gfm")
        cl(gpost[:].rearrange("p a b -> p (a b)"), gpost_d.partition_broadcast(128), "gpost")
        cl(qkg[:].rearrange("p a b -> p (a b)"), qkg_d.partition_broadcast(128), "qkg")
        cl(convw[:].rearrange("p a b -> p (a b)"), convw_d, "convw")
        cl(convb[:], convb_d, "convb")
        cl(bgate[:], bgate_d, "bgate")
        cl(ropet[:].rearrange("p a b c -> p (a b c)"), rope_d, "rope")
        cl(ident[:], ident_d, "ident")
        P.op("dve", lambda e: e.memset(ones32[:], 1.0), writes=["ones32"])
        P.op("dve", lambda e: e.memset(onesbf[:], 1.0), writes=["onesbf"])
        P.op("pool", lambda e: e.memset(chalf[:], -0.5), writes=["chalf"])
        P.op("pool", lambda e: e.memset(cneg1[:], -1.0), writes=["chalf"])
        P.op("pool", lambda e: e.memset(ceps1[:], EPS), writes=["chalf"])
        P.op("pool", lambda e: e.memset(ceps4[:], 4.0 * EPS), writes=["chalf"])
        P.op("dve", lambda e: e.memset(Vaug[:].rearrange("p a b c -> p (a b c)"), 0.0), writes=["Vinit"])
        P.op("dve", lambda e: e.memset(Vaug[:, :, :, 0:1], 1.0), writes=["Vinit"])
        P.op("dve", lambda e: e.memset(Vaug[:, :, :, 128:129], 1.0), writes=["Vinit"])
        P.op("dve", lambda e: e.memset(zT[:, :, 0:1], 0.0), writes=["zinit"])
        P.op("dve", lambda e: e.memset(zT[:, :, T + 1:T + 2], 0.0), writes=["zinit"])
        P.op("dve", lambda e: e.tensor_scalar(out=gpost[:, 0, :], in0=gpost[:, 0, :], scalar1=0.5, scalar2=None,
                                              op0=ALU.mult), reads=["gpost"], writes=["gpost"])
        P.op("dve", lambda e: e.tensor_scalar(out=gpost[:, 2, :], in0=gpost[:, 2, :], scalar1=0.5, scalar2=None,
                                              op0=ALU.mult), reads=["gpost"], writes=["gpost"])
        P.op("dve", lambda e: e.tensor_scalar(out=bgate[:], in0=bgate[:], scalar1=0.5, scalar2=None,
                                              op0=ALU.mult), reads=["bgate"], writes=["bgate"])

    def rstd_chain(ms_ap, n, eps_tile, pre_ap, out_ap, rtok, wtok):
        P.op("pool", lambda e: e.tensor_tensor(out=pre_ap, in0=ms_ap, in1=eps_tile[:, 0:1].broadcast_to([128, n]), op=ALU.add),
             reads=list(rtok) + ["chalf"], writes=[wtok + "_pre"])
        P.op("pool", lambda e: e.tensor_tensor(out=out_ap, in0=pre_ap, in1=chalf[:, 0:n], op=ALU.pow),
             reads=[wtok + "_pre", "chalf"], writes=[wtok])

    def build_T(src_aps, src_toks, gi, dstT, dst_toks, nsub=4):
        for t in range(nsub):
            P.op("act", lambda e, t=t: e.activation(out=junk[:], in_=src_aps[t], func=AF.Square, scale=float(D ** -0.5),
                                                    accum_out=st_ss[:, t:t + 1]),
                 reads=list(src_toks[t]), writes=["junk", ("st_ss", t)])
            rstd_chain(st_ss[:, t:t + 1], 1, ceps1, st_pre[:, t:t + 1], st_rstd[:, t:t + 1], [("st_ss", t)], "st_rstd%d" % t)
        for t in range(nsub):
            sb = t % 2
            P.op("dve", lambda e, t=t, sb=sb: e.tensor_scalar(out=stg[:, sb, :], in0=src_aps[t],
                                                               scalar1=st_rstd[:, t:t + 1], scalar2=None, op0=ALU.mult),
                 reads=list(src_toks[t]) + ["st_rstd%d" % t], writes=[("stg", sb)])
            b = nbank()
            pT = bank(b).bitcast(BF16)

            def tr(e, sb=sb, pT=pT):
                for k in range(8):
                    i = e.transpose(pT[:, k * 128:(k + 1) * 128], stg[:, sb, k * 128:(k + 1) * 128], ident[:])
                return i
            P.op("pe", tr, reads=[("stg", sb), "ident"], writes=[PSB(b)])
            P.op("dve", lambda e, t=t, pT=pT: e.tensor_tensor(
                out=dstT[:, :, t * 128:(t + 1) * 128], in0=pT.rearrange("p (k t) -> p k t", k=8),
                in1=gfm[:, gi, :].unsqueeze(2).broadcast_to([128, 8, 128]), op=ALU.mult),
                reads=["gfm"], writes=[PSB(b)] + list(dst_toks[t]))

    bb = {"i": 0}

    def build_bank():
        b = 4 + (bb["i"] % 4)
        bb["i"] += 1
        return b

    def build_sub_A(t, slot, src_ap, src_toks, stg_ap, stg_toks):
        c = slot * 4 + t
        nm = "stb%d" % c
        P.op("act", lambda e: e.activation(out=junk[:], in_=src_ap, func=AF.Square, scale=float(D ** -0.5),
                                           accum_out=stb_ss[:, c:c + 1]), reads=list(src_toks), writes=["junk", nm + "_ss"])
        rstd_chain(stb_ss[:, c:c + 1], 1, ceps1, stb_pre[:, c:c + 1], stb_rstd[:, c:c + 1], [nm + "_ss"], nm)
        P.op("dve", lambda e: e.tensor_scalar(out=stg_ap, in0=src_ap, scalar1=stb_rstd[:, c:c + 1], scalar2=None, op0=ALU.mult),
             reads=list(src_toks) + [nm], writes=list(stg_toks))

    def build_sub_B(t, stg_ap, stg_toks, gi, dstT, dname, b=None):
        if b is None:
            b = build_bank()
        pT = bank(b).bitcast(BF16)

        def tr(e):
            for k in range(8):
                i = e.transpose(pT[:, k * 128:(k + 1) * 128], stg_ap[:, k * 128:(k + 1) * 128], ident[:])
            return i
        P.op("pe", tr, reads=list(stg_toks) + ["ident"], writes=[PSB(b)])
        P.op("dve", lambda e: e.tensor_tensor(
            out=dstT[:, :, t * 128:(t + 1) * 128], in0=pT.rearrange("p (k t) -> p k t", k=8),
            in1=gfm[:, gi, :].unsqueeze(2).broadcast_to([128, 8, 128]), op=ALU.mult),
            reads=["gfm"], writes=[PSB(b), (dname, t)])

    def build_sub(t, slot, src_ap, src_toks, gi, dstT, dname, b=None):
        sb = (slot * 4 + t) % 2
        build_sub_A(t, slot, src_ap, src_toks, stg[:, sb, :], [("stg", sb)])
        build_sub_B(t, stg[:, sb, :], [("stg", sb)], gi, dstT, dname, b)

    def ostg(t):
        return OTn[:, 2 * t:2 * t + 2, :].rearrange("p a b -> p (a b)"), [("OTn", 2 * t), ("OTn", 2 * t + 1)]

    def epi_half0(t, gi, b0):
        P.op("act", lambda e: e.activation(out=junk[:, 0:512], in_=bank(b0), func=AF.Square, scale=float(D ** -0.5),
                                           accum_out=st_ss[:, 2 * t:2 * t + 1]), writes=[PSB(b0), "junk", ("st_ss", 2 * t)])
        P.op("dve", lambda e: e.tensor_tensor(out=f32w[:, 4 + t, :], in0=bank(b0), in1=gpost[:, gi, 0:512], op=ALU.mult),
             reads=["gpost"], writes=[PSB(b0), ("f32w", 4 + t)])

    def epi_sub(t, xb, gi, eps_tile, b1):
        P.op("act", lambda e: e.activation(out=junk[:, 512:1024], in_=bank(b1), func=AF.Square, scale=float(D ** -0.5),
                                           accum_out=st_ss[:, 2 * t + 1:2 * t + 2]), writes=[PSB(b1), "junk", ("st_ss", 2 * t + 1)])
        P.op("pool", lambda e: e.tensor_tensor(out=st_pre[:, t:t + 1], in0=st_ss[:, 2 * t:2 * t + 1],
                                               in1=st_ss[:, 2 * t + 1:2 * t + 2], op=ALU.add),
             reads=[("st_ss", 2 * t), ("st_ss", 2 * t + 1)], writes=["st_sum%d" % t])
        rstd_chain(st_pre[:, t:t + 1], 1, eps_tile, st_pre[:, t:t + 1], st_rstd[:, t:t + 1], ["st_sum%d" % t], "st_rstd%d" % t)
        P.op("dve", lambda e: e.scalar_tensor_tensor(
            out=xbuf[xb][:, t, 0:512], in0=f32w[:, 4 + t, :], scalar=st_rstd[:, t:t + 1], in1=xbuf[xb][:, t, 0:512],
            op0=ALU.mult, op1=ALU.add), reads=["st_rstd%d" % t, ("f32w", 4 + t), ("xb", xb, t)], writes=[("xb", xb, t)])
        tb = t % 2
        P.op("dve", lambda e: e.scalar_tensor_tensor(
            out=f32w[:, tb, :], in0=bank(b1), scalar=st_rstd[:, t:t + 1], in1=gpost[:, gi, 512:1024],
            op0=ALU.mult, op1=ALU.mult), reads=["st_rstd%d" % t, "gpost"], writes=[PSB(b1), ("f32w", tb)])
        P.op("dve", lambda e: e.tensor_tensor(
            out=xbuf[xb][:, t, 512:1024], in0=xbuf[xb][:, t, 512:1024], in1=f32w[:, tb, :], op=ALU.add),
            reads=[("f32w", tb), ("xb", xb, t)], writes=[("xb", xb, t)])

    def post_epilogue(xb, gi, eps_tile, acc_srcs):
        for t in range(4):
            for h in range(2):
                ap, toks = acc_srcs[h][t]
                P.op("act", lambda e, t=t, h=h, ap=ap: e.activation(out=junk[:, 0:512], in_=ap, func=AF.Square, scale=float(D ** -0.5),
                                                               accum_out=st_ss[:, 2 * t + h:2 * t + h + 1]),
                     writes=list(toks) + ["junk", ("st_ss", 2 * t + h)])
            P.op("pool", lambda e, t=t: e.tensor_tensor(out=st_pre[:, t:t + 1], in0=st_ss[:, 2 * t:2 * t + 1],
                                                        in1=st_ss[:, 2 * t + 1:2 * t + 2], op=ALU.add),
                 reads=[("st_ss", 2 * t), ("st_ss", 2 * t + 1)], writes=["st_sum%d" % t])
            rstd_chain(st_pre[:, t:t + 1], 1, eps_tile, st_pre[:, t:t + 1], st_rstd[:, t:t + 1], ["st_sum%d" % t], "st_rstd%d" % t)
        for t in range(4):
            for h in range(2):
                ap, toks = acc_srcs[h][t]
                tb = 2 + h
                P.op("dve", lambda e, t=t, h=h, ap=ap, tb=tb: e.scalar_tensor_tensor(
                    out=f32w[:, tb, :], in0=ap, scalar=st_rstd[:, t:t + 1], in1=gpost[:, gi, h * 512:(h + 1) * 512],
                    op0=ALU.mult, op1=ALU.mult), reads=["st_rstd%d" % t, "gpost"], writes=list(toks) + [("f32w", tb)])
                P.op("pool" if h == 1 else "dve", lambda e, t=t, h=h, tb=tb: e.tensor_tensor(
                    out=xbuf[xb][:, t, h * 512:(h + 1) * 512], in0=xbuf[xb][:, t, h * 512:(h + 1) * 512],
                    in1=f32w[:, tb, :], op=ALU.add), reads=[("f32w", tb), ("xb", xb, t)], writes=[("xb", xb, t)])

    def ffn(xb, pfx, gi_post, pre_half1=None, mid_sub=None, after_sub=None, defer_tail=False):
        w1, w3, w2 = Wd[pfx + "_w1"], Wd[pfx + "_w3"], Wd[pfx + "_w2"]
        hTtok = [("hT", t) for t in range(4)]
        hb = {"i": 0}

        def w2_part(g):
            nk = 4 if g < 5 else 2
            n, wv, tok = W.get(w2, g * 512, nk, 0, 512)
            for t in range(4):
                b = 4 + t

                def f(e, t=t, b=b, wv=wv, g=g, nk=nk):
                    for k in range(nk):
                        j = g * 4 + k
                        i = e.matmul(bank(b), gT[:, j, t * 128:(t + 1) * 128], wv[:, k, :],
                                     start=(j == 0), stop=(j == NF - 1))
                    return i
                P.op("pe", f, reads=[tok] + [("gT", g * 4 + k) for k in range(nk)], writes=[PSB(b)])
            W.release(n)

        for g in range(6):
            npairs = 2 if g < 5 else 1
            for jp in range(2 * g, 2 * g + npairs):
                n1, wv1, tok1 = W.get(w1, 0, 8, jp * 256, 256)
                n3, wv3, tok3 = W.get(w3, 0, 8, jp * 256, 256)
                for jj in range(2):
                    j = 2 * jp + jj
                    pb = (hb["i"] % 2) * 2
                    hb["i"] += 1
                    b1, b3 = pb, pb + 1
                    fb_i = j % 2

                    def f(e, jj=jj, b1=b1, b3=b3, wv1=wv1, wv3=wv3):
                        for k in range(8):
                            e.matmul(bank(b1), wv1[:, k, jj * 128:(jj + 1) * 128], hT[:, k, :], start=(k == 0), stop=(k == 7))
                        for k in range(8):
                            i = e.matmul(bank(b3), wv3[:, k, jj * 128:(jj + 1) * 128], hT[:, k, :], start=(k == 0), stop=(k == 7))
                        return i
                    P.op("pe", f, reads=[tok1, tok3] + hTtok, writes=[PSB(b1), PSB(b3)])
                    P.op("act", lambda e, b1=b1, fb_i=fb_i: e.activation(out=f32w[:, fb_i, :], in_=bank(b1), func=AF.Tanh, scale=0.5),
                         writes=[PSB(b1), ("f32w", fb_i)])
                    P.op("dve", lambda e, b1=b1, fb_i=fb_i: e.scalar_tensor_tensor(
                        out=f32w[:, 2 + fb_i, :], in0=f32w[:, fb_i, :], scalar=1.0, in1=bank(b1), op0=ALU.add, op1=ALU.mult),
                        reads=[("f32w", fb_i)], writes=[PSB(b1), ("f32w", 2 + fb_i)])
                    P.op("dve", lambda e, b3=b3, fb_i=fb_i, j=j: e.tensor_tensor(
                        out=gT[:, j, :], in0=f32w[:, 2 + fb_i, :], in1=bank(b3), op=ALU.mult),
                        reads=[("f32w", 2 + fb_i)], writes=[PSB(b3), ("gT", j)])
                W.release(n1)
                W.release(n3)
            if g >= 1:
                w2_part(g - 1)
        w2_part(5)
        for t in range(4):
            epi_half0(t, gi_post, 4 + t)
        if pre_half1 is not None:
            pre_half1()
        pcs = [W.get(w2, g * 512, 4 if g < 5 else 2, 512, 512) for g in range(6)]
        for t in range(4):
            for g in range(6):
                if t > 0 and g > 0:
                    break

                def f(e, t=t, g0=g):
                    for g in (range(6) if t > 0 else [g0]):
                        for k in range(4 if g < 5 else 2):
                            j = g * 4 + k
                            i = e.matmul(bank(t), gT[:, j, t * 128:(t + 1) * 128], pcs[g][1][:, k, :], start=(j == 0), stop=(j == NF - 1))
                    return i
                rd = [pc[2] for pc in pcs] if t > 0 else [pcs[g][2]]
                P.op("pe", f, reads=rd + [("gT", j) for j in range(NF)], writes=[PSB(t)])
            epi_sub(t, xb, gi_post, ceps4, t)
            if mid_sub is not None:
                mid_sub(t)
            if after_sub is not None:
                if t >= 2:
                    after_sub[1](t - 2)
                after_sub[0](t)
        for pc in pcs:
            W.release(pc[0])
        if after_sub is not None and not defer_tail:
            after_sub[1](2)
            after_sub[1](3)

    def p1_proj(xb, tile, pre_t=None):
        w_in = Wd["w_in"]
        uTtok = [("W8b", t) for t in range(4)]
        nk_, wk, tokk = W.get(w_in, 0, 8, O_K, 256)
        nv_, wvv, tokv = W.get(w_in, 0, 8, O_V, 256)
        kb = [0, 1]
        vb = [2, 3]
        for t in range(4):
            if pre_t is not None and t in pre_t:
                pre_t[t]()
            def f(e, t=t):
                o = bank(kb[t // 2])[:, (t % 2) * 256:(t % 2) * 256 + 256]
                for k in range(8):
                    i = e.matmul(o, W8b[:, k, t * 128:(t + 1) * 128], wk[:, k, :], start=(k == 0), stop=(k == 7))
                return i
            P.op("pe", f, reads=[tokk, ("W8b", t)], writes=[PSB(kb[t // 2])])

            def f2(e, t=t):
                o = bank(vb[t // 2])[:, (t % 2) * 256:(t % 2) * 256 + 256]
                for k in range(8):
                    i = e.matmul(o, W8b[:, k, t * 128:(t + 1) * 128], wvv[:, k, :], start=(k == 0), stop=(k == 7))
                return i
            P.op("pe", f2, reads=[tokv, ("W8b", t)], writes=[PSB(vb[t // 2])])
        W.release(nk_)
        W.release(nv_)
        sqk = f32w[:, 0:2, :].rearrange("p a b -> p (a b)")
        for t in range(4):
            ch = tile * 4 + t
            src = bank(vb[t // 2])[:, (t % 2) * 256:(t % 2) * 256 + 256]
            P.op("act", lambda e, ch=ch, src=src: e.activation(
                out=Vaug[:, ch, :, 64:128], in_=src.rearrange("p (h d) -> p h d", h=4), func=AF.Copy),
                reads=["Vinit"], writes=[PSB(vb[t // 2]), ("V", tile)])
            srck = bank(kb[t // 2])[:, (t % 2) * 256:(t % 2) * 256 + 256]
            P.op("act", lambda e, t=t, srck=srck: e.activation(out=sqk[:, t * 256:(t + 1) * 256], in_=srck, func=AF.Square, scale=0.125),
                 writes=[PSB(kb[t // 2]), ("f32w", 0), ("f32w", 1)])
        P.op("dve", lambda e: e.tensor_reduce(out=st_q[:], in_=sqk.rearrange("p (a d) -> p a d", d=64), axis=AX.X, op=ALU.add),
             reads=[("f32w", 0), ("f32w", 1)], writes=["st_q"])
        rstd_chain(st_q[:], 16, ceps1, st_qp[:], st_qr[:], ["st_q"], "st_qr")
        kn = f32w[:, 2, 0:256].rearrange("p (h d) -> p h d", h=4)
        ta = f32w[:, 4, 0:128].rearrange("p (h i) -> p h i", h=4)
        tb = f32w[:, 5, 0:128].rearrange("p (h i) -> p h i", h=4)
        pcs = []
        for c0 in (O_CX, O_CC, O_CX + 256, O_CC + 256):
            pcs.append(W.get(w_in, 0, 8, c0, 256))

        def k_chain(t):
            krot = krot2b[:, t % 2]
            kt = "krot%d" % (t % 2)
            ch = tile * 4 + t
            srck = bank(kb[t // 2])[:, (t % 2) * 256:(t % 2) * 256 + 256].rearrange("p (h d) -> p h d", h=4)
            P.op("dve", lambda e: e.tensor_tensor(
                out=kn, in0=srck, in1=st_qr[:, t * 4:(t + 1) * 4].unsqueeze(2).broadcast_to([128, 4, 64]), op=ALU.mult),
                reads=["st_qr"], writes=[PSB(kb[t // 2]), ("f32w", 2)])
            P.op("dve", lambda e: e.tensor_tensor(out=kn, in0=kn, in1=qkg[:, 1, :].unsqueeze(1).broadcast_to([128, 4, 64]),
                                                  op=ALU.mult), reads=[("f32w", 2), "qkg"], writes=[("f32w", 2)])
            rope_ops(kn, krot[:, :, 0, :], 4, ch, ta, tb, [("f32w", 2)], kt)
            P.op("pool", lambda e: e.tensor_copy(out=krot[:, :, 1, :], in_=krot[:, :, 0, :]), reads=[kt + "_e", kt + "_o"], writes=[kt + "_d"])

        def k_T(t):
            krot = krot2b[:, t % 2]
            kt = "krot%d" % (t % 2)
            b = 2 + (t % 2)
            pT = bank(b).bitcast(BF16)

            def tr(e):
                for kv in range(4):
                    i = e.transpose(pT[:, kv * 128:(kv + 1) * 128], krot[:, kv, :, :].rearrange("p a d -> p (a d)"), ident[:])
                return i
            P.op("pe", tr, reads=[kt + "_e", kt + "_o", kt + "_d", "ident"], writes=[PSB(b)])
            P.op("act", lambda e: e.activation(
                out=KT[:, :, tile * 512 + t * 128: tile * 512 + (t + 1) * 128],
                in_=pT[:, 0:512].rearrange("p (k t) -> p k t", k=4), func=AF.Copy),
                writes=[PSB(b), ("KT", tile)])

        for c in range(4):
            k_chain(c)
            bx, bc_ = 4 + 2 * (c % 2), 5 + 2 * (c % 2)
            wx = pcs[2 * (c // 2)][1]
            wc = pcs[2 * (c // 2) + 1][1]

            def f(e, c=c, bx=bx, bc_=bc_, wx=wx, wc=wc):
                for k in range(8):
                    e.matmul(bank(bx), wx[:, k, (c % 2) * 128:(c % 2) * 128 + 128], W8b[:, k, :], start=(k == 0), stop=(k == 7))
                for k in range(8):
                    i = e.matmul(bank(bc_), wc[:, k, (c % 2) * 128:(c % 2) * 128 + 128], W8b[:, k, :], start=(k == 0), stop=(k == 7))
                return i
            P.op("pe", f, reads=[pcs[2 * (c // 2)][2], pcs[2 * (c // 2) + 1][2]] + uTtok, writes=[PSB(bx), PSB(bc_)])
            if c >= 1:
                k_T(c - 1)
            fi = 6 + (c % 2)
            P.op("act", lambda e, bx=bx, fi=fi: e.activation(out=f32w[:, fi, :], in_=bank(bx), func=AF.Copy),
                 writes=[PSB(bx), ("f32w", fi)])
            P.op("dve", lambda e, c=c, bc_=bc_, fi=fi: e.tensor_tensor(
                out=zT[:, c, 1 + tile * 512: 1 + (tile + 1) * 512], in0=f32w[:, fi, :], in1=bank(bc_), op=ALU.mult),
                reads=[("f32w", fi), "zinit"], writes=[PSB(bc_), ("zT", tile)])
        k_T(3)
        for pc in pcs:
            W.release(pc[0])

    def rope_ops(src, dst, H, ch, ta, tb, src_toks, dst_tok, tc=None, td=None, toks=(4, 5, 6, 7)):
        sv = src.rearrange("p h (i two) -> p h i two", two=2)
        dv = dst.rearrange("p h (i two) -> p h i two", two=2)
        cosb = ropet[:, ch, 0, :].unsqueeze(1).broadcast_to([128, H, 32])
        sinb = ropet[:, ch, 1, :].unsqueeze(1).broadcast_to([128, H, 32])
        x1, x2 = sv[:, :, :, 0], sv[:, :, :, 1]
        rA, rB = ("f32w", toks[0]), ("f32w", toks[1])
        srd = list(src_toks) + ["rope"]
        P.op("dve", lambda e: e.tensor_tensor(out=ta, in0=x1, in1=cosb, op=ALU.mult), reads=srd, writes=[rA])
        P.op("dve", lambda e: e.tensor_tensor(out=tb, in0=x2, in1=sinb, op=ALU.mult), reads=srd, writes=[rB])
        P.op("dve", lambda e: e.tensor_tensor(out=dv[:, :, :, 0], in0=ta, in1=tb, op=ALU.subtract),
             reads=[rA, rB], writes=[dst_tok + "_e"])
        if tc is None:
            eng2, t2a, t2b, rC, rD = "dve", ta, tb, rA, rB
        else:
            eng2, t2a, t2b, rC, rD = "pool", tc, td, ("f32w", toks[2]), ("f32w", toks[3])
        P.op(eng2, lambda e: e.tensor_tensor(out=t2a, in0=x1, in1=sinb, op=ALU.mult), reads=srd, writes=[rC])
        P.op(eng2, lambda e: e.tensor_tensor(out=t2b, in0=x2, in1=cosb, op=ALU.mult), reads=srd, writes=[rD])
        P.op(eng2, lambda e: e.tensor_tensor(out=dv[:, :, :, 1], in0=t2a, in1=t2b, op=ALU.add),
             reads=[rC, rD], writes=[dst_tok + "_o"])

    def mem_kv(s):
        wm = Wd["w_mem_kv"]
        mt = f32w[:, 0:4, :].rearrange("p (a b) c -> p a (b c)", a=2)
        P.op("sp", lambda e: e.dma_start(out=mt, in_=mem_d[s].rearrange("(a p) d -> p a d", p=128)),
             writes=[("f32w", i) for i in range(4)], dma_sem=mem_sem)
        memT = OTn[:, :, 0:256]
        otoks = [("OTn", k) for k in range(8)]
        build_T([mt[:, a, :] for a in range(2)], [[("f32w", 0), ("f32w", 1)], [("f32w", 2), ("f32w", 3)]], 3, OTn,
                [otoks, otoks], nsub=2)
        mtok = otoks
        pk = [W.get(wm, 0, 8, 0, 256), W.get(wm, 0, 8, 256, 256)]
        for h in range(4):
            b = nbank()
            wv = pk[h // 2][1]

            def f(e, h=h, b=b, wv=wv):
                for k in range(8):
                    i = e.matmul(bank(b)[:, 0:256], wv[:, k, (h % 2) * 128:(h % 2) * 128 + 128], memT[:, k, :],
                                 start=(k == 0), stop=(k == 7))
                return i
            P.op("pe", f, reads=[pk[h // 2][2]] + mtok, writes=[PSB(b)])
            P.op("act", lambda e, h=h, b=b: e.activation(out=KmT[:, h, :], in_=bank(b)[:, 0:256], func=AF.Copy),
                 writes=[PSB(b), "KmT"])
        W.release(pk[0][0])
        W.release(pk[1][0])
        pv = [W.get(wm, 0, 4, 512, 512), W.get(wm, 512, 4, 512, 512)]
        for mc in range(2):
            b = nbank()

            def f(e, mc=mc, b=b):
                for k in range(8):
                    i = e.matmul(bank(b), memT[:, k, mc * 128:(mc + 1) * 128], pv[k // 4][1][:, k % 4, :],
                                 start=(k == 0), stop=(k == 7))
                return i
            P.op("pe", f, reads=[pv[0][2], pv[1][2]] + mtok, writes=[PSB(b)])
            P.op("act", lambda e, mc=mc, b=b: e.activation(out=Vm[:, mc, :], in_=bank(b), func=AF.Copy),
                 writes=[PSB(b), "Vm"])
        W.release(pv[0][0])
        W.release(pv[1][0])

    W8b_all = [("W8b", t_) for t_ in range(4)] + [("W8b", n_, t_) for n_ in range(2) for t_ in range(4)]

    def mixer(xb, tile):
        w_in = Wd["w_in"]
        uTtok = [("hT", t) for t in range(4)]
        QT = W8b
        pq = [W.get(w_in, kh * 512, 4, O_Q, 512) for kh in range(2)]
        sqh = f32w[:, 0, :]
        qnh = f32w[:, 2, :].rearrange("p (h d) -> p h d", h=8)
        ta = f32w[:, 1, 0:256].rearrange("p (h i) -> p h i", h=8)
        tb = f32w[:, 3, 0:256].rearrange("p (h i) -> p h i", h=8)
        tc_ = f32w[:, 6, 0:256].rearrange("p (h i) -> p h i", h=8)
        td_ = f32w[:, 7, 0:256].rearrange("p (h i) -> p h i", h=8)

        def q_dst(n_, t):
            if n_ == 0:
                return OTn[:, t, :], "qr0_%d" % t
            return qrot[:, 512:1024], "qr1"

        def q_proj(n_, t, qb):
            def f(e):
                for k in range(8):
                    i = e.matmul(bank(qb), hT[:, k, t * 128:(t + 1) * 128], pq[n_ * 2 + k // 4][1][:, k % 4, :],
                                 start=(k == 0), stop=(k == 7))
                return i
            P.op("pe", f, reads=[pq[n_ * 2][2], pq[n_ * 2 + 1][2], ("hT", t)], writes=[PSB(qb)])
            ch = tile * 4 + t
            P.op("act", lambda e: e.activation(out=sqh, in_=bank(qb), func=AF.Square, scale=0.125), writes=[PSB(qb), ("f32w", 0)])
            P.op("dve", lambda e: e.tensor_reduce(out=st_q[:, 0:8], in_=sqh.rearrange("p (a d) -> p a d", d=64), axis=AX.X, op=ALU.add),
                 reads=[("f32w", 0)], writes=["st_q"])
            rstd_chain(st_q[:, 0:8], 8, ceps1, st_qp[:, 0:8], st_qr[:, 0:8], ["st_q"], "st_qr")
            P.op("dve", lambda e: e.tensor_tensor(
                out=qnh, in0=bank(qb).rearrange("p (h d) -> p h d", h=8),
                in1=st_qr[:, 0:8].unsqueeze(2).broadcast_to([128, 8, 64]), op=ALU.mult),
                reads=["st_qr"], writes=[PSB(qb), ("f32w", 2)])
            P.op("dve", lambda e: e.tensor_tensor(out=qnh, in0=qnh, in1=qkg[:, 0, :].unsqueeze(1).broadcast_to([128, 8, 64]),
                                                  op=ALU.mult), reads=[("f32w", 2), "qkg"], writes=[("f32w", 2)])
            dst, dtok = q_dst(n_, t)
            rope_ops(qnh, dst.rearrange("p (h d) -> p h d", h=8), 8, ch, ta, tb, [("f32w", 2)],
                     dtok, tc=tc_, td=td_, toks=(1, 3, 6, 7))

        def q_T(n_, t, b):
            pT = bank(b).bitcast(BF16)

            src, dtok = q_dst(n_, t)

            def tr(e):
                for k in range(4):
                    i = e.transpose(pT[:, k * 128:(k + 1) * 128], src[:, k * 128:(k + 1) * 128], ident[:])
                return i
            P.op("pe", tr, reads=[dtok + "_e", dtok + "_o", "ident"], writes=[PSB(b)])
            P.op("act", lambda e: e.activation(out=QT[:, 4 * n_:4 * n_ + 4, t * 128:(t + 1) * 128],
                                               in_=pT[:, 0:512].rearrange("p (k t) -> p k t", k=4), func=AF.Copy),
                 writes=[PSB(b), ("W8b", n_, t)])

        for t in range(4):
            q_proj(0, t, t)
        W.release(pq[0][0])
        W.release(pq[1][0])
        pqm = [W.get(w_in, 0, 8, O_QM, 256), W.get(w_in, 0, 8, O_QM + 256, 256)]
        for h in range(4):
            b = 4 + h
            wv = pqm[h // 2][1]

            def f(e, h=h, b=b, wv=wv):
                for k in range(8):
                    i = e.matmul(bank(b), wv[:, k, (h % 2) * 128:(h % 2) * 128 + 128], hT[:, k, :], start=(k == 0), stop=(k == 7))
                return i
            P.op("pe", f, reads=[pqm[h // 2][2]] + uTtok, writes=[PSB(b)])
            P.op("act", lambda e, h=h, b=b: e.activation(out=gT[:, h, :], in_=bank(b), func=AF.Copy),
                 writes=[PSB(b), ("gT", h)])
        W.release(pqm[0][0])
        W.release(pqm[1][0])
        pcb = [W.get(w_in, 0, 8, O_CB, 256), W.get(w_in, 0, 8, O_CB + 256, 256)]
        ztoks = [("zT", i) for i in (tile - 1, tile, tile + 1) if 0 <= i < NT] + ["zinit"]
        for c in range(4):
            b = 4 + c
            wv = pcb[c // 2][1]

            def f(e, c=c, b=b, wv=wv):
                for k in range(8):
                    i = e.matmul(bank(b), wv[:, k, (c % 2) * 128:(c % 2) * 128 + 128], hT[:, k, :], start=(k == 0), stop=(k == 7))
                return i
            P.op("pe", f, reads=[pcb[c // 2][2]] + uTtok, writes=[PSB(b)])
            fi = 6 + (c % 2)
            z0 = 1 + tile * 512
            P.op("dve", lambda e, c=c, fi=fi, z0=z0: e.tensor_scalar(
                out=f32w[:, fi, :], in0=zT[:, c, z0:z0 + 512], scalar1=convw[:, 1, c:c + 1], scalar2=convb[:, c:c + 1],
                op0=ALU.mult, op1=ALU.add), reads=ztoks + ["convw", "convb"], writes=[("f32w", fi)])
            P.op("dve", lambda e, c=c, fi=fi, z0=z0: e.scalar_tensor_tensor(
                out=f32w[:, fi, :], in0=zT[:, c, z0 - 1:z0 + 511], scalar=convw[:, 0, c:c + 1], in1=f32w[:, fi, :],
                op0=ALU.mult, op1=ALU.add), reads=ztoks + [("f32w", fi)], writes=[("f32w", fi)])
            P.op("dve", lambda e, c=c, fi=fi, z0=z0: e.scalar_tensor_tensor(
                out=f32w[:, fi, :], in0=zT[:, c, z0 + 1:z0 + 513], scalar=convw[:, 2, c:c + 1], in1=f32w[:, fi, :],
                op0=ALU.mult, op1=ALU.add), reads=ztoks + [("f32w", fi)], writes=[("f32w", fi)])
            P.op("dve", lambda e, c=c, fi=fi, b=b: e.tensor_tensor(out=gT[:, 10 + c, :], in0=f32w[:, fi, :], in1=bank(b), op=ALU.mult),
                 reads=[("f32w", fi)], writes=[PSB(b), ("gT", 10 + c)])
        W.release(pcb[0][0])
        W.release(pcb[1][0])
        def mem_S(h):
            sb = (h % 2) * 2
            pc0 = 4 if h % 2 == 0 else 14
            PmT = gT[:, pc0:pc0 + 2, :]

            def f(e):
                for mc in range(2):
                    i = e.matmul(bank(sb + mc), KmT[:, h, mc * 128:(mc + 1) * 128], gT[:, h, :], start=True, stop=True)
                return i
            P.op("pe", f, reads=["KmT", ("gT", h)], writes=[PSB(sb), PSB(sb + 1)])
            P.op("act", lambda e: e.activation(out=PmT.rearrange("p a b -> p (a b)"), in_=bank(sb, 2), func=AF.Exp,
                                               scale=float(128 ** -0.5)),
                 writes=[PSB(sb), PSB(sb + 1), ("gT", pc0), ("gT", pc0 + 1)])

        def mem_PV(h):
            ob, smb = 4 + (h % 2) * 2, 5 + (h % 2) * 2
            pc0 = 4 if h % 2 == 0 else 14
            PmT = gT[:, pc0:pc0 + 2, :]

            def f2(e):
                for mc in range(2):
                    e.matmul(bank(ob), Vm[:, mc, h * 128:(h + 1) * 128], PmT[:, mc, :], start=(mc == 0), stop=(mc == 1))
                for mc in range(2):
                    i = e.matmul(bank(smb), onesbf[:], PmT[:, mc, :], start=(mc == 0), stop=(mc == 1))
                return i
            P.op("pe", f2, reads=["Vm", "onesbf", ("gT", pc0), ("gT", pc0 + 1)], writes=[PSB(ob), PSB(smb)])
            P.op("dve", lambda e: e.reciprocal(out=f32w[:, 6, :], in_=bank(smb)), writes=[PSB(smb), ("f32w", 6)])
            P.op("dve", lambda e: e.tensor_tensor(out=gT[:, 6 + h, :], in0=f32w[:, 6, :], in1=bank(ob), op=ALU.mult),
                 reads=[("f32w", 6)], writes=[PSB(ob), ("gT", 6 + h)])

        mem_S(0)
        for h in range(4):
            if h + 1 < 4:
                mem_S(h + 1)
            mem_PV(h)
        for t in range(4):
            q_T(0, t, 4 + t)
        for kh in range(2):
            pq.append(W.get(w_in, kh * 512, 4, O_Q + 512, 512))
        hooks = {}
        if NCH >= 16:
            for t in range(4):
                hooks.setdefault((t, 1), []).append(lambda t=t: q_proj(1, t, 6))
                hooks.setdefault((t, 13), []).append(lambda t=t: q_T(1, t, 7))
            hooks.setdefault((3, 14), []).append(lambda: (W.release(pq[2][0]), W.release(pq[3][0])))
        else:
            for t in range(4):
                q_proj(1, t, t % 2)
                q_T(1, t, 2 + t % 2)
            W.release(pq[2][0])
            W.release(pq[3][0])
        QTtok = [[("W8b", n_, t) for t in range(4)] for n_ in range(2)]
        Osb = f32w[:, 4:6, :]
        steps = [(j, c) for j in range(8) for c in range(NCH)]

        def qk(si):
            j, c = steps[si]
            kvh = j // 2
            sb = (si % 2) * 2

            def f(e):
                e.matmul(bank(sb), KT[0:64, kvh, c * 128:(c + 1) * 128], QT[0:64, j, :], start=True, stop=True)
                return e.matmul(bank(sb + 1), KT[64:128, kvh, c * 128:(c + 1) * 128], QT[64:128, j, :], start=True, stop=True)
            P.op("pe", f, reads=[("KT", c // 4)] + QTtok[j // 4], writes=[PSB(sb), PSB(sb + 1)])

        def ex(si):
            sb = (si % 2) * 2
            pb = 14 + (si % 3) * 2
            P.op("act", lambda e: e.activation(out=gT[:, pb:pb + 2, :].rearrange("p a b -> p (a b)"), in_=bank(sb, 2),
                                               func=AF.Exp, scale=0.125),
                 writes=[PSB(sb), PSB(sb + 1), ("gT", pb), ("gT", pb + 1)])

        def pv(si):
            j, c = steps[si]
            kvh = j // 2
            pb = 14 + (si % 3) * 2

            def f(e):
                e.matmul(ps[0:65, 4 * 512:5 * 512], Vaug[:, c, kvh, 64:129], gT[:, pb, :], start=(c == 0), stop=(c == NCH - 1))
                return e.matmul(bank(5), Vaug[:, c, kvh, 0:128], gT[:, pb + 1, :], start=(c == 0), stop=(c == NCH - 1))
            P.op("pe", f, reads=[("V", c // 4), "Vinit", ("gT", pb), ("gT", pb + 1)], writes=[PSB(4), PSB(5)])

        def fin_a(j):
            P.op("dve", lambda e: e.tensor_copy(out=Osb[0:65, 0, :], in_=ps[0:65, 4 * 512:5 * 512]), writes=[PSB(4), ("f32w", 4)])
            P.op("dve", lambda e: e.tensor_copy(out=Osb[:, 1, :], in_=bank(5)), writes=[PSB(5), ("f32w", 5)])
            P.op("dve", lambda e: e.reciprocal(out=Osb[64:65, 0, :], in_=Osb[64:65, 0, :]), reads=[("f32w", 4)], writes=[("f32w", 4)])
            P.op("dve", lambda e: e.reciprocal(out=Osb[0:1, 1, :], in_=Osb[0:1, 1, :]), reads=[("f32w", 5)], writes=[("f32w", 5)])

        def fin_b(j):
            def f(e):
                e.matmul(ps[0:64, 6 * 512:7 * 512], ones32[64:65, 0:64], Osb[64:65, 0, :], start=True, stop=True)
                return e.matmul(ps[64:128, 7 * 512:8 * 512], ones32[0:1, 0:64], Osb[0:1, 1, :], start=True, stop=True)
            P.op("pe", f, reads=["ones32", ("f32w", 4), ("f32w", 5)], writes=[PSB(6), PSB(7)])
            P.op("dve", lambda e: e.tensor_tensor(out=OTn[0:64, j, :], in0=Osb[0:64, 0, :], in1=ps[0:64, 6 * 512:7 * 512], op=ALU.mult),
                 reads=[("f32w", 4)], writes=[PSB(6), ("OTn", j)])
            P.op("dve", lambda e: e.tensor_tensor(out=OTn[64:128, j, :], in0=Osb[64:128, 1, :], in1=ps[64:128, 7 * 512:8 * 512], op=ALU.mult),
                 reads=[("f32w", 5)], writes=[PSB(7), ("OTn", j)])

        nsteps = len(steps)
        qk(0)
        qk(1)
        for si in range(nsteps):
            ex(si)
            if si + 2 < nsteps:
                qk(si + 2)
            pv(si)
            j, c = steps[si]
            if c == NCH - 1:
                fin_a(j)
            if c == NCH // 2 and j > 0:
                fin_b(j - 1)
            for hk in hooks.get((j, c), []):
                hk()
        pend = {"fin": True}
        mTb = W8b

        def mi(fi):
            return 4 + ((fi + 2) % 4)
        for fh in range(2):
            for br in range(3):
                gp = [W.get(w_in, 0, 8, O_G + br * 1024 + fh * 512 + q_ * 256, 256) for q_ in range(2)]
                if br == 0:
                    bp = [W.get(Wd["p_conv"], 0, 4, fh * 512, 512)]
                elif br == 1:
                    bp = [W.get(Wd["p_attn"], 0, 8, fh * 512 + q_ * 256, 256) for q_ in range(2)]
                else:
                    bp = [W.get(Wd["p_mem"], 0, 4, fh * 512, 512)]
                for fi in range(4):
                    if pend["fin"] and fi == 2:
                        pend["fin"] = False
                        fin_b(7)
                    f_ = fh * 4 + fi
                    gb, yb = nbank(), nbank()
                    gwv = gp[fi // 2][1]

                    def fg(e, fi=fi, gb=gb, gwv=gwv):
                        for k in range(8):
                            i = e.matmul(bank(gb), gwv[:, k, (fi % 2) * 128:(fi % 2) * 128 + 128], hT[:, k, :], start=(k == 0), stop=(k == 7))
                        return i
                    P.op("pe", fg, reads=[gp[fi // 2][2]] + uTtok, writes=[PSB(gb)])
                    if br == 0:
                        def fy(e, fi=fi, yb=yb, wv=bp[0][1]):
                            for k in range(4):
                                i = e.matmul(bank(yb), wv[:, k, fi * 128:(fi + 1) * 128], gT[:, 10 + k, :], start=(k == 0), stop=(k == 3))
                            return i
                        rd = [bp[0][2]] + [("gT", 10 + k) for k in range(4)]
                    elif br == 1:
                        def fy(e, fi=fi, yb=yb, wv=bp[fi // 2][1]):
                            for k in range(8):
                                i = e.matmul(bank(yb), wv[:, k, (fi % 2) * 128:(fi % 2) * 128 + 128], OTn[:, k, :], start=(k == 0), stop=(k == 7))
                            return i
                        rd = [bp[fi // 2][2]] + [("OTn", k) for k in range(8)]
                    else:
                        def fy(e, fi=fi, yb=yb, wv=bp[0][1]):
                            for k in range(4):
                                i = e.matmul(bank(yb), wv[:, k, fi * 128:(fi + 1) * 128], gT[:, 6 + k, :], start=(k == 0), stop=(k == 3))
                            return i
                        rd = [bp[0][2]] + [("gT", 6 + k) for k in range(4)]
                    P.op("pe", fy, reads=rd, writes=[PSB(yb)])
                    gi_ = fi % 2
                    gcol = br * 8 + f_
                    P.op("act", lambda e, gb=gb, gi_=gi_, gcol=gcol: e.activation(
                        out=f32w[:, gi_, :], in_=bank(gb), func=AF.Tanh, bias=bgate[:, gcol:gcol + 1], scale=0.5),
                        reads=["bgate"], writes=[PSB(gb), ("f32w", gi_)])
                    if br == 0:
                        P.op("dve", lambda e, fi=fi, yb=yb, gi_=gi_: e.scalar_tensor_tensor(
                            out=f32w[:, mi(fi), :], in0=f32w[:, gi_, :], scalar=1.0, in1=bank(yb), op0=ALU.add, op1=ALU.mult),
                            reads=[("f32w", gi_)], writes=[PSB(yb), ("f32w", mi(fi))])
                    else:
                        P.op("dve", lambda e, fi=fi, yb=yb, gi_=gi_: e.scalar_tensor_tensor(
                            out=f32w[:, 2 + gi_, :], in0=f32w[:, gi_, :], scalar=1.0, in1=bank(yb), op0=ALU.add, op1=ALU.mult),
                            reads=[("f32w", gi_)], writes=[PSB(yb), ("f32w", 2 + gi_)])
                        if br == 1:
                            P.op("dve", lambda e, fi=fi, gi_=gi_: e.tensor_tensor(
                                out=f32w[:, mi(fi), :], in0=f32w[:, mi(fi), :], in1=f32w[:, 2 + gi_, :], op=ALU.add),
                                reads=[("f32w", mi(fi)), ("f32w", 2 + gi_)], writes=[("f32w", mi(fi))])
                        else:
                            P.op("dve", lambda e, fi=fi, gi_=gi_, f_=f_: e.tensor_tensor(
                                out=mTb[:, f_, :], in0=f32w[:, mi(fi), :], in1=f32w[:, 2 + gi_, :], op=ALU.add),
                                reads=[("f32w", mi(fi)), ("f32w", 2 + gi_)], writes=W8b_all)
                for pc in gp[:1]:
                    pass
                for pc in gp + bp:
                    W.release(pc[0])

        mtoks = W8b_all
        pw = [W.get(Wd["w_out"], kh * 512, 4, half * 512, 512) for half in range(2) for kh in range(2)]
        for t in range(4):
            for half in range(2):
                b = (4 + t) if half == 0 else t

                def f(e, t=t, b=b, half=half):
                    for kk in range(8):
                        i = e.matmul(bank(b), mTb[:, kk, t * 128:(t + 1) * 128], pw[half * 2 + kk // 4][1][:, kk % 4, :],
                                     start=(kk == 0), stop=(kk == 7))
                    return i
                P.op("pe", f, reads=[pw[half * 2][2], pw[half * 2 + 1][2]] + mtoks, writes=[PSB(b)])
            epi_half0(t, 1, 4 + t)
            epi_sub(t, xb, 1, ceps4, t)
            if t >= 2:
                build_sub_B(t - 2, stg[:, t % 2, :], [("stg", t % 2)], 2, hT, "hT")
            build_sub_A(t, 0, xbuf[xb][:, t, :], [("xb", xb, t)], stg[:, t % 2, :], [("stg", t % 2)])
        for pc in pw:
            W.release(pc[0])
        build_sub_B(2, stg[:, 0, :], [("stg", 0)], 2, hT, "hT")
        build_sub_B(3, stg[:, 1, :], [("stg", 1)], 2, hT, "hT")

    def program():
        init()
        g = 0
        final_toks = []

        def xdma(dst_b, src_ap, rd):
            P.op("sp", lambda e: e.dma_start(out=xbuf[dst_b][:], in_=src_ap.rearrange("(t p) d -> p t d", p=128)),
                 reads=rd, writes=[("xb", dst_b, t) for t in range(4)], dma_sem=xl_sem[dst_b])

        for s in range(n_seq):
            xdma(g % 2, x_d[s, 0:512, :], [])
            mem_kv(s)
            for t in range(4):
                build_sub(t, 1, xbuf[g % 2][:, t, :], [("xb", g % 2, t)], 0, hT, "hT", b=nbank())
            for i in range(NT):
                b = g % 2
                ob = (g + 1) % 2
                if i + 1 < NT:
                    xdma(ob, x_d[s, (i + 1) * 512:(i + 2) * 512, :], [])

                    def pre(ob=ob):
                        for t in range(4):
                            build_sub_A(t, 1, xbuf[ob][:, t, :], [("xb", ob, t)], *ostg(t))

                    def mid(t):
                        build_sub_B(t, ostg(t)[0], ostg(t)[1], 0, hT, "hT")
                else:
                    pre = mid = None

                def aftA(t, b=b):
                    sb = t % 2
                    build_sub_A(t, 0, xbuf[b][:, t, :], [("xb", b, t)], stg[:, sb, :], [("stg", sb)])

                def aftB(t):
                    sb = t % 2
                    build_sub_B(t, stg[:, sb, :], [("stg", sb)], 1, W8b, "W8b")
                ffn(b, "ffn1", 0, pre_half1=pre, mid_sub=mid, after_sub=(aftA, aftB), defer_tail=True)
                P.op("sp", lambda e, s=s, i=i, b=b: e.dma_start(
                    out=x1s_d[s, i * 512:(i + 1) * 512, :].rearrange("(t p) d -> p t d", p=128), in_=xbuf[b][:]),
                    reads=[("xb", b, t) for t in range(4)], writes=[("x1s", s, i)], dma_sem=xs_sem[b])
                p1_proj(b, i, pre_t={2: (lambda f=aftB: f(2)), 3: (lambda f=aftB: f(3))})
                g += 1
            xdma(g % 2, x1s_d[s, 0:512, :], [("x1s", s, 0)])
            for t in range(4):
                build_sub(t, 1, xbuf[g % 2][:, t, :], [("xb", g % 2, t)], 1, hT, "hT", b=nbank())
            for i in range(NT):
                b = g % 2
                ob = (g + 1) % 2
                if i + 1 < NT:
                    xdma(ob, x1s_d[s, (i + 1) * 512:(i + 2) * 512, :], [("x1s", s, i + 1)])

                    def pre(ob=ob):
                        for t in range(4):
                            build_sub_A(t, 1, xbuf[ob][:, t, :], [("xb", ob, t)], *ostg(t))

                    def mid(t):
                        build_sub_B(t, ostg(t)[0], ostg(t)[1], 1, hT, "hT")
                else:
                    pre = mid = None
                mixer(b, i)
                ffn(b, "ffn2", 2, pre_half1=pre, mid_sub=mid, after_sub=None)
                P.op("sp", lambda e, s=s, i=i, b=b: e.dma_start(
                    out=y_d[s, i * 512:(i + 1) * 512, :].rearrange("(t p) d -> p t d", p=128), in_=xbuf[b][:]),
                    reads=[("xb", b, t) for t in range(4)], writes=[("y", s, i)], dma_sem=xs_sem[b])
                final_toks.append(("y", s, i))
                g += 1
        P.op("sp", lambda e: e.nop(), reads=final_toks)

    P.enabled = False
    W.planning = True
    program()
    P.enabled = True
    P.reset()
    rr["b"] = 0
    bb["i"] = 0
    W.start_real()
    program()
    assert W.n_acq == len(W.plan)
    P.emit()
    return nc


def _rope_tables(T):
    rows = T // 64
    row = np.repeat(np.arange(rows), 64).astype(np.float32)
    col = np.tile(np.arange(64), rows).astype(np.float32)
    inv = (1.0 / (np.float32(10000.0) ** (np.arange(0, 32, 2, dtype=np.float32) / np.float32(32)))).astype(np.float32)
    ang = np.concatenate([row[:, None] * inv, col[:, None] * inv], axis=-1).astype(np.float32)
    cs = np.stack([np.cos(ang), np.sin(ang)], axis=1).astype(np.float32)
    nch = T // 128
    return np.ascontiguousarray(cs.reshape(nch, 128, 2, 32).transpose(1, 0, 2, 3).reshape(128, nch * 64))


def _fm(v):
    return np.ascontiguousarray(np.asarray(v, np.float32).reshape(8, 128).T)


def make_shared_inputs(T, p):
    sh = {}
    for nm in ["ffn1_w1", "ffn1_w3", "ffn1_w2", "w_in", "p_conv", "p_attn", "w_mem_kv", "p_mem", "w_out",
               "ffn2_w1", "ffn2_w3", "ffn2_w2"]:
        sh[nm] = np.ascontiguousarray(p[nm], dtype=np.float32)
    sh["gfm"] = np.ascontiguousarray(np.concatenate(
        [_fm(p["ffn1_pre"]), _fm(p["mix_pre"]), _fm(p["ffn2_pre"]), _fm(p["mem_norm"])], axis=1))
    sh["gpost"] = np.ascontiguousarray(np.concatenate([p["ffn1_post"], p["mix_post"], p["ffn2_post"]]).astype(np.float32))
    sh["qkg"] = np.ascontiguousarray(np.concatenate([p["q_norm"], p["k_norm"]]).astype(np.float32))
    cw = np.asarray(p["conv_w"], np.float32)
    sh["convw"] = np.ascontiguousarray(cw.reshape(3, 4, 128).transpose(2, 0, 1).reshape(128, 12))
    sh["convb"] = np.ascontiguousarray(np.asarray(p["conv_b"], np.float32).reshape(4, 128).T)
    sh["bgate"] = np.ascontiguousarray(np.asarray(p["b_gate"], np.float32).reshape(24, 128).T)
    sh["rope"] = _rope_tables(T)
    sh["ident"] = np.eye(128, dtype=np.float32).astype(ml_dtypes.bfloat16)
    return sh


_NC_CACHE = {}


def kernel(x_prompt, x_sample, mem_prompt, mem_sample, **params):
    x_all = np.concatenate([np.asarray(x_prompt, np.float32), np.asarray(x_sample, np.float32)], axis=0)
    m_all = np.concatenate([np.asarray(mem_prompt, np.float32), np.asarray(mem_sample, np.float32)], axis=0)
    nb = x_all.shape[0]
    T = x_all.shape[1]
    per = nb // N_CORES
    p = {k: np.asarray(v)[0] for k, v in params.items()}
    sh = make_shared_inputs(T, p)
    key = (per, T)
    if key not in _NC_CACHE:
        _NC_CACHE[key] = build_program(per, T)
    nc = _NC_CACHE[key]
    in_maps = []
    for c in range(N_CORES):
        m = dict(sh)
        m["x"] = np.ascontiguousarray(x_all[c * per:(c + 1) * per])
        m["mem"] = np.ascontiguousarray(m_all[c * per:(c + 1) * per])
        in_maps.append(m)
    res = run_bass_kernel_spmd(nc, in_maps, core_ids=list(range(N_CORES)))
    y = np.concatenate([np.asarray(r["y"], np.float32) for r in res.results], axis=0)
    nbp = np.asarray(x_prompt).shape[0]
    return (np.ascontiguousarray(y[:nbp]), np.ascontiguousarray(y[nbp:]))
```

```python
import numpy as np
import ml_dtypes
import concourse.bass as bass
import concourse.mybir as mybir
from concourse.bass_utils import run_bass_kernel_spmd

F32 = mybir.dt.float32
BF16 = mybir.dt.bfloat16
AF = mybir.ActivationFunctionType
ALU = mybir.AluOpType
AX = mybir.AxisListType
ENGS = ["pe", "act", "dve", "pool", "sp"]

D = 1024
DFF = 2816
NF = DFF // 128
EPS = 1e-6
N_CORES = 8
SEQ = 2048
O_CX, O_CB, O_CC, O_Q, O_K, O_V, O_QM, O_G = 0, 512, 1024, 1536, 2560, 2816, 3072, 3584


class Op:
    __slots__ = ("eng", "fn", "deps", "flag", "incval", "dma_sem", "dma_val")


class Prog:
    def __init__(self, nc):
        self.nc = nc
        self.enabled = True
        self.reset()
        self.sem = {e: nc.alloc_semaphore("s_" + e) for e in ENGS}

    def reset(self):
        self.ops = {e: [] for e in ENGS}
        self.lastw = {}
        self.readers = {}
        self.dma_counts = {}

    def op(self, eng, fn, reads=(), writes=(), dma_sem=None):
        if not self.enabled:
            return None
        o = Op()
        o.eng = eng; o.fn = fn; o.deps = []; o.flag = False
        o.dma_sem = dma_sem; o.incval = 0; o.dma_val = 0
        if dma_sem is not None:
            c = self.dma_counts.get(dma_sem, 0) + 1
            self.dma_counts[dma_sem] = c
            o.dma_val = 16 * c
        for t in reads:
            w = self.lastw.get(t)
            if w is not None:
                self._dep(o, w, True)
        for t in writes:
            rd = self.readers.get(t)
            if rd:
                for r in rd.values():
                    self._dep(o, r, False)
            w = self.lastw.get(t)
            if w is not None:
                self._dep(o, w, False)
        for t in writes:
            self.lastw[t] = o
            self.readers[t] = {}
        for t in reads:
            rd = self.readers.setdefault(t, {})
            rd[id(o) if dma_sem is not None else eng] = o
        self.ops[eng].append(o)
        return o

    def _dep(self, c, p, raw):
        if p is c:
            return
        if p.dma_sem is None and c.dma_sem is None and p.eng == c.eng:
            if (not raw) or p.eng == "pe":
                return
        for q in c.deps:
            if q is p:
                return
        c.deps.append(p)
        if p.dma_sem is None:
            p.flag = True

    def emit(self):
        nc = self.nc
        for e in ENGS:
            n = 0
            for o in self.ops[e]:
                if o.flag:
                    n += 1
                    o.incval = n
        semof = self.sem
        ops = self.ops

        def run(e, eng):
            waited = {}
            for o in ops[e]:
                need = {}
                for p in o.deps:
                    if p.dma_sem is not None:
                        s, v = p.dma_sem, p.dma_val
                    else:
                        s, v = semof[p.eng], p.incval
                    if waited.get(s, 0) < v and need.get(s, 0) < v:
                        need[s] = v
                for s, v in need.items():
                    eng.wait_ge(s, v)
                    waited[s] = v
                ins = o.fn(eng)
                if o.dma_sem is not None:
                    ins.then_inc(o.dma_sem, 16)
                elif o.flag:
                    ins.then_inc(semof[e], 1)

        with nc.Block() as block:
            @block.tensor
            def _(t):
                run("pe", t)

            @block.scalar
            def _(t):
                run("act", t)

            @block.vector
            def _(t):
                run("dve", t)

            @block.gpsimd
            def _(t):
                run("pool", t)

            @block.sync
            def _(t):
                run("sp", t)


class WStream:
    def __init__(self, P, nc, nslots=8):
        self.P = P
        self.S = nslots
        self.slots = [nc.alloc_sbuf_tensor("ring%d" % i, [128, 2048], BF16) for i in range(nslots)]
        self.sems = [nc.alloc_semaphore("rsem%d" % i) for i in range(nslots)]
        self.plan = []
        self.planning = True
        self.n_acq = 0
        self.n_issued = 0

    def start_real(self):
        self.planning = False
        self.n_acq = 0
        self.n_issued = 0
        for n in range(min(self.S, len(self.plan))):
            self._issue(n)

    def _view(self, n, nk, ncols):
        return self.slots[n % self.S][:, 0:nk * ncols].rearrange("p (k c) -> p k c", k=nk)

    def _issue(self, n):
        assert n == self.n_issued
        self.n_issued += 1
        W, r0, nk, c0, ncols = self.plan[n]
        dst = self._view(n, nk, ncols)
        src = W[r0:r0 + nk * 128, c0:c0 + ncols].rearrange("(k p) c -> p k c", p=128)
        self.P.op("pool", lambda e, dst=dst, src=src: e.dma_start(out=dst, in_=src),
                  writes=[("ring", n % self.S)], dma_sem=self.sems[n % self.S])

    def get(self, W, r0, nk, c0, ncols):
        n = self.n_acq
        self.n_acq += 1
        if self.planning:
            self.plan.append((W, r0, nk, c0, ncols))
        else:
            d = self.plan[n]
            assert d[1:] == (r0, nk, c0, ncols) and d[0] is W, "weight plan mismatch"
            assert n < self.n_issued, "piece not issued (too many pinned)"
        return n, self._view(n, nk, ncols), ("ring", n % self.S)

    def release(self, n):
        if self.planning:
            return
        m = n + self.S
        if m < len(self.plan):
            assert m == self.n_issued, "out-of-order release"
            self._issue(m)


def build_program(n_seq, T):
    NT = T // 512
    NCH = T // 128
    nc = bass.Bass("TRN2", target_bir_lowering=False)
    P = Prog(nc)

    def din(name, shape, dt=F32):
        return nc.dram_tensor(name, list(shape), dt, kind="ExternalInput").ap()

    x_d = din("x", [n_seq, T, D])
    mem_d = din("mem", [n_seq, 256, D])
    y_d = nc.dram_tensor("y", [n_seq, T, D], F32, kind="ExternalOutput").ap()
    x1s_d = nc.dram_tensor("x1s", [n_seq, T, D], F32).ap()
    Wd = {}
    for nm, shp in [("ffn1_w1", [D, DFF]), ("ffn1_w3", [D, DFF]), ("ffn1_w2", [DFF, D]),
                    ("w_in", [D, 6656]), ("p_conv", [512, D]), ("p_attn", [D, D]),
                    ("w_mem_kv", [D, D]), ("p_mem", [512, D]), ("w_out", [D, D]),
                    ("ffn2_w1", [D, DFF]), ("ffn2_w3", [D, DFF]), ("ffn2_w2", [DFF, D])]:
        Wd[nm] = din(nm, shp)
    gfm_d = din("gfm", [128, 4 * 8])
    gpost_d = din("gpost", [3 * D])
    qkg_d = din("qkg", [2 * 64])
    convw_d = din("convw", [128, 12])
    convb_d = din("convb", [128, 4])
    bgate_d = din("bgate", [128, 24])
    rope_d = din("rope", [128, NCH * 64])
    ident_d = din("ident", [128, 128], BF16)

    A = nc.alloc_sbuf_tensor
    xbuf = [A("xbuf%d" % i, [128, 4, D], F32) for i in range(2)]
    KT = A("KT", [128, 4, T], BF16)
    Vaug = A("Vaug", [128, NCH, 4, 129], BF16)
    zT = A("zT", [128, 4, T + 2], BF16)
    KmT = A("KmT", [128, 4, 256], BF16)
    Vm = A("Vm", [128, 2, 512], BF16)
    ropet = A("ropet", [128, NCH, 2, 32], F32)
    gpost = A("gpostsb", [128, 3, D], F32)
    gfm = A("gfmsb", [128, 4, 8], F32)
    qkg = A("qkgsb", [128, 2, 64], F32)
    convw = A("convwsb", [128, 3, 4], F32)
    convb = A("convbsb", [128, 4], F32)
    bgate = A("bgatesb", [128, 24], F32)
    ident = A("identsb", [128, 128], BF16)
    ones32 = A("ones32", [128, 64], F32)
    onesbf = A("onesbf", [128, 128], BF16)
    chalf = A("chalf", [128, 64], F32)
    cneg1 = A("cneg1", [128, 1], F32)
    ceps1 = A("ceps1", [128, 1], F32)
    ceps4 = A("ceps4", [128, 1], F32)
    hT = A("hT", [128, 8, 512], BF16)
    W8b = A("W8b", [128, 8, 512], BF16)
    OTn = A("OTn", [128, 8, 512], BF16)
    stg = A("stg", [128, 2, D], BF16)
    gT = A("gT", [128, NF, 512], BF16)
    f32w = A("f32w", [128, 8, 512], F32)
    junk = A("junk", [128, D], BF16)
    krot2b = A("krot", [128, 2, 4, 2, 64], BF16)
    qrot = A("qrot", [128, D], BF16)
    st_ss = A("st_ss", [128, 8], F32)
    st_pre = A("st_pre", [128, 4], F32)
    st_rstd = A("st_rstd", [128, 4], F32)
    stb_ss = A("stb_ss", [128, 8], F32)
    stb_pre = A("stb_pre", [128, 8], F32)
    stb_rstd = A("stb_rstd", [128, 8], F32)
    st_q = A("st_q", [128, 16], F32)
    st_qp = A("st_qp", [128, 16], F32)
    st_qr = A("st_qr", [128, 16], F32)
    ps = nc.alloc_psum_tensor("ps", [128, 8 * 512], F32)

    def bank(b, n=1):
        return ps[:, b * 512:(b + n) * 512]

    W = WStream(P, nc, 8)
    xl_sem = [nc.alloc_semaphore("xl%d" % i) for i in range(2)]
    xs_sem = [nc.alloc_semaphore("xs%d" % i) for i in range(2)]
    mem_sem = nc.alloc_semaphore("memsem")
    rr = {"b": 0}
    csems = {}

    def nbank():
        b = rr["b"]
        rr["b"] = (b + 1) % 8
        return b

    def PSB(b):
        return ("ps", b)

    def init():
        def cl(dst, src, tok, **kw):
            if tok not in csems:
                csems[tok] = nc.alloc_semaphore("c_" + tok)
            P.op("sp", lambda e: e.dma_start(out=dst, in_=src, **kw), writes=[tok], dma_sem=csems[tok])
        cl(gfm[:].rearrange("p a b -> p (a b)"), gfm_d, "gfm")
        cl(gpost[:].rearrange("p a b -> p (a b)"), gpost_d.partition_broadcast(128), "gpost")
        cl(qkg[:].rearrange("p a b -> p (a b)"), qkg_d.partition_broadcast(128), "qkg")
        cl(convw[:].rearrange("p a b -> p (a b)"), convw_d, "convw")
        cl(convb[:], convb_d, "convb")
        cl(bgate[:], bgate_d, "bgate")
        cl(ropet[:].rearrange("p a b c -> p (a b c)"), rope_d, "rope")
        cl(ident[:], ident_d, "ident")
        P.op("dve", lambda e: e.memset(ones32[:], 1.0), writes=["ones32"])
        P.op("dve", lambda e: e.memset(onesbf[:], 1.0), writes=["onesbf"])
        P.op("pool", lambda e: e.memset(chalf[:], -0.5), writes=["chalf"])
        P.op("pool", lambda e: e.memset(cneg1[:], -1.0), writes=["chalf"])
        P.op("pool", lambda e: e.memset(ceps1[:], EPS), writes=["chalf"])
        P.op("pool", lambda e: e.memset(ceps4[:], 4.0 * EPS), writes=["chalf"])
        P.op("dve", lambda e: e.memset(Vaug[:].rearrange("p a b c -> p (a b c)"), 0.0), writes=["Vinit"])
        P.op("dve", lambda e: e.memset(Vaug[:, :, :, 0:1], 1.0), writes=["Vinit"])
        P.op("dve", lambda e: e.memset(Vaug[:, :, :, 128:129], 1.0), writes=["Vinit"])
        P.op("dve", lambda e: e.memset(zT[:, :, 0:1], 0.0), writes=["zinit"])
        P.op("dve", lambda e: e.memset(zT[:, :, T + 1:T + 2], 0.0), writes=["zinit"])
        P.op("dve", lambda e: e.tensor_scalar(out=gpost[:, 0, :], in0=gpost[:, 0, :], scalar1=0.5, scalar2=None,
                                              op0=ALU.mult), reads=["gpost"], writes=["gpost"])
        P.op("dve", lambda e: e.tensor_scalar(out=gpost[:, 2, :], in0=gpost[:, 2, :], scalar1=0.5, scalar2=None,
                                              op0=ALU.mult), reads=["gpost"], writes=["gpost"])
        P.op("dve", lambda e: e.tensor_scalar(out=bgate[:], in0=bgate[:], scalar1=0.5, scalar2=None,
                                              op0=ALU.mult), reads=["bgate"], writes=["bgate"])

    def rstd_chain(ms_ap, n, eps_tile, pre_ap, out_ap, rtok, wtok):
        P.op("pool", lambda e: e.tensor_tensor(out=pre_ap, in0=ms_ap, in1=eps_tile[:, 0:1].broadcast_to([128, n]), op=ALU.add),
             reads=list(rtok) + ["chalf"], writes=[wtok + "_pre"])
        P.op("pool", lambda e: e.tensor_tensor(out=out_ap, in0=pre_ap, in1=chalf[:, 0:n], op=ALU.pow),
             reads=[wtok + "_pre", "chalf"], writes=[wtok])

    def build_T(src_aps, src_toks, gi, dstT, dst_toks, nsub=4):
        for t in range(nsub):
            P.op("act", lambda e, t=t: e.activation(out=junk[:], in_=src_aps[t], func=AF.Square, scale=float(D ** -0.5),
                                                    accum_out=st_ss[:, t:t + 1]),
                 reads=list(src_toks[t]), writes=["junk", ("st_ss", t)])
            rstd_chain(st_ss[:, t:t + 1], 1, ceps1, st_pre[:, t:t + 1], st_rstd[:, t:t + 1], [("st_ss", t)], "st_rstd%d" % t)
        for t in range(nsub):
            sb = t % 2
            P.op("dve", lambda e, t=t, sb=sb: e.tensor_scalar(out=stg[:, sb, :], in0=src_aps[t],
                                                               scalar1=st_rstd[:, t:t + 1], scalar2=None, op0=ALU.mult),
                 reads=list(src_toks[t]) + ["st_rstd%d" % t], writes=[("stg", sb)])
            b = nbank()
            pT = bank(b).bitcast(BF16)

            def tr(e, sb=sb, pT=pT):
                for k in range(8):
                    i = e.transpose(pT[:, k * 128:(k + 1) * 128], stg[:, sb, k * 128:(k + 1) * 128], ident[:])
                return i
            P.op("pe", tr, reads=[("stg", sb), "ident"], writes=[PSB(b)])
            P.op("dve", lambda e, t=t, pT=pT: e.tensor_tensor(
                out=dstT[:, :, t * 128:(t + 1) * 128], in0=pT.rearrange("p (k t) -> p k t", k=8),
                in1=gfm[:, gi, :].unsqueeze(2).broadcast_to([128, 8, 128]), op=ALU.mult),
                reads=["gfm"], writes=[PSB(b)] + list(dst_toks[t]))

    bb = {"i": 0}

    def build_bank():
        b = 4 + (bb["i"] % 4)
        bb["i"] += 1
        return b

    def build_sub_A(t, slot, src_ap, src_toks, stg_ap, stg_toks):
        c = slot * 4 + t
        nm = "stb%d" % c
        P.op("act", lambda e: e.activation(out=junk[:], in_=src_ap, func=AF.Square, scale=float(D ** -0.5),
                                           accum_out=stb_ss[:, c:c + 1]), reads=list(src_toks), writes=["junk", nm + "_ss"])
        rstd_chain(stb_ss[:, c:c + 1], 1, ceps1, stb_pre[:, c:c + 1], stb_rstd[:, c:c + 1], [nm + "_ss"], nm)
        P.op("dve", lambda e: e.tensor_scalar(out=stg_ap, in0=src_ap, scalar1=stb_rstd[:, c:c + 1], scalar2=None, op0=ALU.mult),
             reads=list(src_toks) + [nm], writes=list(stg_toks))

    def build_sub_B(t, stg_ap, stg_toks, gi, dstT, dname, b=None):
        if b is None:
            b = build_bank()
        pT = bank(b).bitcast(BF16)

        def tr(e):
            for k in range(8):
                i = e.transpose(pT[:, k * 128:(k + 1) * 128], stg_ap[:, k * 128:(k + 1) * 128], ident[:])
            return i
        P.op("pe", tr, reads=list(stg_toks) + ["ident"], writes=[PSB(b)])
        P.op("dve", lambda e: e.tensor_tensor(
            out=dstT[:, :, t * 128:(t + 1) * 128], in0=pT.rearrange("p (k t) -> p k t", k=8),
            in1=gfm[:, gi, :].unsqueeze(2).broadcast_to([128, 8, 128]), op=ALU.mult),
            reads=["gfm"], writes=[PSB(b), (dname, t)])

    def build_sub(t, slot, src_ap, src_toks, gi, dstT, dname, b=None):
        sb = (slot * 4 + t) % 2
        build_sub_A(t, slot, src_ap, src_toks, stg[:, sb, :], [("stg", sb)])
        build_sub_B(t, stg[:, sb, :], [("stg", sb)], gi, dstT, dname, b)

    def ostg(t):
        return OTn[:, 2 * t:2 * t + 2, :].rearrange("p a b -> p (a b)"), [("OTn", 2 * t), ("OTn", 2 * t + 1)]

    def epi_half0(t, gi, b0):
        P.op("act", lambda e: e.activation(out=junk[:, 0:512], in_=bank(b0), func=AF.Square, scale=float(D ** -0.5),
                                           accum_out=st_ss[:, 2 * t:2 * t + 1]), writes=[PSB(b0), "junk", ("st_ss", 2 * t)])
        P.op("dve", lambda e: e.tensor_tensor(out=f32w[:, 4 + t, :], in0=bank(b0), in1=gpost[:, gi, 0:512], op=ALU.mult),
             reads=["gpost"], writes=[PSB(b0), ("f32w", 4 + t)])

    def epi_sub(t, xb, gi, eps_tile, b1):
        P.op("act", lambda e: e.activation(out=junk[:, 512:1024], in_=bank(b1), func=AF.Square, scale=float(D ** -0.5),
                                           accum_out=st_ss[:, 2 * t + 1:2 * t + 2]), writes=[PSB(b1), "junk", ("st_ss", 2 * t + 1)])
        P.op("pool", lambda e: e.tensor_tensor(out=st_pre[:, t:t + 1], in0=st_ss[:, 2 * t:2 * t + 1],
                                               in1=st_ss[:, 2 * t + 1:2 * t + 2], op=ALU.add),
             reads=[("st_ss", 2 * t), ("st_ss", 2 * t + 1)], writes=["st_sum%d" % t])
        rstd_chain(st_pre[:, t:t + 1], 1, eps_tile, st_pre[:, t:t + 1], st_rstd[:, t:t + 1], ["st_sum%d" % t], "st_rstd%d" % t)
        P.op("dve", lambda e: e.scalar_tensor_tensor(
            out=xbuf[xb][:, t, 0:512], in0=f32w[:, 4 + t, :], scalar=st_rstd[:, t:t + 1], in1=xbuf[xb][:, t, 0:512],
            op0=ALU.mult, op1=ALU.add), reads=["st_rstd%d" % t, ("f32w", 4 + t), ("xb", xb, t)], writes=[("xb", xb, t)])
        tb = t % 2
        P.op("dve", lambda e: e.scalar_tensor_tensor(
            out=f32w[:, tb, :], in0=bank(b1), scalar=st_rstd[:, t:t + 1], in1=gpost[:, gi, 512:1024],
            op0=ALU.mult, op1=ALU.mult), reads=["st_rstd%d" % t, "gpost"], writes=[PSB(b1), ("f32w", tb)])
        P.op("dve", lambda e: e.tensor_tensor(
            out=xbuf[xb][:, t, 512:1024], in0=xbuf[xb][:, t, 512:1024], in1=f32w[:, tb, :], op=ALU.add),
            reads=[("f32w", tb), ("xb", xb, t)], writes=[("xb", xb, t)])

    def post_epilogue(xb, gi, eps_tile, acc_srcs):
        for t in range(4):
            for h in range(2):
                ap, toks = acc_srcs[h][t]
                P.op("act", lambda e, t=t, h=h, ap=ap: e.activation(out=junk[:, 0:512], in_=ap, func=AF.Square, scale=float(D ** -0.5),
                                                               accum_out=st_ss[:, 2 * t + h:2 * t + h + 1]),
                     writes=list(toks) + ["junk", ("st_ss", 2 * t + h)])
            P.op("pool", lambda e, t=t: e.tensor_tensor(out=st_pre[:, t:t + 1], in0=st_ss[:, 2 * t:2 * t + 1],
                                                        in1=st_ss[:, 2 * t + 1:2 * t + 2], op=ALU.add),
                 reads=[("st_ss", 2 * t), ("st_ss", 2 * t + 1)], writes=["st_sum%d" % t])
            rstd_chain(st_pre[:, t:t + 1], 1, eps_tile, st_pre[:, t:t + 1], st_rstd[:, t:t + 1], ["st_sum%d" % t], "st_rstd%d" % t)
        for t in range(4):
            for h in range(2):
                ap, toks = acc_srcs[h][t]
                tb = 2 + h
                P.op("dve", lambda e, t=t, h=h, ap=ap, tb=tb: e.scalar_tensor_tensor(
                    out=f32w[:, tb, :], in0=ap, scalar=st_rstd[:, t:t + 1], in1=gpost[:, gi, h * 512:(h + 1) * 512],
                    op0=ALU.mult, op1=ALU.mult), reads=["st_rstd%d" % t, "gpost"], writes=list(toks) + [("f32w", tb)])
                P.op("pool" if h == 1 else "dve", lambda e, t=t, h=h, tb=tb: e.tensor_tensor(
                    out=xbuf[xb][:, t, h * 512:(h + 1) * 512], in0=xbuf[xb][:, t, h * 512:(h + 1) * 512],
                    in1=f32w[:, tb, :], op=ALU.add), reads=[("f32w", tb), ("xb", xb, t)], writes=[("xb", xb, t)])

    def ffn(xb, pfx, gi_post, pre_half1=None, mid_sub=None, after_sub=None, defer_tail=False):
        w1, w3, w2 = Wd[pfx + "_w1"], Wd[pfx + "_w3"], Wd[pfx + "_w2"]
        hTtok = [("hT", t) for t in range(4)]
        hb = {"i": 0}

        def w2_part(g):
            nk = 4 if g < 5 else 2
            n, wv, tok = W.get(w2, g * 512, nk, 0, 512)
            for t in range(4):
                b = 4 + t

                def f(e, t=t, b=b, wv=wv, g=g, nk=nk):
                    for k in range(nk):
                        j = g * 4 + k
                        i = e.matmul(bank(b), gT[:, j, t * 128:(t + 1) * 128], wv[:, k, :],
                                     start=(j == 0), stop=(j == NF - 1))
                    return i
                P.op("pe", f, reads=[tok] + [("gT", g * 4 + k) for k in range(nk)], writes=[PSB(b)])
            W.release(n)

        for g in range(6):
            npairs = 2 if g < 5 else 1
            for jp in range(2 * g, 2 * g + npairs):
                n1, wv1, tok1 = W.get(w1, 0, 8, jp * 256, 256)
                n3, wv3, tok3 = W.get(w3, 0, 8, jp * 256, 256)
                for jj in range(2):
                    j = 2 * jp + jj
                    pb = (hb["i"] % 2) * 2
                    hb["i"] += 1
                    b1, b3 = pb, pb + 1
                    fb_i = j % 2

                    def f(e, jj=jj, b1=b1, b3=b3, wv1=wv1, wv3=wv3):
                        for k in range(8):
                            e.matmul(bank(b1), wv1[:, k, jj * 128:(jj + 1) * 128], hT[:, k, :], start=(k == 0), stop=(k == 7))
                        for k in range(8):
                            i = e.matmul(bank(b3), wv3[:, k, jj * 128:(jj + 1) * 128], hT[:, k, :], start=(k == 0), stop=(k == 7))
                        return i
                    P.op("pe", f, reads=[tok1, tok3] + hTtok, writes=[PSB(b1), PSB(b3)])
                    P.op("act", lambda e, b1=b1, fb_i=fb_i: e.activation(out=f32w[:, fb_i, :], in_=bank(b1), func=AF.Tanh, scale=0.5),
                         writes=[PSB(b1), ("f32w", fb_i)])
                    P.op("dve", lambda e, b1=b1, fb_i=fb_i: e.scalar_tensor_tensor(
                        out=f32w[:, 2 + fb_i, :], in0=f32w[:, fb_i, :], scalar=1.0, in1=bank(b1), op0=ALU.add, op1=ALU.mult),
                        reads=[("f32w", fb_i)], writes=[PSB(b1), ("f32w", 2 + fb_i)])
                    P.op("dve", lambda e, b3=b3, fb_i=fb_i, j=j: e.tensor_tensor(
                        out=gT[:, j, :], in0=f32w[:, 2 + fb_i, :], in1=bank(b3), op=ALU.mult),
                        reads=[("f32w", 2 + fb_i)], writes=[PSB(b3), ("gT", j)])
                W.release(n1)
                W.release(n3)
            if g >= 1:
                w2_part(g - 1)
        w2_part(5)
        for t in range(4):
            epi_half0(t, gi_post, 4 + t)
        if pre_half1 is not None:
            pre_half1()
        pcs = [W.get(w2, g * 512, 4 if g < 5 else 2, 512, 512) for g in range(6)]
        for t in range(4):
            for g in range(6):
                if t > 0 and g > 0:
                    break

                def f(e, t=t, g0=g):
                    for g in (range(6) if t > 0 else [g0]):
                        for k in range(4 if g < 5 else 2):
                            j = g * 4 + k
                            i = e.matmul(bank(t), gT[:, j, t * 128:(t + 1) * 128], pcs[g][1][:, k, :], start=(j == 0), stop=(j == NF - 1))
                    return i
                rd = [pc[2] for pc in pcs] if t > 0 else [pcs[g][2]]
                P.op("pe", f, reads=rd + [("gT", j) for j in range(NF)], writes=[PSB(t)])
            epi_sub(t, xb, gi_post, ceps4, t)
            if mid_sub is not None:
                mid_sub(t)
            if after_sub is not None:
                if t >= 2:
                    after_sub[1](t - 2)
                after_sub[0](t)
        for pc in pcs:
            W.release(pc[0])
        if after_sub is not None and not defer_tail:
            after_sub[1](2)
            after_sub[1](3)

    def p1_proj(xb, tile, pre_t=None):
        w_in = Wd["w_in"]
        uTtok = [("W8b", t) for t in range(4)]
        nk_, wk, tokk = W.get(w_in, 0, 8, O_K, 256)
        nv_, wvv, tokv = W.get(w_in, 0, 8, O_V, 256)
        kb = [0, 1]
        vb = [2, 3]
        for t in range(4):
            if pre_t is not None and t in pre_t:
                pre_t[t]()
            def f(e, t=t):
                o = bank(kb[t // 2])[:, (t % 2) * 256:(t % 2) * 256 + 256]
                for k in range(8):
                    i = e.matmul(o, W8b[:, k, t * 128:(t + 1) * 128], wk[:, k, :], start=(k == 0), stop=(k == 7))
                return i
            P.op("pe", f, reads=[tokk, ("W8b", t)], writes=[PSB(kb[t // 2])])

            def f2(e, t=t):
                o = bank(vb[t // 2])[:, (t % 2) * 256:(t % 2) * 256 + 256]
                for k in range(8):
                    i = e.matmul(o, W8b[:, k, t * 128:(t + 1) * 128], wvv[:, k, :], start=(k == 0), stop=(k == 7))
                return i
            P.op("pe", f2, reads=[tokv, ("W8b", t)], writes=[PSB(vb[t // 2])])
        W.release(nk_)
        W.release(nv_)
        sqk = f32w[:, 0:2, :].rearrange("p a b -> p (a b)")
        for t in range(4):
            ch = tile * 4 + t
            src = bank(vb[t // 2])[:, (t % 2) * 256:(t % 2) * 256 + 256]
            P.op("act", lambda e, ch=ch, src=src: e.activation(
                out=Vaug[:, ch, :, 64:128], in_=src.rearrange("p (h d) -> p h d", h=4), func=AF.Copy),
                reads=["Vinit"], writes=[PSB(vb[t // 2]), ("V", tile)])
            srck = bank(kb[t // 2])[:, (t % 2) * 256:(t % 2) * 256 + 256]
            P.op("act", lambda e, t=t, srck=srck: e.activation(out=sqk[:, t * 256:(t + 1) * 256], in_=srck, func=AF.Square, scale=0.125),
                 writes=[PSB(kb[t // 2]), ("f32w", 0), ("f32w", 1)])
        P.op("dve", lambda e: e.tensor_reduce(out=st_q[:], in_=sqk.rearrange("p (a d) -> p a d", d=64), axis=AX.X, op=ALU.add),
             reads=[("f32w", 0), ("f32w", 1)], writes=["st_q"])
        rstd_chain(st_q[:], 16, ceps1, st_qp[:], st_qr[:], ["st_q"], "st_qr")
        kn = f32w[:, 2, 0:256].rearrange("p (h d) -> p h d", h=4)
        ta = f32w[:, 4, 0:128].rearrange("p (h i) -> p h i", h=4)
        tb = f32w[:, 5, 0:128].rearrange("p (h i) -> p h i", h=4)
        pcs = []
        for c0 in (O_CX, O_CC, O_CX + 256, O_CC + 256):
            pcs.append(W.get(w_in, 0, 8, c0, 256))

        def k_chain(t):
            krot = krot2b[:, t % 2]
            kt = "krot%d" % (t % 2)
            ch = tile * 4 + t
            srck = bank(kb[t // 2])[:, (t % 2) * 256:(t % 2) * 256 + 256].rearrange("p (h d) -> p h d", h=4)
            P.op("dve", lambda e: e.tensor_tensor(
                out=kn, in0=srck, in1=st_qr[:, t * 4:(t + 1) * 4].unsqueeze(2).broadcast_to([128, 4, 64]), op=ALU.mult),
                reads=["st_qr"], writes=[PSB(kb[t // 2]), ("f32w", 2)])
            P.op("dve", lambda e: e.tensor_tensor(out=kn, in0=kn, in1=qkg[:, 1, :].unsqueeze(1).broadcast_to([128, 4, 64]),
                                                  op=ALU.mult), reads=[("f32w", 2), "qkg"], writes=[("f32w", 2)])
            rope_ops(kn, krot[:, :, 0, :], 4, ch, ta, tb, [("f32w", 2)], kt)
            P.op("pool", lambda e: e.tensor_copy(out=krot[:, :, 1, :], in_=krot[:, :, 0, :]), reads=[kt + "_e", kt + "_o"], writes=[kt + "_d"])

        def k_T(t):
            krot = krot2b[:, t % 2]
            kt = "krot%d" % (t % 2)
            b = 2 + (t % 2)
            pT = bank(b).bitcast(BF16)

            def tr(e):
                for kv in range(4):
                    i = e.transpose(pT[:, kv * 128:(kv + 1) * 128], krot[:, kv, :, :].rearrange("p a d -> p (a d)"), ident[:])
                return i
            P.op("pe", tr, reads=[kt + "_e", kt + "_o", kt + "_d", "ident"], writes=[PSB(b)])
            P.op("act", lambda e: e.activation(
                out=KT[:, :, tile * 512 + t * 128: tile * 512 + (t + 1) * 128],
                in_=pT[:, 0:512].rearrange("p (k t) -> p k t", k=4), func=AF.Copy),
                writes=[PSB(b), ("KT", tile)])

        for c in range(4):
            k_chain(c)
            bx, bc_ = 4 + 2 * (c % 2), 5 + 2 * (c % 2)
            wx = pcs[2 * (c // 2)][1]
            wc = pcs[2 * (c // 2) + 1][1]

            def f(e, c=c, bx=bx, bc_=bc_, wx=wx, wc=wc):
                for k in range(8):
                    e.matmul(bank(bx), wx[:, k, (c % 2) * 128:(c % 2) * 128 + 128], W8b[:, k, :], start=(k == 0), stop=(k == 7))
                for k in range(8):
                    i = e.matmul(bank(bc_), wc[:, k, (c % 2) * 128:(c % 2) * 128 + 128], W8b[:, k, :], start=(k == 0), stop=(k == 7))
                return i
            P.op("pe", f, reads=[pcs[2 * (c // 2)][2], pcs[2 * (c // 2) + 1][2]] + uTtok, writes=[PSB(bx), PSB(bc_)])
            if c >= 1:
                k_T(c - 1)
            fi = 6 + (c % 2)
            P.op("act", lambda e, bx=bx, fi=fi: e.activation(out=f32w[:, fi, :], in_=bank(bx), func=AF.Copy),
                 writes=[PSB(bx), ("f32w", fi)])
            P.op("dve", lambda e, c=c, bc_=bc_, fi=fi: e.tensor_tensor(
                out=zT[:, c, 1 + tile * 512: 1 + (tile + 1) * 512], in0=f32w[:, fi, :], in1=bank(bc_), op=ALU.mult),
                reads=[("f32w", fi), "zinit"], writes=[PSB(bc_), ("zT", tile)])
        k_T(3)
        for pc in pcs:
            W.release(pc[0])

    def rope_ops(src, dst, H, ch, ta, tb, src_toks, dst_tok, tc=None, td=None, toks=(4, 5, 6, 7)):
        sv = src.rearrange("p h (i two) -> p h i two", two=2)
        dv = dst.rearrange("p h (i two) -> p h i two", two=2)
        cosb = ropet[:, ch, 0, :].unsqueeze(1).broadcast_to([128, H, 32])
        sinb = ropet[:, ch, 1, :].unsqueeze(1).broadcast_to([128, H, 32])
        x1, x2 = sv[:, :, :, 0], sv[:, :, :, 1]
        rA, rB = ("f32w", toks[0]), ("f32w", toks[1])
        srd = list(src_toks) + ["rope"]
        P.op("dve", lambda e: e.tensor_tensor(out=ta, in0=x1, in1=cosb, op=ALU.mult), reads=srd, writes=[rA])
        P.op("dve", lambda e: e.tensor_tensor(out=tb, in0=x2, in1=sinb, op=ALU.mult), reads=srd, writes=[rB])
        P.op("dve", lambda e: e.tensor_tensor(out=dv[:, :, :, 0], in0=ta, in1=tb, op=ALU.subtract),
             reads=[rA, rB], writes=[dst_tok + "_e"])
        if tc is None:
            eng2, t2a, t2b, rC, rD = "dve", ta, tb, rA, rB
        else:
            eng2, t2a, t2b, rC, rD = "pool", tc, td, ("f32w", toks[2]), ("f32w", toks[3])
        P.op(eng2, lambda e: e.tensor_tensor(out=t2a, in0=x1, in1=sinb, op=ALU.mult), reads=srd, writes=[rC])
        P.op(eng2, lambda e: e.tensor_tensor(out=t2b, in0=x2, in1=cosb, op=ALU.mult), reads=srd, writes=[rD])
        P.op("dve", lambda e: e.tensor_tensor(out=dv[:, :, :, 1], in0=t2a, in1=t2b, op=ALU.add),
             reads=[rC, rD], writes=[dst_tok + "_o"])

    def mem_kv(s):
        wm = Wd["w_mem_kv"]
        mt = f32w[:, 0:4, :].rearrange("p (a b) c -> p a (b c)", a=2)
        P.op("sp", lambda e: e.dma_start(out=mt, in_=mem_d[s].rearrange("(a p) d -> p a d", p=128)),
             writes=[("f32w", i) for i in range(4)], dma_sem=mem_sem)
        memT = OTn[:, :, 0:256]
        otoks = [("OTn", k) for k in range(8)]
        build_T([mt[:, a, :] for a in range(2)], [[("f32w", 0), ("f32w", 1)], [("f32w", 2), ("f32w", 3)]], 3, OTn,
                [otoks, otoks], nsub=2)
        mtok = otoks
        pk = [W.get(wm, 0, 8, 0, 256), W.get(wm, 0, 8, 256, 256)]
        for h in range(4):
            b = nbank()
            wv = pk[h // 2][1]

            def f(e, h=h, b=b, wv=wv):
                for k in range(8):
                    i = e.matmul(bank(b)[:, 0:256], wv[:, k, (h % 2) * 128:(h % 2) * 128 + 128], memT[:, k, :],
                                 start=(k == 0), stop=(k == 7))
                return i
            P.op("pe", f, reads=[pk[h // 2][2]] + mtok, writes=[PSB(b)])
            P.op("act", lambda e, h=h, b=b: e.activation(out=KmT[:, h, :], in_=bank(b)[:, 0:256], func=AF.Copy),
                 writes=[PSB(b), "KmT"])
        W.release(pk[0][0])
        W.release(pk[1][0])
        pv = [W.get(wm, 0, 4, 512, 512), W.get(wm, 512, 4, 512, 512)]
        for mc in range(2):
            b = nbank()

            def f(e, mc=mc, b=b):
                for k in range(8):
                    i = e.matmul(bank(b), memT[:, k, mc * 128:(mc + 1) * 128], pv[k // 4][1][:, k % 4, :],
                                 start=(k == 0), stop=(k == 7))
                return i
            P.op("pe", f, reads=[pv[0][2], pv[1][2]] + mtok, writes=[PSB(b)])
            P.op("act", lambda e, mc=mc, b=b: e.activation(out=Vm[:, mc, :], in_=bank(b), func=AF.Copy),
                 writes=[PSB(b), "Vm"])
        W.release(pv[0][0])
        W.release(pv[1][0])

    W8b_all = [("W8b", t_) for t_ in range(4)] + [("W8b", n_, t_) for n_ in range(2) for t_ in range(4)]

    def mixer(xb, tile):
        w_in = Wd["w_in"]
        uTtok = [("hT", t) for t in range(4)]
        QT = W8b
        pq = [W.get(w_in, kh * 512, 4, O_Q, 512) for kh in range(2)]
        sqh = f32w[:, 0, :]
        qnh = f32w[:, 2, :].rearrange("p (h d) -> p h d", h=8)
        ta = f32w[:, 1, 0:256].rearrange("p (h i) -> p h i", h=8)
        tb = f32w[:, 3, 0:256].rearrange("p (h i) -> p h i", h=8)
        tc_ = f32w[:, 6, 0:256].rearrange("p (h i) -> p h i", h=8)
        td_ = f32w[:, 7, 0:256].rearrange("p (h i) -> p h i", h=8)

        def q_dst(n_, t):
            if n_ == 0:
                return OTn[:, t, :], "qr0_%d" % t
            return qrot[:, 512:1024], "qr1"

        def q_proj(n_, t, qb):
            def f(e):
                for k in range(8):
                    i = e.matmul(bank(qb), hT[:, k, t * 128:(t + 1) * 128], pq[n_ * 2 + k // 4][1][:, k % 4, :],
                                 start=(k == 0), stop=(k == 7))
                return i
            P.op("pe", f, reads=[pq[n_ * 2][2], pq[n_ * 2 + 1][2], ("hT", t)], writes=[PSB(qb)])
            ch = tile * 4 + t
            P.op("act", lambda e: e.activation(out=sqh, in_=bank(qb), func=AF.Square, scale=0.125), writes=[PSB(qb), ("f32w", 0)])
            P.op("dve", lambda e: e.tensor_reduce(out=st_q[:, 0:8], in_=sqh.rearrange("p (a d) -> p a d", d=64), axis=AX.X, op=ALU.add),
                 reads=[("f32w", 0)], writes=["st_q"])
            rstd_chain(st_q[:, 0:8], 8, ceps1, st_qp[:, 0:8], st_qr[:, 0:8], ["st_q"], "st_qr")
            P.op("dve", lambda e: e.tensor_tensor(
                out=qnh, in0=bank(qb).rearrange("p (h d) -> p h d", h=8),
                in1=st_qr[:, 0:8].unsqueeze(2).broadcast_to([128, 8, 64]), op=ALU.mult),
                reads=["st_qr"], writes=[PSB(qb), ("f32w", 2)])
            P.op("dve", lambda e: e.tensor_tensor(out=qnh, in0=qnh, in1=qkg[:, 0, :].unsqueeze(1).broadcast_to([128, 8, 64]),
                                                  op=ALU.mult), reads=[("f32w", 2), "qkg"], writes=[("f32w", 2)])
            dst, dtok = q_dst(n_, t)
            rope_ops(qnh, dst.rearrange("p (h d) -> p h d", h=8), 8, ch, ta, tb, [("f32w", 2)],
                     dtok, tc=tc_, td=td_, toks=(1, 3, 6, 7))

        def q_T(n_, t, b):
            pT = bank(b).bitcast(BF16)

            src, dtok = q_dst(n_, t)

            def tr(e):
                for k in range(4):
                    i = e.transpose(pT[:, k * 128:(k + 1) * 128], src[:, k * 128:(k + 1) * 128], ident[:])
                return i
            P.op("pe", tr, reads=[dtok + "_e", dtok + "_o", "ident"], writes=[PSB(b)])
            P.op("act", lambda e: e.activation(out=QT[:, 4 * n_:4 * n_ + 4, t * 128:(t + 1) * 128],
                                               in_=pT[:, 0:512].rearrange("p (k t) -> p k t", k=4), func=AF.Copy),
                 writes=[PSB(b), ("W8b", n_, t)])

        for t in range(4):
            q_proj(0, t, t)
        W.release(pq[0][0])
        W.release(pq[1][0])
        pqm = [W.get(w_in, 0, 8, O_QM, 256), W.get(w_in, 0, 8, O_QM + 256, 256)]
        for h in range(4):
            b = 4 + h
            wv = pqm[h // 2][1]

            def f(e, h=h, b=b, wv=wv):
                for k in range(8):
                    i = e.matmul(bank(b), wv[:, k, (h % 2) * 128:(h % 2) * 128 + 128], hT[:, k, :], start=(k == 0), stop=(k == 7))
                return i
            P.op("pe", f, reads=[pqm[h // 2][2]] + uTtok, writes=[PSB(b)])
            P.op("act", lambda e, h=h, b=b: e.activation(out=gT[:, h, :], in_=bank(b), func=AF.Copy),
                 writes=[PSB(b), ("gT", h)])
        W.release(pqm[0][0])
        W.release(pqm[1][0])
        pcb = [W.get(w_in, 0, 8, O_CB, 256), W.get(w_in, 0, 8, O_CB + 256, 256)]
        ztoks = [("zT", i) for i in (tile - 1, tile, tile + 1) if 0 <= i < NT] + ["zinit"]
        for c in range(4):
            b = 4 + c
            wv = pcb[c // 2][1]

            def f(e, c=c, b=b, wv=wv):
                for k in range(8):
                    i = e.matmul(bank(b), wv[:, k, (c % 2) * 128:(c % 2) * 128 + 128], hT[:, k, :], start=(k == 0), stop=(k == 7))
                return i
            P.op("pe", f, reads=[pcb[c // 2][2]] + uTtok, writes=[PSB(b)])
            fi = 6 + (c % 2)
            z0 = 1 + tile * 512
            P.op("dve", lambda e, c=c, fi=fi, z0=z0: e.tensor_scalar(
                out=f32w[:, fi, :], in0=zT[:, c, z0:z0 + 512], scalar1=convw[:, 1, c:c + 1], scalar2=convb[:, c:c + 1],
                op0=ALU.mult, op1=ALU.add), reads=ztoks + ["convw", "convb"], writes=[("f32w", fi)])
            P.op("dve", lambda e, c=c, fi=fi, z0=z0: e.scalar_tensor_tensor(
                out=f32w[:, fi, :], in0=zT[:, c, z0 - 1:z0 + 511], scalar=convw[:, 0, c:c + 1], in1=f32w[:, fi, :],
                op0=ALU.mult, op1=ALU.add), reads=ztoks + [("f32w", fi)], writes=[("f32w", fi)])
            P.op("dve", lambda e, c=c, fi=fi, z0=z0: e.scalar_tensor_tensor(
                out=f32w[:, fi, :], in0=zT[:, c, z0 + 1:z0 + 513], scalar=convw[:, 2, c:c + 1], in1=f32w[:, fi, :],
                op0=ALU.mult, op1=ALU.add), reads=ztoks + [("f32w", fi)], writes=[("f32w", fi)])
            P.op("dve", lambda e, c=c, fi=fi, b=b: e.tensor_tensor(out=gT[:, 10 + c, :], in0=f32w[:, fi, :], in1=bank(b), op=ALU.mult),
                 reads=[("f32w", fi)], writes=[PSB(b), ("gT", 10 + c)])
        W.release(pcb[0][0])
        W.release(pcb[1][0])
        def mem_S(h):
            sb = (h % 2) * 2
            pc0 = 4 if h % 2 == 0 else 14
            PmT = gT[:, pc0:pc0 + 2, :]

            def f(e):
                for mc in range(2):
                    i = e.matmul(bank(sb + mc), KmT[:, h, mc * 128:(mc + 1) * 128], gT[:, h, :], start=True, stop=True)
                return i
            P.op("pe", f, reads=["KmT", ("gT", h)], writes=[PSB(sb), PSB(sb + 1)])
            P.op("act", lambda e: e.activation(out=PmT.rearrange("p a b -> p (a b)"), in_=bank(sb, 2), func=AF.Exp,
                                               scale=float(128 ** -0.5)),
                 writes=[PSB(sb), PSB(sb + 1), ("gT", pc0), ("gT", pc0 + 1)])

        def mem_PV(h):
            ob, smb = 4 + (h % 2) * 2, 5 + (h % 2) * 2
            pc0 = 4 if h % 2 == 0 else 14
            PmT = gT[:, pc0:pc0 + 2, :]

            def f2(e):
                for mc in range(2):
                    e.matmul(bank(ob), Vm[:, mc, h * 128:(h + 1) * 128], PmT[:, mc, :], start=(mc == 0), stop=(mc == 1))
                for mc in range(2):
                    i = e.matmul(bank(smb), onesbf[:], PmT[:, mc, :], start=(mc == 0), stop=(mc == 1))
                return i
            P.op("pe", f2, reads=["Vm", "onesbf", ("gT", pc0), ("gT", pc0 + 1)], writes=[PSB(ob), PSB(smb)])
            P.op("dve", lambda e: e.reciprocal(out=f32w[:, 6, :], in_=bank(smb)), writes=[PSB(smb), ("f32w", 6)])
            P.op("dve", lambda e: e.tensor_tensor(out=gT[:, 6 + h, :], in0=f32w[:, 6, :], in1=bank(ob), op=ALU.mult),
                 reads=[("f32w", 6)], writes=[PSB(ob), ("gT", 6 + h)])

        mem_S(0)
        for h in range(4):
            if h + 1 < 4:
                mem_S(h + 1)
            mem_PV(h)
        for t in range(4):
            q_T(0, t, 4 + t)
        for kh in range(2):
            pq.append(W.get(w_in, kh * 512, 4, O_Q + 512, 512))
        hooks = {}
        if NCH >= 16:
            for t in range(4):
                hooks.setdefault((t, 1), []).append(lambda t=t: q_proj(1, t, 6))
                hooks.setdefault((t, 13), []).append(lambda t=t: q_T(1, t, 7))
            hooks.setdefault((3, 14), []).append(lambda: (W.release(pq[2][0]), W.release(pq[3][0])))
        else:
            for t in range(4):
                q_proj(1, t, t % 2)
                q_T(1, t, 2 + t % 2)
            W.release(pq[2][0])
            W.release(pq[3][0])
        QTtok = [[("W8b", n_, t) for t in range(4)] for n_ in range(2)]
        Osb = f32w[:, 4:6, :]
        steps = [(j, c) for j in range(8) for c in range(NCH)]

        def qk(si):
            j, c = steps[si]
            kvh = j // 2
            sb = (si % 2) * 2

            def f(e):
                e.matmul(bank(sb), KT[0:64, kvh, c * 128:(c + 1) * 128], QT[0:64, j, :], start=True, stop=True)
                return e.matmul(bank(sb + 1), KT[64:128, kvh, c * 128:(c + 1) * 128], QT[64:128, j, :], start=True, stop=True)
            P.op("pe", f, reads=[("KT", c // 4)] + QTtok[j // 4], writes=[PSB(sb), PSB(sb + 1)])

        def ex(si):
            sb = (si % 2) * 2
            pb = 14 + (si % 3) * 2
            P.op("act", lambda e: e.activation(out=gT[:, pb:pb + 2, :].rearrange("p a b -> p (a b)"), in_=bank(sb, 2),
                                               func=AF.Exp, scale=0.125),
                 writes=[PSB(sb), PSB(sb + 1), ("gT", pb), ("gT", pb + 1)])

        def pv(si):
            j, c = steps[si]
            kvh = j // 2
            pb = 14 + (si % 3) * 2

            def f(e):
                e.matmul(ps[0:65, 4 * 512:5 * 512], Vaug[:, c, kvh, 64:129], gT[:, pb, :], start=(c == 0), stop=(c == NCH - 1))
                return e.matmul(bank(5), Vaug[:, c, kvh, 0:128], gT[:, pb + 1, :], start=(c == 0), stop=(c == NCH - 1))
            P.op("pe", f, reads=[("V", c // 4), "Vinit", ("gT", pb), ("gT", pb + 1)], writes=[PSB(4), PSB(5)])

        def fin_a(j):
            P.op("dve", lambda e: e.tensor_copy(out=Osb[0:65, 0, :], in_=ps[0:65, 4 * 512:5 * 512]), writes=[PSB(4), ("f32w", 4)])
            P.op("dve", lambda e: e.tensor_copy(out=Osb[:, 1, :], in_=bank(5)), writes=[PSB(5), ("f32w", 5)])
            P.op("dve", lambda e: e.reciprocal(out=Osb[64:65, 0, :], in_=Osb[64:65, 0, :]), reads=[("f32w", 4)], writes=[("f32w", 4)])
            P.op("dve", lambda e: e.reciprocal(out=Osb[0:1, 1, :], in_=Osb[0:1, 1, :]), reads=[("f32w", 5)], writes=[("f32w", 5)])

        def fin_b(j):
            def f(e):
                e.matmul(ps[0:64, 6 * 512:7 * 512], ones32[64:65, 0:64], Osb[64:65, 0, :], start=True, stop=True)
                return e.matmul(ps[64:128, 7 * 512:8 * 512], ones32[0:1, 0:64], Osb[0:1, 1, :], start=True, stop=True)
            P.op("pe", f, reads=["ones32", ("f32w", 4), ("f32w", 5)], writes=[PSB(6), PSB(7)])
            P.op("dve", lambda e: e.tensor_tensor(out=OTn[0:64, j, :], in0=Osb[0:64, 0, :], in1=ps[0:64, 6 * 512:7 * 512], op=ALU.mult),
                 reads=[("f32w", 4)], writes=[PSB(6), ("OTn", j)])
            P.op("dve", lambda e: e.tensor_tensor(out=OTn[64:128, j, :], in0=Osb[64:128, 1, :], in1=ps[64:128, 7 * 512:8 * 512], op=ALU.mult),
                 reads=[("f32w", 5)], writes=[PSB(7), ("OTn", j)])

        nsteps = len(steps)
        qk(0)
        qk(1)
        for si in range(nsteps):
            ex(si)
            if si + 2 < nsteps:
                qk(si + 2)
            pv(si)
            j, c = steps[si]
            if c == NCH - 1:
                fin_a(j)
            if c == NCH // 2 and j > 0:
                fin_b(j - 1)
            for hk in hooks.get((j, c), []):
                hk()
        pend = {"fin": True}
        mTb = W8b

        def mi(fi):
            return 4 + ((fi + 2) % 4)
        for fh in range(2):
            for br in range(3):
                gp = [W.get(w_in, 0, 8, O_G + br * 1024 + fh * 512 + q_ * 256, 256) for q_ in range(2)]
                if br == 0:
                    bp = [W.get(Wd["p_conv"], 0, 4, fh * 512, 512)]
                elif br == 1:
                    bp = [W.get(Wd["p_attn"], 0, 8, fh * 512 + q_ * 256, 256) for q_ in range(2)]
                else:
                    bp = [W.get(Wd["p_mem"], 0, 4, fh * 512, 512)]
                for fi in range(4):
                    if pend["fin"] and fi == 2:
                        pend["fin"] = False
                        fin_b(7)
                    f_ = fh * 4 + fi
                    gb, yb = nbank(), nbank()
                    gwv = gp[fi // 2][1]

                    def fg(e, fi=fi, gb=gb, gwv=gwv):
                        for k in range(8):
                            i = e.matmul(bank(gb), gwv[:, k, (fi % 2) * 128:(fi % 2) * 128 + 128], hT[:, k, :], start=(k == 0), stop=(k == 7))
                        return i
                    P.op("pe", fg, reads=[gp[fi // 2][2]] + uTtok, writes=[PSB(gb)])
                    if br == 0:
                        def fy(e, fi=fi, yb=yb, wv=bp[0][1]):
                            for k in range(4):
                                i = e.matmul(bank(yb), wv[:, k, fi * 128:(fi + 1) * 128], gT[:, 10 + k, :], start=(k == 0), stop=(k == 3))
                            return i
                        rd = [bp[0][2]] + [("gT", 10 + k) for k in range(4)]
                    elif br == 1:
                        def fy(e, fi=fi, yb=yb, wv=bp[fi // 2][1]):
                            for k in range(8):
                                i = e.matmul(bank(yb), wv[:, k, (fi % 2) * 128:(fi % 2) * 128 + 128], OTn[:, k, :], start=(k == 0), stop=(k == 7))
                            return i
                        rd = [bp[fi // 2][2]] + [("OTn", k) for k in range(8)]
                    else:
                        def fy(e, fi=fi, yb=yb, wv=bp[0][1]):
                            for k in range(4):
                                i = e.matmul(bank(yb), wv[:, k, fi * 128:(fi + 1) * 128], gT[:, 6 + k, :], start=(k == 0), stop=(k == 3))
                            return i
                        rd = [bp[0][2]] + [("gT", 6 + k) for k in range(4)]
                    P.op("pe", fy, reads=rd, writes=[PSB(yb)])
                    gi_ = fi % 2
                    gcol = br * 8 + f_
                    P.op("act", lambda e, gb=gb, gi_=gi_, gcol=gcol: e.activation(
                        out=f32w[:, gi_, :], in_=bank(gb), func=AF.Tanh, bias=bgate[:, gcol:gcol + 1], scale=0.5),
                        reads=["bgate"], writes=[PSB(gb), ("f32w", gi_)])
                    if br == 0:
                        P.op("dve", lambda e, fi=fi, yb=yb, gi_=gi_: e.scalar_tensor_tensor(
                            out=f32w[:, mi(fi), :], in0=f32w[:, gi_, :], scalar=1.0, in1=bank(yb), op0=ALU.add, op1=ALU.mult),
                            reads=[("f32w", gi_)], writes=[PSB(yb), ("f32w", mi(fi))])
                    else:
                        P.op("dve", lambda e, fi=fi, yb=yb, gi_=gi_: e.scalar_tensor_tensor(
                            out=f32w[:, 2 + gi_, :], in0=f32w[:, gi_, :], scalar=1.0, in1=bank(yb), op0=ALU.add, op1=ALU.mult),
                            reads=[("f32w", gi_)], writes=[PSB(yb), ("f32w", 2 + gi_)])
                        if br == 1:
                            P.op("dve", lambda e, fi=fi, gi_=gi_: e.tensor_tensor(
                                out=f32w[:, mi(fi), :], in0=f32w[:, mi(fi), :], in1=f32w[:, 2 + gi_, :], op=ALU.add),
                                reads=[("f32w", mi(fi)), ("f32w", 2 + gi_)], writes=[("f32w", mi(fi))])
                        else:
                            P.op("dve", lambda e, fi=fi, gi_=gi_, f_=f_: e.tensor_tensor(
                                out=mTb[:, f_, :], in0=f32w[:, mi(fi), :], in1=f32w[:, 2 + gi_, :], op=ALU.add),
                                reads=[("f32w", mi(fi)), ("f32w", 2 + gi_)], writes=W8b_all)
                for pc in gp[:1]:
                    pass
                for pc in gp + bp:
                    W.release(pc[0])

        mtoks = W8b_all
        pw = [W.get(Wd["w_out"], kh * 512, 4, half * 512, 512) for half in range(2) for kh in range(2)]
        for t in range(4):
            for half in range(2):
                b = (4 + t) if half == 0 else t

                def f(e, t=t, b=b, half=half):
                    for kk in range(8):
                        i = e.matmul(bank(b), mTb[:, kk, t * 128:(t + 1) * 128], pw[half * 2 + kk // 4][1][:, kk % 4, :],
                                     start=(kk == 0), stop=(kk == 7))
                    return i
                P.op("pe", f, reads=[pw[half * 2][2], pw[half * 2 + 1][2]] + mtoks, writes=[PSB(b)])
            epi_half0(t, 1, 4 + t)
            epi_sub(t, xb, 1, ceps4, t)
            if t >= 2:
                build_sub_B(t - 2, stg[:, t % 2, :], [("stg", t % 2)], 2, hT, "hT")
            build_sub_A(t, 0, xbuf[xb][:, t, :], [("xb", xb, t)], stg[:, t % 2, :], [("stg", t % 2)])
        for pc in pw:
            W.release(pc[0])
        build_sub_B(2, stg[:, 0, :], [("stg", 0)], 2, hT, "hT")
        build_sub_B(3, stg[:, 1, :], [("stg", 1)], 2, hT, "hT")

    def program():
        init()
        g = 0
        final_toks = []

        def xdma(dst_b, src_ap, rd):
            P.op("sp", lambda e: e.dma_start(out=xbuf[dst_b][:], in_=src_ap.rearrange("(t p) d -> p t d", p=128)),
                 reads=rd, writes=[("xb", dst_b, t) for t in range(4)], dma_sem=xl_sem[dst_b])

        for s in range(n_seq):
            xdma(g % 2, x_d[s, 0:512, :], [])
            mem_kv(s)
            for t in range(4):
                build_sub(t, 1, xbuf[g % 2][:, t, :], [("xb", g % 2, t)], 0, hT, "hT", b=nbank())
            for i in range(NT):
                b = g % 2
                ob = (g + 1) % 2
                if i + 1 < NT:
                    xdma(ob, x_d[s, (i + 1) * 512:(i + 2) * 512, :], [])

                    def pre(ob=ob):
                        for t in range(4):
                            build_sub_A(t, 1, xbuf[ob][:, t, :], [("xb", ob, t)], *ostg(t))

                    def mid(t):
                        build_sub_B(t, ostg(t)[0], ostg(t)[1], 0, hT, "hT")
                else:
                    pre = mid = None

                def aftA(t, b=b):
                    sb = t % 2
                    build_sub_A(t, 0, xbuf[b][:, t, :], [("xb", b, t)], stg[:, sb, :], [("stg", sb)])

                def aftB(t):
                    sb = t % 2
                    build_sub_B(t, stg[:, sb, :], [("stg", sb)], 1, W8b, "W8b")
                ffn(b, "ffn1", 0, pre_half1=pre, mid_sub=mid, after_sub=(aftA, aftB), defer_tail=True)
                P.op("sp", lambda e, s=s, i=i, b=b: e.dma_start(
                    out=x1s_d[s, i * 512:(i + 1) * 512, :].rearrange("(t p) d -> p t d", p=128), in_=xbuf[b][:]),
                    reads=[("xb", b, t) for t in range(4)], writes=[("x1s", s, i)], dma_sem=xs_sem[b])
                p1_proj(b, i, pre_t={2: (lambda f=aftB: f(2)), 3: (lambda f=aftB: f(3))})
                g += 1
            xdma(g % 2, x1s_d[s, 0:512, :], [("x1s", s, 0)])
            for t in range(4):
                build_sub(t, 1, xbuf[g % 2][:, t, :], [("xb", g % 2, t)], 1, hT, "hT", b=nbank())
            for i in range(NT):
                b = g % 2
                ob = (g + 1) % 2
                if i + 1 < NT:
                    xdma(ob, x1s_d[s, (i + 1) * 512:(i + 2) * 512, :], [("x1s", s, i + 1)])

                    def pre(ob=ob):
                        for t in range(4):
                            build_sub_A(t, 1, xbuf[ob][:, t, :], [("xb", ob, t)], *ostg(t))

                    def mid(t):
                        build_sub_B(t, ostg(t)[0], ostg(t)[1], 1, hT, "hT")
                else:
                    pre = mid = None
                mixer(b, i)
                ffn(b, "ffn2", 2, pre_half1=pre, mid_sub=mid, after_sub=None)
                P.op("sp", lambda e, s=s, i=i, b=b: e.dma_start(
                    out=y_d[s, i * 512:(i + 1) * 512, :].rearrange("(t p) d -> p t d", p=128), in_=xbuf[b][:]),
                    reads=[("xb", b, t) for t in range(4)], writes=[("y", s, i)], dma_sem=xs_sem[b])
                final_toks.append(("y", s, i))
                g += 1
        P.op("sp", lambda e: e.nop(), reads=final_toks)

    P.enabled = False
    W.planning = True
    program()
    P.enabled = True
    P.reset()
    rr["b"] = 0
    bb["i"] = 0
    W.start_real()
    program()
    assert W.n_acq == len(W.plan)
    P.emit()
    return nc


def _rope_tables(T):
    rows = T // 64
    row = np.repeat(np.arange(rows), 64).astype(np.float32)
    col = np.tile(np.arange(64), rows).astype(np.float32)
    inv = (1.0 / (np.float32(10000.0) ** (np.arange(0, 32, 2, dtype=np.float32) / np.float32(32)))).astype(np.float32)
    ang = np.concatenate([row[:, None] * inv, col[:, None] * inv], axis=-1).astype(np.float32)
    cs = np.stack([np.cos(ang), np.sin(ang)], axis=1).astype(np.float32)
    nch = T // 128
    return np.ascontiguousarray(cs.reshape(nch, 128, 2, 32).transpose(1, 0, 2, 3).reshape(128, nch * 64))


def _fm(v):
    return np.ascontiguousarray(np.asarray(v, np.float32).reshape(8, 128).T)


def make_shared_inputs(T, p):
    sh = {}
    for nm in ["ffn1_w1", "ffn1_w3", "ffn1_w2", "w_in", "p_conv", "p_attn", "w_mem_kv", "p_mem", "w_out",
               "ffn2_w1", "ffn2_w3", "ffn2_w2"]:
        sh[nm] = np.ascontiguousarray(p[nm], dtype=np.float32)
    sh["gfm"] = np.ascontiguousarray(np.concatenate(
        [_fm(p["ffn1_pre"]), _fm(p["mix_pre"]), _fm(p["ffn2_pre"]), _fm(p["mem_norm"])], axis=1))
    sh["gpost"] = np.ascontiguousarray(np.concatenate([p["ffn1_post"], p["mix_post"], p["ffn2_post"]]).astype(np.float32))
    sh["qkg"] = np.ascontiguousarray(np.concatenate([p["q_norm"], p["k_norm"]]).astype(np.float32))
    cw = np.asarray(p["conv_w"], np.float32)
    sh["convw"] = np.ascontiguousarray(cw.reshape(3, 4, 128).transpose(2, 0, 1).reshape(128, 12))
    sh["convb"] = np.ascontiguousarray(np.asarray(p["conv_b"], np.float32).reshape(4, 128).T)
    sh["bgate"] = np.ascontiguousarray(np.asarray(p["b_gate"], np.float32).reshape(24, 128).T)
    sh["rope"] = _rope_tables(T)
    sh["ident"] = np.eye(128, dtype=np.float32).astype(ml_dtypes.bfloat16)
    return sh


_NC_CACHE = {}


def kernel(x_prompt, x_sample, mem_prompt, mem_sample, **params):
    x_all = np.concatenate([np.asarray(x_prompt, np.float32), np.asarray(x_sample, np.float32)], axis=0)
    m_all = np.concatenate([np.asarray(mem_prompt, np.float32), np.asarray(mem_sample, np.float32)], axis=0)
    nb = x_all.shape[0]
    T = x_all.shape[1]
    per = nb // N_CORES
    p = {k: np.asarray(v)[0] for k, v in params.items()}
    sh = make_shared_inputs(T, p)
    key = (per, T)
    if key not in _NC_CACHE:
        _NC_CACHE[key] = build_program(per, T)
    nc = _NC_CACHE[key]
    in_maps = []
    for c in range(N_CORES):
        m = dict(sh)
        m["x"] = np.ascontiguousarray(x_all[c * per:(c + 1) * per])
        m["mem"] = np.ascontiguousarray(m_all[c * per:(c + 1) * per])
        in_maps.append(m)
    res = run_bass_kernel_spmd(nc, in_maps, core_ids=list(range(N_CORES)))
    y = np.concatenate([np.asarray(r["y"], np.float32) for r in res.results], axis=0)
    nbp = np.asarray(x_prompt).shape[0]
    return (np.ascontiguousarray(y[:nbp]), np.ascontiguousarray(y[nbp:]))
```

```python
import numpy as np
import ml_dtypes
import concourse.bass as bass
import concourse.mybir as mybir
from concourse.bass_utils import run_bass_kernel_spmd

F32 = mybir.dt.float32
BF16 = mybir.dt.bfloat16
AF = mybir.ActivationFunctionType
ALU = mybir.AluOpType
AX = mybir.AxisListType
ENGS = ["pe", "act", "dve", "pool", "sp"]

D = 1024
DFF = 2816
NF = DFF // 128
EPS = 1e-6
N_CORES = 8
SEQ = 2048
O_CX, O_CB, O_CC, O_Q, O_K, O_V, O_QM, O_G = 0, 512, 1024, 1536, 2560, 2816, 3072, 3584


class Op:
    __slots__ = ("eng", "fn", "deps", "flag", "incval", "dma_sem", "dma_val")


class Prog:
    def __init__(self, nc):
        self.nc = nc
        self.enabled = True
        self.reset()
        self.sem = {e: nc.alloc_semaphore("s_" + e) for e in ENGS}

    def reset(self):
        self.ops = {e: [] for e in ENGS}
        self.lastw = {}
        self.readers = {}
        self.dma_counts = {}

    def op(self, eng, fn, reads=(), writes=(), dma_sem=None):
        if not self.enabled:
            return None
        o = Op()
        o.eng = eng; o.fn = fn; o.deps = []; o.flag = False
        o.dma_sem = dma_sem; o.incval = 0; o.dma_val = 0
        if dma_sem is not None:
            c = self.dma_counts.get(dma_sem, 0) + 1
            self.dma_counts[dma_sem] = c
            o.dma_val = 16 * c
        for t in reads:
            w = self.lastw.get(t)
            if w is not None:
                self._dep(o, w, True)
        for t in writes:
            rd = self.readers.get(t)
            if rd:
                for r in rd.values():
                    self._dep(o, r, False)
            w = self.lastw.get(t)
            if w is not None:
                self._dep(o, w, False)
        for t in writes:
            self.lastw[t] = o
            self.readers[t] = {}
        for t in reads:
            rd = self.readers.setdefault(t, {})
            rd[id(o) if dma_sem is not None else eng] = o
        self.ops[eng].append(o)
        return o

    def _dep(self, c, p, raw):
        if p is c:
            return
        if p.dma_sem is None and c.dma_sem is None and p.eng == c.eng:
            if (not raw) or p.eng == "pe":
                return
        for q in c.deps:
            if q is p:
                return
        c.deps.append(p)
        if p.dma_sem is None:
            p.flag = True

    def emit(self):
        nc = self.nc
        for e in ENGS:
            n = 0
            for o in self.ops[e]:
                if o.flag:
                    n += 1
                    o.incval = n
        semof = self.sem
        ops = self.ops

        def run(e, eng):
            waited = {}
            for o in ops[e]:
                need = {}
                for p in o.deps:
                    if p.dma_sem is not None:
                        s, v = p.dma_sem, p.dma_val
                    else:
                        s, v = semof[p.eng], p.incval
                    if waited.get(s, 0) < v and need.get(s, 0) < v:
                        need[s] = v
                for s, v in need.items():
                    eng.wait_ge(s, v)
                    waited[s] = v
                ins = o.fn(eng)
                if o.dma_sem is not None:
                    ins.then_inc(o.dma_sem, 16)
                elif o.flag:
                    ins.then_inc(semof[e], 1)

        with nc.Block() as block:
            @block.tensor
            def _(t):
                run("pe", t)

            @block.scalar
            def _(t):
                run("act", t)

            @block.vector
            def _(t):
                run("dve", t)

            @block.gpsimd
            def _(t):
                run("pool", t)

            @block.sync
            def _(t):
                run("sp", t)


class WStream:
    def __init__(self, P, nc, nslots=8):
        self.P = P
        self.S = nslots
        self.slots = [nc.alloc_sbuf_tensor("ring%d" % i, [128, 2048], BF16) for i in range(nslots)]
        self.sems = [nc.alloc_semaphore("rsem%d" % i) for i in range(nslots)]
        self.plan = []
        self.planning = True
        self.n_acq = 0
        self.n_issued = 0

    def start_real(self):
        self.planning = False
        self.n_acq = 0
        self.n_issued = 0
        for n in range(min(self.S, len(self.plan))):
            self._issue(n)

    def _view(self, n, nk, ncols):
        return self.slots[n % self.S][:, 0:nk * ncols].rearrange("p (k c) -> p k c", k=nk)

    def _issue(self, n):
        assert n == self.n_issued
        self.n_issued += 1
        W, r0, nk, c0, ncols = self.plan[n]
        dst = self._view(n, nk, ncols)
        src = W[r0:r0 + nk * 128, c0:c0 + ncols].rearrange("(k p) c -> p k c", p=128)
        self.P.op("pool", lambda e, dst=dst, src=src: e.dma_start(out=dst, in_=src),
                  writes=[("ring", n % self.S)], dma_sem=self.sems[n % self.S])

    def get(self, W, r0, nk, c0, ncols):
        n = self.n_acq
        self.n_acq += 1
        if self.planning:
            self.plan.append((W, r0, nk, c0, ncols))
        else:
            d = self.plan[n]
            assert d[1:] == (r0, nk, c0, ncols) and d[0] is W, "weight plan mismatch"
            assert n < self.n_issued, "piece not issued (too many pinned)"
        return n, self._view(n, nk, ncols), ("ring", n % self.S)

    def release(self, n):
        if self.planning:
            return
        m = n + self.S
        if m < len(self.plan):
            assert m == self.n_issued, "out-of-order release"
            self._issue(m)


def build_program(n_seq, T):
    NT = T // 512
    NCH = T // 128
    nc = bass.Bass("TRN2", target_bir_lowering=False)
    P = Prog(nc)

    def din(name, shape, dt=F32):
        return nc.dram_tensor(name, list(shape), dt, kind="ExternalInput").ap()

    x_d = din("x", [n_seq, T, D])
    mem_d = din("mem", [n_seq, 256, D])
    y_d = nc.dram_tensor("y", [n_seq, T, D], F32, kind="ExternalOutput").ap()
    x1s_d = nc.dram_tensor("x1s", [n_seq, T, D], F32).ap()
    Wd = {}
    for nm, shp in [("ffn1_w1", [D, DFF]), ("ffn1_w3", [D, DFF]), ("ffn1_w2", [DFF, D]),
                    ("w_in", [D, 6656]), ("p_conv", [512, D]), ("p_attn", [D, D]),
                    ("w_mem_kv", [D, D]), ("p_mem", [512, D]), ("w_out", [D, D]),
                    ("ffn2_w1", [D, DFF]), ("ffn2_w3", [D, DFF]), ("ffn2_w2", [DFF, D])]:
        Wd[nm] = din(nm, shp)
    gfm_d = din("gfm", [128, 4 * 8])
    gpost_d = din("gpost", [3 * D])
    qkg_d = din("qkg", [2 * 64])
    convw_d = din("convw", [128, 12])
    convb_d = din("convb", [128, 4])
    bgate_d = din("bgate", [128, 24])
    rope_d = din("rope", [128, NCH * 64])
    ident_d = din("ident", [128, 128], BF16)

    A = nc.alloc_sbuf_tensor
    xbuf = [A("xbuf%d" % i, [128, 4, D], F32) for i in range(2)]
    KT = A("KT", [128, 4, T], BF16)
    Vaug = A("Vaug", [128, NCH, 4, 129], BF16)
    zT = A("zT", [128, 4, T + 2], BF16)
    KmT = A("KmT", [128, 4, 256], BF16)
    Vm = A("Vm", [128, 2, 512], BF16)
    ropet = A("ropet", [128, NCH, 2, 32], F32)
    gpost = A("gpostsb", [128, 3, D], F32)
    gfm = A("gfmsb", [128, 4, 8], F32)
    qkg = A("qkgsb", [128, 2, 64], F32)
    convw = A("convwsb", [128, 3, 4], F32)
    convb = A("convbsb", [128, 4], F32)
    bgate = A("bgatesb", [128, 24], F32)
    ident = A("identsb", [128, 128], BF16)
    ones32 = A("ones32", [128, 64], F32)
    onesbf = A("onesbf", [128, 128], BF16)
    chalf = A("chalf", [128, 64], F32)
    cneg1 = A("cneg1", [128, 1], F32)
    ceps1 = A("ceps1", [128, 1], F32)
    ceps4 = A("ceps4", [128, 1], F32)
    hT = A("hT", [128, 8, 512], BF16)
    W8b = A("W8b", [128, 8, 512], BF16)
    OTn = A("OTn", [128, 8, 512], BF16)
    stg = A("stg", [128, 2, D], BF16)
    gT = A("gT", [128, NF, 512], BF16)
    f32w = A("f32w", [128, 8, 512], F32)
    junk = A("junk", [128, D], BF16)
    krot2b = A("krot", [128, 2, 4, 2, 64], BF16)
    qrot = A("qrot", [128, D], BF16)
    st_ss = A("st_ss", [128, 8], F32)
    st_pre = A("st_pre", [128, 4], F32)
    st_rstd = A("st_rstd", [128, 4], F32)
    stb_ss = A("stb_ss", [128, 8], F32)
    stb_pre = A("stb_pre", [128, 8], F32)
    stb_rstd = A("stb_rstd", [128, 8], F32)
    st_q2 = A("st_q2", [128, 32], F32)
    st_qp2 = A("st_qp2", [128, 32], F32)
    st_qr2 = A("st_qr2", [128, 32], F32)
    st_q = A("st_q", [128, 16], F32)
    st_qp = A("st_qp", [128, 16], F32)
    st_qr = A("st_qr", [128, 16], F32)
    ps = nc.alloc_psum_tensor("ps", [128, 8 * 512], F32)

    def bank(b, n=1):
        return ps[:, b * 512:(b + n) * 512]

    W = WStream(P, nc, 8)
    xl_sem = [nc.alloc_semaphore("xl%d" % i) for i in range(2)]
    xs_sem = [nc.alloc_semaphore("xs%d" % i) for i in range(2)]
    mem_sem = nc.alloc_semaphore("memsem")
    rr = {"b": 0}
    csems = {}

    def nbank():
        b = rr["b"]
        rr["b"] = (b + 1) % 8
        return b

    def PSB(b):
        return ("ps", b)

    def init():
        def cl(dst, src, tok, **kw):
            if tok not in csems:
                csems[tok] = nc.alloc_semaphore("c_" + tok)
            P.op("sp", lambda e: e.dma_start(out=dst, in_=src, **kw), writes=[tok], dma_sem=csems[tok])
        cl(gfm[:].rearrange("p a b -> p (a b)"), gfm_d, "gfm")
        cl(gpost[:].rearrange("p a b -> p (a b)"), gpost_d.partition_broadcast(128), "gpost")
        cl(qkg[:].rearrange("p a b -> p (a b)"), qkg_d.partition_broadcast(128), "qkg")
        cl(convw[:].rearrange("p a b -> p (a b)"), convw_d, "convw")
        cl(convb[:], convb_d, "convb")
        cl(bgate[:], bgate_d, "bgate")
        cl(ropet[:].rearrange("p a b c -> p (a b c)"), rope_d, "rope")
        cl(ident[:], ident_d, "ident")
        P.op("dve", lambda e: e.memset(ones32[:], 1.0), writes=["ones32"])
        P.op("dve", lambda e: e.memset(onesbf[:], 1.0), writes=["onesbf"])
        P.op("pool", lambda e: e.memset(chalf[:], -0.5), writes=["chalf"])
        P.op("pool", lambda e: e.memset(cneg1[:], -1.0), writes=["chalf"])
        P.op("pool", lambda e: e.memset(ceps1[:], EPS), writes=["chalf"])
        P.op("pool", lambda e: e.memset(ceps4[:], 4.0 * EPS), writes=["chalf"])
        P.op("dve", lambda e: e.memset(Vaug[:].rearrange("p a b c -> p (a b c)"), 0.0), writes=["Vinit"])
        P.op("dve", lambda e: e.memset(Vaug[:, :, :, 0:1], 1.0), writes=["Vinit"])
        P.op("dve", lambda e: e.memset(Vaug[:, :, :, 128:129], 1.0), writes=["Vinit"])
        P.op("dve", lambda e: e.memset(zT[:, :, 0:1], 0.0), writes=["zinit"])
        P.op("dve", lambda e: e.memset(zT[:, :, T + 1:T + 2], 0.0), writes=["zinit"])
        P.op("dve", lambda e: e.tensor_scalar(out=gpost[:, 0, :], in0=gpost[:, 0, :], scalar1=0.5, scalar2=None,
                                              op0=ALU.mult), reads=["gpost"], writes=["gpost"])
        P.op("dve", lambda e: e.tensor_scalar(out=gpost[:, 2, :], in0=gpost[:, 2, :], scalar1=0.5, scalar2=None,
                                              op0=ALU.mult), reads=["gpost"], writes=["gpost"])
        P.op("dve", lambda e: e.tensor_scalar(out=bgate[:], in0=bgate[:], scalar1=0.5, scalar2=None,
                                              op0=ALU.mult), reads=["bgate"], writes=["bgate"])

    def rstd_chain(ms_ap, n, eps_tile, pre_ap, out_ap, rtok, wtok):
        P.op("pool", lambda e: e.tensor_tensor(out=pre_ap, in0=ms_ap, in1=eps_tile[:, 0:1].broadcast_to([128, n]), op=ALU.add),
             reads=list(rtok) + ["chalf"], writes=[wtok + "_pre"])
        P.op("pool", lambda e: e.tensor_tensor(out=out_ap, in0=pre_ap, in1=chalf[:, 0:n], op=ALU.pow),
             reads=[wtok + "_pre", "chalf"], writes=[wtok])

    def build_T(src_aps, src_toks, gi, dstT, dst_toks, nsub=4):
        for t in range(nsub):
            P.op("act", lambda e, t=t: e.activation(out=junk[:], in_=src_aps[t], func=AF.Square, scale=float(D ** -0.5),
                                                    accum_out=st_ss[:, t:t + 1]),
                 reads=list(src_toks[t]), writes=["junk", ("st_ss", t)])
            rstd_chain(st_ss[:, t:t + 1], 1, ceps1, st_pre[:, t:t + 1], st_rstd[:, t:t + 1], [("st_ss", t)], "st_rstd%d" % t)
        for t in range(nsub):
            sb = t % 2
            P.op("dve", lambda e, t=t, sb=sb: e.tensor_scalar(out=stg[:, sb, :], in0=src_aps[t],
                                                               scalar1=st_rstd[:, t:t + 1], scalar2=None, op0=ALU.mult),
                 reads=list(src_toks[t]) + ["st_rstd%d" % t], writes=[("stg", sb)])
            b = nbank()
            pT = bank(b).bitcast(BF16)

            def tr(e, sb=sb, pT=pT):
                for k in range(8):
                    i = e.transpose(pT[:, k * 128:(k + 1) * 128], stg[:, sb, k * 128:(k + 1) * 128], ident[:])
                return i
            P.op("pe", tr, reads=[("stg", sb), "ident"], writes=[PSB(b)])
            P.op("dve", lambda e, t=t, pT=pT: e.tensor_tensor(
                out=dstT[:, :, t * 128:(t + 1) * 128], in0=pT.rearrange("p (k t) -> p k t", k=8),
                in1=gfm[:, gi, :].unsqueeze(2).broadcast_to([128, 8, 128]), op=ALU.mult),
                reads=["gfm"], writes=[PSB(b)] + list(dst_toks[t]))

    bb = {"i": 0}

    def build_bank():
        b = 4 + (bb["i"] % 4)
        bb["i"] += 1
        return b

    def build_sub_A(t, slot, src_ap, src_toks, stg_ap, stg_toks):
        c = slot * 4 + t
        nm = "stb%d" % c
        P.op("act", lambda e: e.activation(out=junk[:], in_=src_ap, func=AF.Square, scale=float(D ** -0.5),
                                           accum_out=stb_ss[:, c:c + 1]), reads=list(src_toks), writes=["junk", nm + "_ss"])
        rstd_chain(stb_ss[:, c:c + 1], 1, ceps1, stb_pre[:, c:c + 1], stb_rstd[:, c:c + 1], [nm + "_ss"], nm)
        P.op("dve", lambda e: e.tensor_scalar(out=stg_ap, in0=src_ap, scalar1=stb_rstd[:, c:c + 1], scalar2=None, op0=ALU.mult),
             reads=list(src_toks) + [nm], writes=list(stg_toks))

    def build_sub_B(t, stg_ap, stg_toks, gi, dstT, dname, b=None):
        if b is None:
            b = build_bank()
        pT = bank(b).bitcast(BF16)

        def tr(e):
            for k in range(8):
                i = e.transpose(pT[:, k * 128:(k + 1) * 128], stg_ap[:, k * 128:(k + 1) * 128], ident[:])
            return i
        P.op("pe", tr, reads=list(stg_toks) + ["ident"], writes=[PSB(b)])
        P.op("dve", lambda e: e.tensor_tensor(
            out=dstT[:, :, t * 128:(t + 1) * 128], in0=pT.rearrange("p (k t) -> p k t", k=8),
            in1=gfm[:, gi, :].unsqueeze(2).broadcast_to([128, 8, 128]), op=ALU.mult),
            reads=["gfm"], writes=[PSB(b), (dname, t)])

    def build_sub(t, slot, src_ap, src_toks, gi, dstT, dname, b=None):
        sb = (slot * 4 + t) % 2
        build_sub_A(t, slot, src_ap, src_toks, stg[:, sb, :], [("stg", sb)])
        build_sub_B(t, stg[:, sb, :], [("stg", sb)], gi, dstT, dname, b)

    def ostg(t):
        return OTn[:, 2 * t:2 * t + 2, :].rearrange("p a b -> p (a b)"), [("OTn", 2 * t), ("OTn", 2 * t + 1)]

    def epi_half0(t, gi, b0):
        P.op("act", lambda e: e.activation(out=junk[:, 0:512], in_=bank(b0), func=AF.Square, scale=float(D ** -0.5),
                                           accum_out=st_ss[:, 2 * t:2 * t + 1]), writes=[PSB(b0), "junk", ("st_ss", 2 * t)])
        P.op("dve", lambda e: e.tensor_tensor(out=f32w[:, 4 + t, :], in0=bank(b0), in1=gpost[:, gi, 0:512], op=ALU.mult),
             reads=["gpost"], writes=[PSB(b0), ("f32w", 4 + t)])

    def epi_sub(t, xb, gi, eps_tile, b1):
        P.op("act", lambda e: e.activation(out=junk[:, 512:1024], in_=bank(b1), func=AF.Square, scale=float(D ** -0.5),
                                           accum_out=st_ss[:, 2 * t + 1:2 * t + 2]), writes=[PSB(b1), "junk", ("st_ss", 2 * t + 1)])
        P.op("pool", lambda e: e.tensor_tensor(out=st_pre[:, t:t + 1], in0=st_ss[:, 2 * t:2 * t + 1],
                                               in1=st_ss[:, 2 * t + 1:2 * t + 2], op=ALU.add),
             reads=[("st_ss", 2 * t), ("st_ss", 2 * t + 1)], writes=["st_sum%d" % t])
        rstd_chain(st_pre[:, t:t + 1], 1, eps_tile, st_pre[:, t:t + 1], st_rstd[:, t:t + 1], ["st_sum%d" % t], "st_rstd%d" % t)
        P.op("dve", lambda e: e.scalar_tensor_tensor(
            out=xbuf[xb][:, t, 0:512], in0=f32w[:, 4 + t, :], scalar=st_rstd[:, t:t + 1], in1=xbuf[xb][:, t, 0:512],
            op0=ALU.mult, op1=ALU.add), reads=["st_rstd%d" % t, ("f32w", 4 + t), ("xb", xb, t)], writes=[("xb", xb, t)])
        tb = t % 2
        P.op("dve", lambda e: e.scalar_tensor_tensor(
            out=f32w[:, tb, :], in0=bank(b1), scalar=st_rstd[:, t:t + 1], in1=gpost[:, gi, 512:1024],
            op0=ALU.mult, op1=ALU.mult), reads=["st_rstd%d" % t, "gpost"], writes=[PSB(b1), ("f32w", tb)])
        P.op("dve", lambda e: e.tensor_tensor(
            out=xbuf[xb][:, t, 512:1024], in0=xbuf[xb][:, t, 512:1024], in1=f32w[:, tb, :], op=ALU.add),
            reads=[("f32w", tb), ("xb", xb, t)], writes=[("xb", xb, t)])

    def post_epilogue(xb, gi, eps_tile, acc_srcs):
        for t in range(4):
            for h in range(2):
                ap, toks = acc_srcs[h][t]
                P.op("act", lambda e, t=t, h=h, ap=ap: e.activation(out=junk[:, 0:512], in_=ap, func=AF.Square, scale=float(D ** -0.5),
                                                               accum_out=st_ss[:, 2 * t + h:2 * t + h + 1]),
                     writes=list(toks) + ["junk", ("st_ss", 2 * t + h)])
            P.op("pool", lambda e, t=t: e.tensor_tensor(out=st_pre[:, t:t + 1], in0=st_ss[:, 2 * t:2 * t + 1],
                                                        in1=st_ss[:, 2 * t + 1:2 * t + 2], op=ALU.add),
                 reads=[("st_ss", 2 * t), ("st_ss", 2 * t + 1)], writes=["st_sum%d" % t])
            rstd_chain(st_pre[:, t:t + 1], 1, eps_tile, st_pre[:, t:t + 1], st_rstd[:, t:t + 1], ["st_sum%d" % t], "st_rstd%d" % t)
        for t in range(4):
            for h in range(2):
                ap, toks = acc_srcs[h][t]
                tb = 2 + h
                P.op("dve", lambda e, t=t, h=h, ap=ap, tb=tb: e.scalar_tensor_tensor(
                    out=f32w[:, tb, :], in0=ap, scalar=st_rstd[:, t:t + 1], in1=gpost[:, gi, h * 512:(h + 1) * 512],
                    op0=ALU.mult, op1=ALU.mult), reads=["st_rstd%d" % t, "gpost"], writes=list(toks) + [("f32w", tb)])
                P.op("pool" if h == 1 else "dve", lambda e, t=t, h=h, tb=tb: e.tensor_tensor(
                    out=xbuf[xb][:, t, h * 512:(h + 1) * 512], in0=xbuf[xb][:, t, h * 512:(h + 1) * 512],
                    in1=f32w[:, tb, :], op=ALU.add), reads=[("f32w", tb), ("xb", xb, t)], writes=[("xb", xb, t)])

    def ffn(xb, pfx, gi_post, pre_half1=None, mid_sub=None, after_sub=None, defer_tail=False):
        w1, w3, w2 = Wd[pfx + "_w1"], Wd[pfx + "_w3"], Wd[pfx + "_w2"]
        hTtok = [("hT", t) for t in range(4)]
        hb = {"i": 0}

        def w2_part(g):
            nk = 4 if g < 5 else 2
            n, wv, tok = W.get(w2, g * 512, nk, 0, 512)
            for t in range(4):
                b = 4 + t

                def f(e, t=t, b=b, wv=wv, g=g, nk=nk):
                    for k in range(nk):
                        j = g * 4 + k
                        i = e.matmul(bank(b), gT[:, j, t * 128:(t + 1) * 128], wv[:, k, :],
                                     start=(j == 0), stop=(j == NF - 1))
                    return i
                P.op("pe", f, reads=[tok] + [("gT", g * 4 + k) for k in range(nk)], writes=[PSB(b)])
            W.release(n)

        for g in range(6):
            npairs = 2 if g < 5 else 1
            for jp in range(2 * g, 2 * g + npairs):
                n1, wv1, tok1 = W.get(w1, 0, 8, jp * 256, 256)
                n3, wv3, tok3 = W.get(w3, 0, 8, jp * 256, 256)
                for jj in range(2):
                    j = 2 * jp + jj
                    pb = (hb["i"] % 2) * 2
                    hb["i"] += 1
                    b1, b3 = pb, pb + 1
                    fb_i = j % 2

                    def f(e, jj=jj, b1=b1, b3=b3, wv1=wv1, wv3=wv3):
                        for k in range(8):
                            e.matmul(bank(b1), wv1[:, k, jj * 128:(jj + 1) * 128], hT[:, k, :], start=(k == 0), stop=(k == 7))
                        for k in range(8):
                            i = e.matmul(bank(b3), wv3[:, k, jj * 128:(jj + 1) * 128], hT[:, k, :], start=(k == 0), stop=(k == 7))
                        return i
                    P.op("pe", f, reads=[tok1, tok3] + hTtok, writes=[PSB(b1), PSB(b3)])
                    P.op("act", lambda e, b1=b1, fb_i=fb_i: e.activation(out=f32w[:, fb_i, :], in_=bank(b1), func=AF.Tanh, scale=0.5),
                         writes=[PSB(b1), ("f32w", fb_i)])
                    P.op("dve", lambda e, b1=b1, fb_i=fb_i: e.scalar_tensor_tensor(
                        out=f32w[:, 2 + fb_i, :], in0=f32w[:, fb_i, :], scalar=1.0, in1=bank(b1), op0=ALU.add, op1=ALU.mult),
                        reads=[("f32w", fb_i)], writes=[PSB(b1), ("f32w", 2 + fb_i)])
                    P.op("dve", lambda e, b3=b3, fb_i=fb_i, j=j: e.tensor_tensor(
                        out=gT[:, j, :], in0=f32w[:, 2 + fb_i, :], in1=bank(b3), op=ALU.mult),
                        reads=[("f32w", 2 + fb_i)], writes=[PSB(b3), ("gT", j)])
                W.release(n1)
                W.release(n3)
            if g >= 1:
                w2_part(g - 1)
        w2_part(5)
        for t in range(4):
            epi_half0(t, gi_post, 4 + t)
        if pre_half1 is not None:
            pre_half1()
        pcs = [W.get(w2, g * 512, 4 if g < 5 else 2, 512, 512) for g in range(6)]
        for t in range(4):
            for g in range(6):
                if t > 0 and g > 0:
                    break

                def f(e, t=t, g0=g):
                    for g in (range(6) if t > 0 else [g0]):
                        for k in range(4 if g < 5 else 2):
                            j = g * 4 + k
                            i = e.matmul(bank(t), gT[:, j, t * 128:(t + 1) * 128], pcs[g][1][:, k, :], start=(j == 0), stop=(j == NF - 1))
                    return i
                rd = [pc[2] for pc in pcs] if t > 0 else [pcs[g][2]]
                P.op("pe", f, reads=rd + [("gT", j) for j in range(NF)], writes=[PSB(t)])
            epi_sub(t, xb, gi_post, ceps4, t)
            if mid_sub is not None:
                mid_sub(t)
            if after_sub is not None:
                if t >= 2:
                    after_sub[1](t - 2)
                after_sub[0](t)
        for pc in pcs:
            W.release(pc[0])
        if after_sub is not None and not defer_tail:
            after_sub[1](2)
            after_sub[1](3)

    def p1_proj(xb, tile, pre_t=None):
        w_in = Wd["w_in"]
        uTtok = [("W8b", t) for t in range(4)]
        nk_, wk, tokk = W.get(w_in, 0, 8, O_K, 256)
        nv_, wvv, tokv = W.get(w_in, 0, 8, O_V, 256)
        kb = [0, 1]
        vb = [2, 3]
        for t in range(4):
            if pre_t is not None and t in pre_t:
                pre_t[t]()
            def f(e, t=t):
                o = bank(kb[t // 2])[:, (t % 2) * 256:(t % 2) * 256 + 256]
                for k in range(8):
                    i = e.matmul(o, W8b[:, k, t * 128:(t + 1) * 128], wk[:, k, :], start=(k == 0), stop=(k == 7))
                return i
            P.op("pe", f, reads=[tokk, ("W8b", t)], writes=[PSB(kb[t // 2])])

            def f2(e, t=t):
                o = bank(vb[t // 2])[:, (t % 2) * 256:(t % 2) * 256 + 256]
                for k in range(8):
                    i = e.matmul(o, W8b[:, k, t * 128:(t + 1) * 128], wvv[:, k, :], start=(k == 0), stop=(k == 7))
                return i
            P.op("pe", f2, reads=[tokv, ("W8b", t)], writes=[PSB(vb[t // 2])])
        W.release(nk_)
        W.release(nv_)
        sqk = f32w[:, 0:2, :].rearrange("p a b -> p (a b)")
        for t in range(4):
            ch = tile * 4 + t
            src = bank(vb[t // 2])[:, (t % 2) * 256:(t % 2) * 256 + 256]
            P.op("act", lambda e, ch=ch, src=src: e.activation(
                out=Vaug[:, ch, :, 64:128], in_=src.rearrange("p (h d) -> p h d", h=4), func=AF.Copy),
                reads=["Vinit"], writes=[PSB(vb[t // 2]), ("V", tile)])
            srck = bank(kb[t // 2])[:, (t % 2) * 256:(t % 2) * 256 + 256]
            P.op("act", lambda e, t=t, srck=srck: e.activation(out=sqk[:, t * 256:(t + 1) * 256], in_=srck, func=AF.Square, scale=0.125),
                 writes=[PSB(kb[t // 2]), ("f32w", 0), ("f32w", 1)])
        P.op("dve", lambda e: e.tensor_reduce(out=st_q[:], in_=sqk.rearrange("p (a d) -> p a d", d=64), axis=AX.X, op=ALU.add),
             reads=[("f32w", 0), ("f32w", 1)], writes=["st_q"])
        rstd_chain(st_q[:], 16, ceps1, st_qp[:], st_qr[:], ["st_q"], "st_qr")
        kn = f32w[:, 2, 0:256].rearrange("p (h d) -> p h d", h=4)
        ta = f32w[:, 4, 0:128].rearrange("p (h i) -> p h i", h=4)
        tb = f32w[:, 5, 0:128].rearrange("p (h i) -> p h i", h=4)
        pcs = []
        for c0 in (O_CX, O_CC, O_CX + 256, O_CC + 256):
            pcs.append(W.get(w_in, 0, 8, c0, 256))

        def k_chain(t):
            krot = krot2b[:, t % 2]
            kt = "krot%d" % (t % 2)
            ch = tile * 4 + t
            srck = bank(kb[t // 2])[:, (t % 2) * 256:(t % 2) * 256 + 256].rearrange("p (h d) -> p h d", h=4)
            P.op("dve", lambda e: e.tensor_tensor(
                out=kn, in0=srck, in1=st_qr[:, t * 4:(t + 1) * 4].unsqueeze(2).broadcast_to([128, 4, 64]), op=ALU.mult),
                reads=["st_qr"], writes=[PSB(kb[t // 2]), ("f32w", 2)])
            P.op("dve", lambda e: e.tensor_tensor(out=kn, in0=kn, in1=qkg[:, 1, :].unsqueeze(1).broadcast_to([128, 4, 64]),
                                                  op=ALU.mult), reads=[("f32w", 2), "qkg"], writes=[("f32w", 2)])
            rope_ops(kn, krot[:, :, 0, :], 4, ch, ta, tb, [("f32w", 2)], kt)
            P.op("pool", lambda e: e.tensor_copy(out=krot[:, :, 1, :], in_=krot[:, :, 0, :]), reads=[kt + "_e", kt + "_o"], writes=[kt + "_d"])

        def k_T(t):
            krot = krot2b[:, t % 2]
            kt = "krot%d" % (t % 2)
            b = 2 + (t % 2)
            pT = bank(b).bitcast(BF16)

            def tr(e):
                for kv in range(4):
                    i = e.transpose(pT[:, kv * 128:(kv + 1) * 128], krot[:, kv, :, :].rearrange("p a d -> p (a d)"), ident[:])
                return i
            P.op("pe", tr, reads=[kt + "_e", kt + "_o", kt + "_d", "ident"], writes=[PSB(b)])
            P.op("act", lambda e: e.activation(
                out=KT[:, :, tile * 512 + t * 128: tile * 512 + (t + 1) * 128],
                in_=pT[:, 0:512].rearrange("p (k t) -> p k t", k=4), func=AF.Copy),
                writes=[PSB(b), ("KT", tile)])

        for c in range(4):
            k_chain(c)
            bx, bc_ = 4 + 2 * (c % 2), 5 + 2 * (c % 2)
            wx = pcs[2 * (c // 2)][1]
            wc = pcs[2 * (c // 2) + 1][1]

            def f(e, c=c, bx=bx, bc_=bc_, wx=wx, wc=wc):
                for k in range(8):
                    e.matmul(bank(bx), wx[:, k, (c % 2) * 128:(c % 2) * 128 + 128], W8b[:, k, :], start=(k == 0), stop=(k == 7))
                for k in range(8):
                    i = e.matmul(bank(bc_), wc[:, k, (c % 2) * 128:(c % 2) * 128 + 128], W8b[:, k, :], start=(k == 0), stop=(k == 7))
                return i
            P.op("pe", f, reads=[pcs[2 * (c // 2)][2], pcs[2 * (c // 2) + 1][2]] + uTtok, writes=[PSB(bx), PSB(bc_)])
            if c >= 1:
                k_T(c - 1)
            fi = 6 + (c % 2)
            P.op("act", lambda e, bx=bx, fi=fi: e.activation(out=f32w[:, fi, :], in_=bank(bx), func=AF.Copy),
                 writes=[PSB(bx), ("f32w", fi)])
            P.op("dve", lambda e, c=c, bc_=bc_, fi=fi: e.tensor_tensor(
                out=zT[:, c, 1 + tile * 512: 1 + (tile + 1) * 512], in0=f32w[:, fi, :], in1=bank(bc_), op=ALU.mult),
                reads=[("f32w", fi), "zinit"], writes=[PSB(bc_), ("zT", tile)])
        k_T(3)
        for pc in pcs:
            W.release(pc[0])

    def rope_ops(src, dst, H, ch, ta, tb, src_toks, dst_tok, tc=None, td=None, toks=(4, 5, 6, 7)):
        sv = src.rearrange("p h (i two) -> p h i two", two=2)
        dv = dst.rearrange("p h (i two) -> p h i two", two=2)
        cosb = ropet[:, ch, 0, :].unsqueeze(1).broadcast_to([128, H, 32])
        sinb = ropet[:, ch, 1, :].unsqueeze(1).broadcast_to([128, H, 32])
        x1, x2 = sv[:, :, :, 0], sv[:, :, :, 1]
        rA, rB = ("f32w", toks[0]), ("f32w", toks[1])
        srd = list(src_toks) + ["rope"]
        P.op("dve", lambda e: e.tensor_tensor(out=ta, in0=x1, in1=cosb, op=ALU.mult), reads=srd, writes=[rA])
        P.op("dve", lambda e: e.tensor_tensor(out=tb, in0=x2, in1=sinb, op=ALU.mult), reads=srd, writes=[rB])
        P.op("dve", lambda e: e.tensor_tensor(out=dv[:, :, :, 0], in0=ta, in1=tb, op=ALU.subtract),
             reads=[rA, rB], writes=[dst_tok + "_e"])
        if tc is None:
            eng2, t2a, t2b, rC, rD = "dve", ta, tb, rA, rB
        else:
            eng2, t2a, t2b, rC, rD = "pool", tc, td, ("f32w", toks[2]), ("f32w", toks[3])
        P.op(eng2, lambda e: e.tensor_tensor(out=t2a, in0=x1, in1=sinb, op=ALU.mult), reads=srd, writes=[rC])
        P.op(eng2, lambda e: e.tensor_tensor(out=t2b, in0=x2, in1=cosb, op=ALU.mult), reads=srd, writes=[rD])
        P.op("dve", lambda e: e.tensor_tensor(out=dv[:, :, :, 1], in0=t2a, in1=t2b, op=ALU.add),
             reads=[rC, rD], writes=[dst_tok + "_o"])

    def mem_kv(s):
        wm = Wd["w_mem_kv"]
        mt = f32w[:, 0:4, :].rearrange("p (a b) c -> p a (b c)", a=2)
        P.op("sp", lambda e: e.dma_start(out=mt, in_=mem_d[s].rearrange("(a p) d -> p a d", p=128)),
             writes=[("f32w", i) for i in range(4)], dma_sem=mem_sem)
        memT = OTn[:, :, 0:256]
        otoks = [("OTn", k) for k in range(8)]
        build_T([mt[:, a, :] for a in range(2)], [[("f32w", 0), ("f32w", 1)], [("f32w", 2), ("f32w", 3)]], 3, OTn,
                [otoks, otoks], nsub=2)
        mtok = otoks
        pk = [W.get(wm, 0, 8, 0, 256), W.get(wm, 0, 8, 256, 256)]
        for h in range(4):
            b = nbank()
            wv = pk[h // 2][1]

            def f(e, h=h, b=b, wv=wv):
                for k in range(8):
                    i = e.matmul(bank(b)[:, 0:256], wv[:, k, (h % 2) * 128:(h % 2) * 128 + 128], memT[:, k, :],
                                 start=(k == 0), stop=(k == 7))
                return i
            P.op("pe", f, reads=[pk[h // 2][2]] + mtok, writes=[PSB(b)])
            P.op("act", lambda e, h=h, b=b: e.activation(out=KmT[:, h, :], in_=bank(b)[:, 0:256], func=AF.Copy),
                 writes=[PSB(b), "KmT"])
        W.release(pk[0][0])
        W.release(pk[1][0])
        pv = [W.get(wm, 0, 4, 512, 512), W.get(wm, 512, 4, 512, 512)]
        for mc in range(2):
            b = nbank()

            def f(e, mc=mc, b=b):
                for k in range(8):
                    i = e.matmul(bank(b), memT[:, k, mc * 128:(mc + 1) * 128], pv[k // 4][1][:, k % 4, :],
                                 start=(k == 0), stop=(k == 7))
                return i
            P.op("pe", f, reads=[pv[0][2], pv[1][2]] + mtok, writes=[PSB(b)])
            P.op("act", lambda e, mc=mc, b=b: e.activation(out=Vm[:, mc, :], in_=bank(b), func=AF.Copy),
                 writes=[PSB(b), "Vm"])
        W.release(pv[0][0])
        W.release(pv[1][0])

    W8b_all = [("W8b", t_) for t_ in range(4)] + [("W8b", n_, t_) for n_ in range(2) for t_ in range(4)]

    def mixer(xb, tile):
        w_in = Wd["w_in"]
        uTtok = [("hT", t) for t in range(4)]
        QT = W8b
        pq = [W.get(w_in, kh * 512, 4, O_Q, 512) for kh in range(2)]
        sqh = f32w[:, 0, :]
        qnh = f32w[:, 2, :].rearrange("p (h d) -> p h d", h=8)
        ta = f32w[:, 1, 0:256].rearrange("p (h i) -> p h i", h=8)
        tb = f32w[:, 3, 0:256].rearrange("p (h i) -> p h i", h=8)
        tc_ = f32w[:, 6, 0:256].rearrange("p (h i) -> p h i", h=8)
        td_ = f32w[:, 7, 0:256].rearrange("p (h i) -> p h i", h=8)

        def q_dst(n_, t):
            if n_ == 0:
                return OTn[:, t, :], "qr0_%d" % t
            return qrot[:, 512:1024], "qr1"

        def q_proj(n_, t, qb, stage=None):
            if n_ == 0:
                sq_, sp_, sr_, nm = st_q2[:, 8 * t:8 * t + 8], st_qp2[:, 8 * t:8 * t + 8], st_qr2[:, 8 * t:8 * t + 8], "st_qr2_%d" % t
            else:
                sq_, sp_, sr_, nm = st_q[:, 0:8], st_qp[:, 0:8], st_qr[:, 0:8], "st_qr"
            if stage in (None, 1):
                def f(e):
                    for k in range(8):
                        i = e.matmul(bank(qb), hT[:, k, t * 128:(t + 1) * 128], pq[n_ * 2 + k // 4][1][:, k % 4, :],
                                     start=(k == 0), stop=(k == 7))
                    return i
                P.op("pe", f, reads=[pq[n_ * 2][2], pq[n_ * 2 + 1][2], ("hT", t)], writes=[PSB(qb)])
                P.op("act", lambda e: e.activation(out=sqh, in_=bank(qb), func=AF.Square, scale=0.125), writes=[PSB(qb), ("f32w", 0)])
                P.op("dve", lambda e: e.tensor_reduce(out=sq_, in_=sqh.rearrange("p (a d) -> p a d", d=64), axis=AX.X, op=ALU.add),
                     reads=[("f32w", 0)], writes=[nm + "_ss"])
                rstd_chain(sq_, 8, ceps1, sp_, sr_, [nm + "_ss"], nm)
            if stage in (None, 2):
                ch = tile * 4 + t
                P.op("dve", lambda e: e.tensor_tensor(
                    out=qnh, in0=bank(qb).rearrange("p (h d) -> p h d", h=8),
                    in1=sr_.unsqueeze(2).broadcast_to([128, 8, 64]), op=ALU.mult),
                    reads=[nm], writes=[PSB(qb), ("f32w", 2)])
                P.op("dve", lambda e: e.tensor_tensor(out=qnh, in0=qnh, in1=qkg[:, 0, :].unsqueeze(1).broadcast_to([128, 8, 64]),
                                                      op=ALU.mult), reads=[("f32w", 2), "qkg"], writes=[("f32w", 2)])
                dst, dtok = q_dst(n_, t)
                rope_ops(qnh, dst.rearrange("p (h d) -> p h d", h=8), 8, ch, ta, tb, [("f32w", 2)],
                         dtok, tc=tc_, td=td_, toks=(1, 3, 6, 7))

        def q_T(n_, t, b):
            pT = bank(b).bitcast(BF16)

            src, dtok = q_dst(n_, t)

            def tr(e):
                for k in range(4):
                    i = e.transpose(pT[:, k * 128:(k + 1) * 128], src[:, k * 128:(k + 1) * 128], ident[:])
                return i
            P.op("pe", tr, reads=[dtok + "_e", dtok + "_o", "ident"], writes=[PSB(b)])
            P.op("act", lambda e: e.activation(out=QT[:, 4 * n_:4 * n_ + 4, t * 128:(t + 1) * 128],
                                               in_=pT[:, 0:512].rearrange("p (k t) -> p k t", k=4), func=AF.Copy),
                 writes=[PSB(b), ("W8b", n_, t)])

        for t in range(4):
            q_proj(0, t, t, stage=1)
        for t in range(4):
            q_proj(0, t, t, stage=2)
        W.release(pq[0][0])
        W.release(pq[1][0])
        pqm = [W.get(w_in, 0, 8, O_QM, 256), W.get(w_in, 0, 8, O_QM + 256, 256)]
        for h in range(4):
            b = 4 + h
            wv = pqm[h // 2][1]

            def f(e, h=h, b=b, wv=wv):
                for k in range(8):
                    i = e.matmul(bank(b), wv[:, k, (h % 2) * 128:(h % 2) * 128 + 128], hT[:, k, :], start=(k == 0), stop=(k == 7))
                return i
            P.op("pe", f, reads=[pqm[h // 2][2]] + uTtok, writes=[PSB(b)])
            P.op("act", lambda e, h=h, b=b: e.activation(out=gT[:, h, :], in_=bank(b), func=AF.Copy),
                 writes=[PSB(b), ("gT", h)])
        W.release(pqm[0][0])
        W.release(pqm[1][0])
        pcb = [W.get(w_in, 0, 8, O_CB, 256), W.get(w_in, 0, 8, O_CB + 256, 256)]
        ztoks = [("zT", i) for i in (tile - 1, tile, tile + 1) if 0 <= i < NT] + ["zinit"]
        for c in range(4):
            b = 4 + c
            wv = pcb[c // 2][1]

            def f(e, c=c, b=b, wv=wv):
                for k in range(8):
                    i = e.matmul(bank(b), wv[:, k, (c % 2) * 128:(c % 2) * 128 + 128], hT[:, k, :], start=(k == 0), stop=(k == 7))
                return i
            P.op("pe", f, reads=[pcb[c // 2][2]] + uTtok, writes=[PSB(b)])
            fi = 6 + (c % 2)
            z0 = 1 + tile * 512
            P.op("dve", lambda e, c=c, fi=fi, z0=z0: e.tensor_scalar(
                out=f32w[:, fi, :], in0=zT[:, c, z0:z0 + 512], scalar1=convw[:, 1, c:c + 1], scalar2=convb[:, c:c + 1],
                op0=ALU.mult, op1=ALU.add), reads=ztoks + ["convw", "convb"], writes=[("f32w", fi)])
            P.op("dve", lambda e, c=c, fi=fi, z0=z0: e.scalar_tensor_tensor(
                out=f32w[:, fi, :], in0=zT[:, c, z0 - 1:z0 + 511], scalar=convw[:, 0, c:c + 1], in1=f32w[:, fi, :],
                op0=ALU.mult, op1=ALU.add), reads=ztoks + [("f32w", fi)], writes=[("f32w", fi)])
            P.op("dve", lambda e, c=c, fi=fi, z0=z0: e.scalar_tensor_tensor(
                out=f32w[:, fi, :], in0=zT[:, c, z0 + 1:z0 + 513], scalar=convw[:, 2, c:c + 1], in1=f32w[:, fi, :],
                op0=ALU.mult, op1=ALU.add), reads=ztoks + [("f32w", fi)], writes=[("f32w", fi)])
            P.op("dve", lambda e, c=c, fi=fi, b=b: e.tensor_tensor(out=gT[:, 10 + c, :], in0=f32w[:, fi, :], in1=bank(b), op=ALU.mult),
                 reads=[("f32w", fi)], writes=[PSB(b), ("gT", 10 + c)])
        W.release(pcb[0][0])
        W.release(pcb[1][0])
        def mem_S(h):
            sb = (h % 2) * 2
            pc0 = 4 if h % 2 == 0 else 14
            PmT = gT[:, pc0:pc0 + 2, :]

            def f(e):
                for mc in range(2):
                    i = e.matmul(bank(sb + mc), KmT[:, h, mc * 128:(mc + 1) * 128], gT[:, h, :], start=True, stop=True)
                return i
            P.op("pe", f, reads=["KmT", ("gT", h)], writes=[PSB(sb), PSB(sb + 1)])
            P.op("act", lambda e: e.activation(out=PmT.rearrange("p a b -> p (a b)"), in_=bank(sb, 2), func=AF.Exp,
                                               scale=float(128 ** -0.5)),
                 writes=[PSB(sb), PSB(sb + 1), ("gT", pc0), ("gT", pc0 + 1)])

        def mem_PV(h):
            ob, smb = 4 + (h % 2) * 2, 5 + (h % 2) * 2
            pc0 = 4 if h % 2 == 0 else 14
            PmT = gT[:, pc0:pc0 + 2, :]

            def f2(e):
                for mc in range(2):
                    e.matmul(bank(ob), Vm[:, mc, h * 128:(h + 1) * 128], PmT[:, mc, :], start=(mc == 0), stop=(mc == 1))
                for mc in range(2):
                    i = e.matmul(bank(smb), onesbf[:], PmT[:, mc, :], start=(mc == 0), stop=(mc == 1))
                return i
            P.op("pe", f2, reads=["Vm", "onesbf", ("gT", pc0), ("gT", pc0 + 1)], writes=[PSB(ob), PSB(smb)])
            P.op("dve", lambda e: e.reciprocal(out=f32w[:, 6, :], in_=bank(smb)), writes=[PSB(smb), ("f32w", 6)])
            P.op("dve", lambda e: e.tensor_tensor(out=gT[:, 6 + h, :], in0=f32w[:, 6, :], in1=bank(ob), op=ALU.mult),
                 reads=[("f32w", 6)], writes=[PSB(ob), ("gT", 6 + h)])

        mem_S(0)
        for h in range(4):
            if h + 1 < 4:
                mem_S(h + 1)
            mem_PV(h)
        for t in range(4):
            q_T(0, t, 4 + t)
        for kh in range(2):
            pq.append(W.get(w_in, kh * 512, 4, O_Q + 512, 512))
        hooks = {}
        if NCH >= 16:
            for t in range(4):
                hooks.setdefault((t, 1), []).append(lambda t=t: q_proj(1, t, 6))
                hooks.setdefault((t, 13), []).append(lambda t=t: q_T(1, t, 7))
            hooks.setdefault((3, 14), []).append(lambda: (W.release(pq[2][0]), W.release(pq[3][0])))
        else:
            for t in range(4):
                q_proj(1, t, t % 2)
                q_T(1, t, 2 + t % 2)
            W.release(pq[2][0])
            W.release(pq[3][0])
        QTtok = [[("W8b", n_, t) for t in range(4)] for n_ in range(2)]
        Osb = f32w[:, 4:6, :]
        steps = [(j, c) for j in range(8) for c in range(NCH)]

        def qk(si):
            j, c = steps[si]
            kvh = j // 2
            sb = (si % 2) * 2

            def f(e):
                e.matmul(bank(sb), KT[0:64, kvh, c * 128:(c + 1) * 128], QT[0:64, j, :], start=True, stop=True)
                return e.matmul(bank(sb + 1), KT[64:128, kvh, c * 128:(c + 1) * 128], QT[64:128, j, :], start=True, stop=True)
            P.op("pe", f, reads=[("KT", c // 4)] + QTtok[j // 4], writes=[PSB(sb), PSB(sb + 1)])

        def ex(si):
            sb = (si % 2) * 2
            pb = 14 + (si % 3) * 2
            P.op("act", lambda e: e.activation(out=gT[:, pb:pb + 2, :].rearrange("p a b -> p (a b)"), in_=bank(sb, 2),
                                               func=AF.Exp, scale=0.125),
                 writes=[PSB(sb), PSB(sb + 1), ("gT", pb), ("gT", pb + 1)])

        def pv(si):
            j, c = steps[si]
            kvh = j // 2
            pb = 14 + (si % 3) * 2

            def f(e):
                e.matmul(ps[0:65, 4 * 512:5 * 512], Vaug[:, c, kvh, 64:129], gT[:, pb, :], start=(c == 0), stop=(c == NCH - 1))
                return e.matmul(bank(5), Vaug[:, c, kvh, 0:128], gT[:, pb + 1, :], start=(c == 0), stop=(c == NCH - 1))
            P.op("pe", f, reads=[("V", c // 4), "Vinit", ("gT", pb), ("gT", pb + 1)], writes=[PSB(4), PSB(5)])

        def fin_a(j):
            P.op("dve", lambda e: e.tensor_copy(out=Osb[0:65, 0, :], in_=ps[0:65, 4 * 512:5 * 512]), writes=[PSB(4), ("f32w", 4)])
            P.op("dve", lambda e: e.tensor_copy(out=Osb[:, 1, :], in_=bank(5)), writes=[PSB(5), ("f32w", 5)])
            P.op("dve", lambda e: e.reciprocal(out=Osb[64:65, 0, :], in_=Osb[64:65, 0, :]), reads=[("f32w", 4)], writes=[("f32w", 4)])
            P.op("dve", lambda e: e.reciprocal(out=Osb[0:1, 1, :], in_=Osb[0:1, 1, :]), reads=[("f32w", 5)], writes=[("f32w", 5)])

        def fin_b(j):
            def f(e):
                e.matmul(ps[0:64, 6 * 512:7 * 512], ones32[64:65, 0:64], Osb[64:65, 0, :], start=True, stop=True)
                return e.matmul(ps[64:128, 7 * 512:8 * 512], ones32[0:1, 0:64], Osb[0:1, 1, :], start=True, stop=True)
            P.op("pe", f, reads=["ones32", ("f32w", 4), ("f32w", 5)], writes=[PSB(6), PSB(7)])
            P.op("dve", lambda e: e.tensor_tensor(out=OTn[0:64, j, :], in0=Osb[0:64, 0, :], in1=ps[0:64, 6 * 512:7 * 512], op=ALU.mult),
                 reads=[("f32w", 4)], writes=[PSB(6), ("OTn", j)])
            P.op("dve", lambda e: e.tensor_tensor(out=OTn[64:128, j, :], in0=Osb[64:128, 1, :], in1=ps[64:128, 7 * 512:8 * 512], op=ALU.mult),
                 reads=[("f32w", 5)], writes=[PSB(7), ("OTn", j)])

        nsteps = len(steps)
        qk(0)
        qk(1)
        for si in range(nsteps):
            ex(si)
            if si + 2 < nsteps:
                qk(si + 2)
            pv(si)
            j, c = steps[si]
            if c == NCH - 1:
                fin_a(j)
            if c == NCH // 2 and j > 0:
                fin_b(j - 1)
            for hk in hooks.get((j, c), []):
                hk()
        pend = {"fin": True}
        mTb = W8b

        def mi(fi):
            return 4 + ((fi + 2) % 4)
        for fh in range(2):
            for br in range(3):
                gp = [W.get(w_in, 0, 8, O_G + br * 1024 + fh * 512 + q_ * 256, 256) for q_ in range(2)]
                if br == 0:
                    bp = [W.get(Wd["p_conv"], 0, 4, fh * 512, 512)]
                elif br == 1:
                    bp = [W.get(Wd["p_attn"], 0, 8, fh * 512 + q_ * 256, 256) for q_ in range(2)]
                else:
                    bp = [W.get(Wd["p_mem"], 0, 4, fh * 512, 512)]
                for fi in range(4):
                    if pend["fin"] and fi == 2:
                        pend["fin"] = False
                        fin_b(7)
                    f_ = fh * 4 + fi
                    gb, yb = nbank(), nbank()
                    gwv = gp[fi // 2][1]

                    def fg(e, fi=fi, gb=gb, gwv=gwv):
                        for k in range(8):
                            i = e.matmul(bank(gb), gwv[:, k, (fi % 2) * 128:(fi % 2) * 128 + 128], hT[:, k, :], start=(k == 0), stop=(k == 7))
                        return i
                    P.op("pe", fg, reads=[gp[fi // 2][2]] + uTtok, writes=[PSB(gb)])
                    if br == 0:
                        def fy(e, fi=fi, yb=yb, wv=bp[0][1]):
                            for k in range(4):
                                i = e.matmul(bank(yb), wv[:, k, fi * 128:(fi + 1) * 128], gT[:, 10 + k, :], start=(k == 0), stop=(k == 3))
                            return i
                        rd = [bp[0][2]] + [("gT", 10 + k) for k in range(4)]
                    elif br == 1:
                        def fy(e, fi=fi, yb=yb, wv=bp[fi // 2][1]):
                            for k in range(8):
                                i = e.matmul(bank(yb), wv[:, k, (fi % 2) * 128:(fi % 2) * 128 + 128], OTn[:, k, :], start=(k == 0), stop=(k == 7))
                            return i
                        rd = [bp[fi // 2][2]] + [("OTn", k) for k in range(8)]
                    else:
                        def fy(e, fi=fi, yb=yb, wv=bp[0][1]):
                            for k in range(4):
                                i = e.matmul(bank(yb), wv[:, k, fi * 128:(fi + 1) * 128], gT[:, 6 + k, :], start=(k == 0), stop=(k == 3))
                            return i
                        rd = [bp[0][2]] + [("gT", 6 + k) for k in range(4)]
                    P.op("pe", fy, reads=rd, writes=[PSB(yb)])
                    gi_ = fi % 2
                    gcol = br * 8 + f_
                    P.op("act", lambda e, gb=gb, gi_=gi_, gcol=gcol: e.activation(
                        out=f32w[:, gi_, :], in_=bank(gb), func=AF.Tanh, bias=bgate[:, gcol:gcol + 1], scale=0.5),
                        reads=["bgate"], writes=[PSB(gb), ("f32w", gi_)])
                    if br == 0:
                        P.op("dve", lambda e, fi=fi, yb=yb, gi_=gi_: e.scalar_tensor_tensor(
                            out=f32w[:, mi(fi), :], in0=f32w[:, gi_, :], scalar=1.0, in1=bank(yb), op0=ALU.add, op1=ALU.mult),
                            reads=[("f32w", gi_)], writes=[PSB(yb), ("f32w", mi(fi))])
                    else:
                        P.op("dve", lambda e, fi=fi, yb=yb, gi_=gi_: e.scalar_tensor_tensor(
                            out=f32w[:, 2 + gi_, :], in0=f32w[:, gi_, :], scalar=1.0, in1=bank(yb), op0=ALU.add, op1=ALU.mult),
                            reads=[("f32w", gi_)], writes=[PSB(yb), ("f32w", 2 + gi_)])
                        if br == 1:
                            P.op("dve", lambda e, fi=fi, gi_=gi_: e.tensor_tensor(
                                out=f32w[:, mi(fi), :], in0=f32w[:, mi(fi), :], in1=f32w[:, 2 + gi_, :], op=ALU.add),
                                reads=[("f32w", mi(fi)), ("f32w", 2 + gi_)], writes=[("f32w", mi(fi))])
                        else:
                            P.op("dve", lambda e, fi=fi, gi_=gi_, f_=f_: e.tensor_tensor(
                                out=mTb[:, f_, :], in0=f32w[:, mi(fi), :], in1=f32w[:, 2 + gi_, :], op=ALU.add),
                                reads=[("f32w", mi(fi)), ("f32w", 2 + gi_)], writes=W8b_all)
                for pc in gp[:1]:
                    pass
                for pc in gp + bp:
                    W.release(pc[0])

        mtoks = W8b_all
        pw = [W.get(Wd["w_out"], kh * 512, 4, half * 512, 512) for half in range(2) for kh in range(2)]
        for t in range(4):
            for half in range(2):
                b = (4 + t) if half == 0 else t

                def f(e, t=t, b=b, half=half):
                    for kk in range(8):
                        i = e.matmul(bank(b), mTb[:, kk, t * 128:(t + 1) * 128], pw[half * 2 + kk // 4][1][:, kk % 4, :],
                                     start=(kk == 0), stop=(kk == 7))
                    return i
                P.op("pe", f, reads=[pw[half * 2][2], pw[half * 2 + 1][2]] + mtoks, writes=[PSB(b)])
            epi_half0(t, 1, 4 + t)
            epi_sub(t, xb, 1, ceps4, t)
            if t >= 2:
                build_sub_B(t - 2, stg[:, t % 2, :], [("stg", t % 2)], 2, hT, "hT")
            build_sub_A(t, 0, xbuf[xb][:, t, :], [("xb", xb, t)], stg[:, t % 2, :], [("stg", t % 2)])
        for pc in pw:
            W.release(pc[0])
        build_sub_B(2, stg[:, 0, :], [("stg", 0)], 2, hT, "hT")
        build_sub_B(3, stg[:, 1, :], [("stg", 1)], 2, hT, "hT")

    def program():
        init()
        g = 0
        final_toks = []

        def xdma(dst_b, src_ap, rd):
            P.op("sp", lambda e: e.dma_start(out=xbuf[dst_b][:], in_=src_ap.rearrange("(t p) d -> p t d", p=128)),
                 reads=rd, writes=[("xb", dst_b, t) for t in range(4)], dma_sem=xl_sem[dst_b])

        for s in range(n_seq):
            xdma(g % 2, x_d[s, 0:512, :], [])
            mem_kv(s)
            for t in range(4):
                build_sub(t, 1, xbuf[g % 2][:, t, :], [("xb", g % 2, t)], 0, hT, "hT", b=nbank())
            for i in range(NT):
                b = g % 2
                ob = (g + 1) % 2
                if i + 1 < NT:
                    xdma(ob, x_d[s, (i + 1) * 512:(i + 2) * 512, :], [])

                    def pre(ob=ob):
                        for t in range(4):
                            build_sub_A(t, 1, xbuf[ob][:, t, :], [("xb", ob, t)], *ostg(t))

                    def mid(t):
                        build_sub_B(t, ostg(t)[0], ostg(t)[1], 0, hT, "hT")
                else:
                    pre = mid = None

                def aftA(t, b=b):
                    sb = t % 2
                    build_sub_A(t, 0, xbuf[b][:, t, :], [("xb", b, t)], stg[:, sb, :], [("stg", sb)])

                def aftB(t):
                    sb = t % 2
                    build_sub_B(t, stg[:, sb, :], [("stg", sb)], 1, W8b, "W8b")
                ffn(b, "ffn1", 0, pre_half1=pre, mid_sub=mid, after_sub=(aftA, aftB), defer_tail=True)
                P.op("sp", lambda e, s=s, i=i, b=b: e.dma_start(
                    out=x1s_d[s, i * 512:(i + 1) * 512, :].rearrange("(t p) d -> p t d", p=128), in_=xbuf[b][:]),
                    reads=[("xb", b, t) for t in range(4)], writes=[("x1s", s, i)], dma_sem=xs_sem[b])
                p1_proj(b, i, pre_t={2: (lambda f=aftB: f(2)), 3: (lambda f=aftB: f(3))})
                g += 1
            xdma(g % 2, x1s_d[s, 0:512, :], [("x1s", s, 0)])
            for t in range(4):
                build_sub(t, 1, xbuf[g % 2][:, t, :], [("xb", g % 2, t)], 1, hT, "hT", b=nbank())
            for i in range(NT):
                b = g % 2
                ob = (g + 1) % 2
                if i + 1 < NT:
                    xdma(ob, x1s_d[s, (i + 1) * 512:(i + 2) * 512, :], [("x1s", s, i + 1)])

                    def pre(ob=ob):
                        for t in range(4):
                            build_sub_A(t, 1, xbuf[ob][:, t, :], [("xb", ob, t)], *ostg(t))

                    def mid(t):
                        build_sub_B(t, ostg(t)[0], ostg(t)[1], 1, hT, "hT")
                else:
                    pre = mid = None
                mixer(b, i)
                ffn(b, "ffn2", 2, pre_half1=pre, mid_sub=mid, after_sub=None)
                P.op("sp", lambda e, s=s, i=i, b=b: e.dma_start(
                    out=y_d[s, i * 512:(i + 1) * 512, :].rearrange("(t p) d -> p t d", p=128), in_=xbuf[b][:]),
                    reads=[("xb", b, t) for t in range(4)], writes=[("y", s, i)], dma_sem=xs_sem[b])
                final_toks.append(("y", s, i))
                g += 1
        P.op("sp", lambda e: e.nop(), reads=final_toks)

    P.enabled = False
    W.planning = True
    program()
    P.enabled = True
    P.reset()
    rr["b"] = 0
    bb["i"] = 0
    W.start_real()
    program()
    assert W.n_acq == len(W.plan)
    P.emit()
    return nc


def _rope_tables(T):
    rows = T // 64
    row = np.repeat(np.arange(rows), 64).astype(np.float32)
    col = np.tile(np.arange(64), rows).astype(np.float32)
    inv = (1.0 / (np.float32(10000.0) ** (np.arange(0, 32, 2, dtype=np.float32) / np.float32(32)))).astype(np.float32)
    ang = np.concatenate([row[:, None] * inv, col[:, None] * inv], axis=-1).astype(np.float32)
    cs = np.stack([np.cos(ang), np.sin(ang)], axis=1).astype(np.float32)
    nch = T // 128
    return np.ascontiguousarray(cs.reshape(nch, 128, 2, 32).transpose(1, 0, 2, 3).reshape(128, nch * 64))


def _fm(v):
    return np.ascontiguousarray(np.asarray(v, np.float32).reshape(8, 128).T)


def make_shared_inputs(T, p):
    sh = {}
    for nm in ["ffn1_w1", "ffn1_w3", "ffn1_w2", "w_in", "p_conv", "p_attn", "w_mem_kv", "p_mem", "w_out",
               "ffn2_w1", "ffn2_w3", "ffn2_w2"]:
        sh[nm] = np.ascontiguousarray(p[nm], dtype=np.float32)
    sh["gfm"] = np.ascontiguousarray(np.concatenate(
        [_fm(p["ffn1_pre"]), _fm(p["mix_pre"]), _fm(p["ffn2_pre"]), _fm(p["mem_norm"])], axis=1))
    sh["gpost"] = np.ascontiguousarray(np.concatenate([p["ffn1_post"], p["mix_post"], p["ffn2_post"]]).astype(np.float32))
    sh["qkg"] = np.ascontiguousarray(np.concatenate([p["q_norm"], p["k_norm"]]).astype(np.float32))
    cw = np.asarray(p["conv_w"], np.float32)
    sh["convw"] = np.ascontiguousarray(cw.reshape(3, 4, 128).transpose(2, 0, 1).reshape(128, 12))
    sh["convb"] = np.ascontiguousarray(np.asarray(p["conv_b"], np.float32).reshape(4, 128).T)
    sh["bgate"] = np.ascontiguousarray(np.asarray(p["b_gate"], np.float32).reshape(24, 128).T)
    sh["rope"] = _rope_tables(T)
    sh["ident"] = np.eye(128, dtype=np.float32).astype(ml_dtypes.bfloat16)
    return sh


_NC_CACHE = {}


def kernel(x_prompt, x_sample, mem_prompt, mem_sample, **params):
    x_all = np.concatenate([np.asarray(x_prompt, np.float32), np.asarray(x_sample, np.float32)], axis=0)
    m_all = np.concatenate([np.asarray(mem_prompt, np.float32), np.asarray(mem_sample, np.float32)], axis=0)
    nb = x_all.shape[0]
    T = x_all.shape[1]
    per = nb // N_CORES
    p = {k: np.asarray(v)[0] for k, v in params.items()}
    sh = make_shared_inputs(T, p)
    key = (per, T)
    if key not in _NC_CACHE:
        _NC_CACHE[key] = build_program(per, T)
    nc = _NC_CACHE[key]
    in_maps = []
    for c in range(N_CORES):
        m = dict(sh)
        m["x"] = np.ascontiguousarray(x_all[c * per:(c + 1) * per])
        m["mem"] = np.ascontiguousarray(m_all[c * per:(c + 1) * per])
        in_maps.append(m)
    res = run_bass_kernel_spmd(nc, in_maps, core_ids=list(range(N_CORES)))
    y = np.concatenate([np.asarray(r["y"], np.float32) for r in res.results], axis=0)
    nbp = np.asarray(x_prompt).shape[0]
    return (np.ascontiguousarray(y[:nbp]), np.ascontiguousarray(y[nbp:]))
```
